# Optimizing a Trainium2 kernel written in Bass

```python
import math
import jax
import jax.numpy as jnp
from jax import lax
import numpy as np

D_MODEL = 1024
BATCH = 2
SEQ = 16384
DEPTH = 4
DEC_BATCH = 32
DEC_SEQ = 16
PAST_LEN = 1024

CHUNK = 64
EPS = 1e-6
D_FF = 2816
A_HEADS = 4
A_DK = 128
A_DV = 128
B_HEADS = 4
B_DK = 128
B_DV = 128
C_HEADS = 4
C_DK = 64
C_DV = 128
C_RANK = 16
GLA_GATE_TEMP = 16.0
N_BRANCH = 3
BRANCH_W = 512

IN_SIZES = (
    A_HEADS * A_DK, A_HEADS * A_DK, A_HEADS * A_DV, A_HEADS * A_DV,
    B_HEADS * B_DK, B_HEADS * B_DK, B_HEADS * B_DV, B_HEADS * B_DV, 2 * B_HEADS,
    C_HEADS * C_DK, C_HEADS * C_DK, C_HEADS * C_DV, C_HEADS * C_DV, C_RANK,
    N_BRANCH * D_MODEL,
)
D_IN = 4 * 512 + (4 * 512 + 8) + (256 + 256 + 512 + 512 + 16) + 3 * 1024

kernel_name = "hybrid_streaming_hgrn2_mlstm_gla_step"


def split_points():
    return tuple(int(v) for v in np.cumsum(np.array(IN_SIZES))[:-1])


def rms_norm(x, w):
    xf = x.astype(jnp.float32)
    y = xf * lax.rsqrt(jnp.mean(xf * xf, axis=-1, keepdims=True) + EPS)
    return (y * w.astype(jnp.float32)).astype(x.dtype)


def head_rms_norm(o, w):
    n_heads = o.shape[2]
    y = rms_norm(o, w.reshape(n_heads, -1))
    return y.reshape(o.shape[0], o.shape[1], -1)


def half_swiglu(x, norm_w, w_up, w_down):
    h = rms_norm(x, norm_w)
    g, u = jnp.split(h @ w_up, 2, axis=-1)
    return x + 0.5 * ((jax.nn.silu(g) * u) @ w_down)


def to_blocks(a, blk):
    b, t, h, d = a.shape
    return a.reshape(b, t // blk, blk, h, d).transpose(1, 0, 3, 2, 4)


def from_blocks(a):
    nb, b, h, blk, d = a.shape
    return a.transpose(1, 0, 3, 2, 4).reshape(b, nb * blk, h, d)


def gated_linear_recurrence(q, k, v, log_f, s0):
    blk = math.gcd(q.shape[1], CHUNK)
    causal = jnp.tril(jnp.ones((blk, blk), dtype=bool))

    def step(state, blocks):
        qb, kb, vb, gb = blocks
        b = jnp.cumsum(gb, axis=2)
        diff = b[:, :, :, None, :] - b[:, :, None, :, :]
        decay = jnp.exp(jnp.where(causal[:, :, None], diff, -jnp.inf))
        scores = jnp.einsum('bhtd,bhsd,bhtsd->bhts', qb, kb, decay)
        o = (jnp.einsum('bhts,bhsv->bhtv', scores, vb)
             + jnp.einsum('bhtd,bhdv->bhtv', qb * jnp.exp(b), state))
        b_last = b[:, :, -1:, :]
        new_state = (jnp.exp(b_last[:, :, 0, :])[..., None] * state
                     + jnp.einsum('bhsd,bhsv->bhdv', kb * jnp.exp(b_last - b), vb))
        return new_state, o

    blocks = tuple(to_blocks(a.astype(jnp.float32), blk) for a in (q, k, v, log_f))
    s_final, o = lax.scan(step, s0.astype(jnp.float32), blocks)
    return from_blocks(o), s_final


def mlstm_recurrence(q, k, v, log_i, log_f, c0, n0, m0):
    blk = math.gcd(q.shape[1], CHUNK)
    causal = jnp.tril(jnp.ones((blk, blk), dtype=bool))

    def step(carry, blocks):
        c, n, m = carry
        qb, kb, vb, ib, fb = blocks
        ib = ib[..., 0]
        b = jnp.cumsum(fb[..., 0], axis=-1)
        d_intra = jnp.where(causal, b[..., :, None] - b[..., None, :] + ib[..., None, :], -jnp.inf)
        d_inter = b + m[..., None]
        m_t = jnp.maximum(d_inter, jnp.max(d_intra, axis=-1))
        w_intra = jnp.exp(d_intra - m_t[..., None]) * jnp.einsum('bhtd,bhsd->bhts', qb, kb)
        w_inter = jnp.exp(d_inter - m_t)
        num = (jnp.einsum('bhts,bhsv->bhtv', w_intra, vb)
               + w_inter[..., None] * jnp.einsum('bhtd,bhdv->bhtv', qb, c))
        den = jnp.sum(w_intra, axis=-1) + w_inter * jnp.einsum('bhtd,bhd->bht', qb, n)
        h = num / jnp.maximum(jnp.abs(den), jnp.exp(-m_t))[..., None]
        e_intra = b[..., -1:] - b + ib
        e_inter = b[..., -1] + m
        m_new = jnp.maximum(e_inter, jnp.max(e_intra, axis=-1))
        w_s = jnp.exp(e_intra - m_new[..., None])
        w_c = jnp.exp(e_inter - m_new)
        c_new = w_c[..., None, None] * c + jnp.einsum('bhs,bhsd,bhsv->bhdv', w_s, kb, vb)
        n_new = w_c[..., None] * n + jnp.einsum('bhs,bhsd->bhd', w_s, kb)
        return (c_new, n_new, m_new), h

    blocks = tuple(to_blocks(a.astype(jnp.float32), blk)
                   for a in (q, k, v, log_i[..., None], log_f[..., None]))
    init = (c0.astype(jnp.float32), n0.astype(jnp.float32), m0.astype(jnp.float32))
    (c_f, n_f, m_f), h = lax.scan(step, init, blocks)
    return from_blocks(h), c_f, n_f, m_f


def token_mixer(x, norm_w, w_in, lb, hgrn_norm_w, mlstm_gate_b, mlstm_norm_w,
                gla_w_a2, gla_b_a, gla_norm_w, w_branch, w_out,
                s_hgrn, c_ml, n_ml, m_ml, s_gla):
    bsz, t_len, _ = x.shape
    f32 = jnp.float32
    h = rms_norm(x, norm_w)
    p = h @ w_in
    (a_q, a_f, a_i, a_g, b_q, b_k, b_v, b_o, b_if,
     c_q, c_k, c_v, c_g, c_lr, merge_g) = jnp.split(p, split_points(), axis=-1)

    def heads(a, n_heads):
        return a.reshape(bsz, t_len, n_heads, -1)

    lbh = lb.reshape(A_HEADS, A_DK)
    a_x = heads(a_f, A_HEADS).astype(f32)
    forget = lbh + (1.0 - lbh) * jax.nn.sigmoid(a_x)
    key_a = (1.0 - lbh) * jax.nn.sigmoid(-a_x)
    o_a, s_hgrn_new = gated_linear_recurrence(
        jax.nn.silu(heads(a_q, A_HEADS)), key_a, heads(a_i, A_HEADS), jnp.log(forget), s_hgrn)
    o_a = head_rms_norm(o_a * jax.nn.sigmoid(heads(a_g, A_HEADS)), hgrn_norm_w)

    gates_b = b_if.astype(f32) + mlstm_gate_b.astype(f32)
    log_i = gates_b[..., :B_HEADS]
    log_f = jax.nn.log_sigmoid(gates_b[..., B_HEADS:])
    h_b, c_new, n_new, m_new = mlstm_recurrence(
        heads(b_q, B_HEADS), heads(b_k, B_HEADS) * (B_DK ** -0.5), heads(b_v, B_HEADS),
        log_i, log_f, c_ml, n_ml, m_ml)
    o_b = jax.nn.sigmoid(b_o) * head_rms_norm(h_b, mlstm_norm_w)

    log_alpha = jax.nn.log_sigmoid((c_lr @ gla_w_a2 + gla_b_a).astype(f32)) / GLA_GATE_TEMP
    o_c, s_gla_new = gated_linear_recurrence(
        heads(c_q, C_HEADS) * (C_DK ** -0.5), heads(c_k, C_HEADS), heads(c_v, C_HEADS),
        heads(log_alpha, C_HEADS), s_gla)
    o_c = head_rms_norm(o_c, gla_norm_w) * jax.nn.silu(c_g)

    branches = jnp.stack([o_a, o_b, o_c], axis=2).astype(x.dtype)
    proj_b = jnp.einsum('btnc,ncd->btnd', branches, w_branch)
    gate = jax.nn.sigmoid(merge_g).reshape(bsz, t_len, N_BRANCH, D_MODEL)
    merged = jnp.sum(gate * proj_b, axis=2)
    return x + merged @ w_out, (s_hgrn_new, c_new, n_new, m_new, s_gla_new)


def run_trunk(x, states, weights):
    (ffn1_norm, ffn1_w_up, ffn1_w_down, mix_norm, w_in, hgrn_lb_logits, hgrn_norm,
     mlstm_gate_bias, mlstm_norm, gla_w_a2, gla_b_a, gla_norm, w_branch, w_out,
     ffn2_norm, ffn2_w_up, ffn2_w_down, final_norm) = weights
    cum = jnp.cumsum(jax.nn.softmax(hgrn_lb_logits.astype(jnp.float32), axis=0), axis=0)
    lower_bounds = cum - cum[0:1]
    new_states = ([], [], [], [], [])
    for l in range(DEPTH):
        x = half_swiglu(x, ffn1_norm[l], ffn1_w_up[l], ffn1_w_down[l])
        x, st = token_mixer(x, mix_norm[l], w_in[l], lower_bounds[l], hgrn_norm[l],
                            mlstm_gate_bias[l], mlstm_norm[l], gla_w_a2[l], gla_b_a[l],
                            gla_norm[l], w_branch[l], w_out[l],
                            states[0][l], states[1][l], states[2][l], states[3][l], states[4][l])
        x = half_swiglu(x, ffn2_norm[l], ffn2_w_up[l], ffn2_w_down[l])
        for acc, s in zip(new_states, st):
            acc.append(s)
    y = rms_norm(x, final_norm)
    return y, tuple(jnp.stack(acc) for acc in new_states)


def setup_inputs(seed: int = 0) -> dict:
    key = jax.random.key(seed)
    ks = jax.random.split(key, 32)
    f32 = jnp.float32

    def nrm(k, shape, scale):
        return scale * jax.random.normal(k, shape, f32)

    def gain(k, shape):
        return 1.0 + 0.01 * jax.random.normal(k, shape, f32)

    f_bias = 3.0 + jnp.linspace(0.0, 3.0, B_HEADS, dtype=f32) + nrm(ks[20], (DEPTH, B_HEADS), 0.1)
    i_bias = nrm(ks[21], (DEPTH, B_HEADS), 0.1)
    return {
        "x_prompt": nrm(ks[0], (BATCH, SEQ, D_MODEL), 1.0),
        "x_sample": nrm(ks[1], (DEC_BATCH, DEC_SEQ, D_MODEL), 1.0),
        "state_hgrn": nrm(ks[2], (DEPTH, DEC_BATCH, A_HEADS, A_DK, A_DV), 0.5),
        "state_mlstm_c": nrm(ks[3], (DEPTH, DEC_BATCH, B_HEADS, B_DK, B_DV), 0.5),
        "state_mlstm_n": nrm(ks[4], (DEPTH, DEC_BATCH, B_HEADS, B_DK), 0.5),
        "state_mlstm_m": nrm(ks[5], (DEPTH, DEC_BATCH, B_HEADS), 1.0),
        "state_gla": nrm(ks[6], (DEPTH, DEC_BATCH, C_HEADS, C_DK, C_DV), 1.0),
        "ffn1_norm": gain(ks[7], (DEPTH, D_MODEL)),
        "ffn1_w_up": nrm(ks[8], (DEPTH, D_MODEL, 2 * D_FF), D_MODEL ** -0.5),
        "ffn1_w_down": nrm(ks[9], (DEPTH, D_FF, D_MODEL), D_FF ** -0.5),
        "mix_norm": gain(ks[10], (DEPTH, D_MODEL)),
        "w_in": nrm(ks[11], (DEPTH, D_MODEL, D_IN), D_MODEL ** -0.5),
        "hgrn_lb_logits": nrm(ks[12], (DEPTH, A_HEADS * A_DK), 0.5),
        "hgrn_norm": gain(ks[13], (DEPTH, A_HEADS * A_DV)),
        "mlstm_gate_bias": jnp.concatenate([i_bias, f_bias], axis=-1),
        "mlstm_norm": gain(ks[14], (DEPTH, B_HEADS * B_DV)),
        "gla_w_a2": nrm(ks[15], (DEPTH, C_RANK, C_HEADS * C_DK), C_RANK ** -0.5),
        "gla_b_a": nrm(ks[16], (DEPTH, C_HEADS * C_DK), 0.1),
        "gla_norm": gain(ks[17], (DEPTH, C_HEADS * C_DV)),
        "w_branch": nrm(ks[18], (DEPTH, N_BRANCH, BRANCH_W, D_MODEL), BRANCH_W ** -0.5),
        "w_out": nrm(ks[19], (DEPTH, D_MODEL, D_MODEL), D_MODEL ** -0.5),
        "ffn2_norm": gain(ks[22], (DEPTH, D_MODEL)),
        "ffn2_w_up": nrm(ks[23], (DEPTH, D_MODEL, 2 * D_FF), D_MODEL ** -0.5),
        "ffn2_w_down": nrm(ks[24], (DEPTH, D_FF, D_MODEL), D_FF ** -0.5),
        "final_norm": gain(ks[25], (D_MODEL,)),
    }


def reference(x_prompt, x_sample, state_hgrn, state_mlstm_c, state_mlstm_n, state_mlstm_m,
              state_gla, ffn1_norm, ffn1_w_up, ffn1_w_down, mix_norm, w_in, hgrn_lb_logits,
              hgrn_norm, mlstm_gate_bias, mlstm_norm, gla_w_a2, gla_b_a, gla_norm, w_branch,
              w_out, ffn2_norm, ffn2_w_up, ffn2_w_down, final_norm):
    weights = (ffn1_norm, ffn1_w_up, ffn1_w_down, mix_norm, w_in, hgrn_lb_logits, hgrn_norm,
               mlstm_gate_bias, mlstm_norm, gla_w_a2, gla_b_a, gla_norm, w_branch, w_out,
               ffn2_norm, ffn2_w_up, ffn2_w_down, final_norm)
    f32 = jnp.float32
    bp = x_prompt.shape[0]
    zero_states = (
        jnp.zeros((DEPTH, bp, A_HEADS, A_DK, A_DV), f32),
        jnp.zeros((DEPTH, bp, B_HEADS, B_DK, B_DV), f32),
        jnp.zeros((DEPTH, bp, B_HEADS, B_DK), f32),
        jnp.zeros((DEPTH, bp, B_HEADS), f32),
        jnp.zeros((DEPTH, bp, C_HEADS, C_DK, C_DV), f32),
    )
    y_prompt, (hgrn_p, mc_p, mn_p, mm_p, gla_p) = run_trunk(x_prompt, zero_states, weights)
    sample_states = (state_hgrn, state_mlstm_c, state_mlstm_n, state_mlstm_m, state_gla)
    y_sample, (hgrn_s, mc_s, mn_s, mm_s, gla_s) = run_trunk(x_sample, sample_states, weights)
    return (y_prompt, y_sample, hgrn_p, mc_p, mn_p, mm_p, gla_p, hgrn_s, mc_s, mn_s, mm_s, gla_s)
```

```python
import math
from contextlib import ExitStack

import numpy as np
import concourse.bass as bass
import concourse.mybir as mybir
from concourse.bass_utils import run_bass_kernel_spmd

F32 = mybir.dt.float32
BF16 = mybir.dt.bfloat16
AF = mybir.ActivationFunctionType
ALU = mybir.AluOpType
AX = mybir.AxisListType

D_MODEL = 1024
DEPTH = 4
D_FF = 2816
D_IN = 8728
EPS = 1e-6
KC = 8
FC = 22
NEG_BIG = -1.0e30


class Buf:
    def __init__(self, name, t, dsem=None):
        self.name = name
        self.t = t
        self.w = None
        self.r = {}
        self.dsem = dsem
        self.dcnt = 0

    def __getitem__(self, k):
        return self.t[k]


class Eng:
    def __init__(self, name, h, sem):
        self.name = name
        self.h = h
        self.sem = sem
        self.cnt = 0
        self.seen = {}


class Prog:
    def __init__(self, nc, es):
        self.nc = nc
        self.es = es
        self.nsem = 0
        self.PE = self._eng("pe", nc.tensor)
        self.ACT = self._eng("act", nc.scalar)
        self.DVE = self._eng("dve", nc.vector)
        self.POOL = self._eng("pool", nc.gpsimd)
        self.SP = self._eng("sp", nc.sync)
        self.engs = [self.PE, self.ACT, self.DVE, self.POOL, self.SP]
        self.n_inst = 0
        self.n_wait = 0

    def sem(self, name):
        self.nsem += 1
        return self.es.enter_context(self.nc.semaphore(name))

    def _eng(self, name, h):
        return Eng(name, h, self.sem("e_" + name))

    def sb(self, name, shape, dtype, dma=False):
        t = self.es.enter_context(self.nc.sbuf_tensor(name, list(shape), dtype))
        return Buf(name, t, self.sem("d_" + name) if dma else None)

    def ps(self, name, shape, dtype=F32):
        t = self.es.enter_context(self.nc.psum_tensor(name, list(shape), dtype))
        return Buf(name, t)

    def _need(self, e, tok, acc):
        sem, val = tok
        if e is self.PE and sem is self.PE.sem:
            return
        if e.seen.get(sem, 0) >= val:
            return
        e.seen[sem] = val
        acc[sem] = max(acc.get(sem, 0), val)

    def _wait(self, e, tok):
        acc = {}
        self._need(e, tok, acc)
        for sem, val in acc.items():
            e.h.wait_ge(sem, val)
            self.n_wait += 1

    def _deps(self, e, reads, writes):
        acc = {}
        for b in reads:
            if b.w is not None:
                self._need(e, b.w, acc)
        for b in writes:
            if b.w is not None:
                self._need(e, b.w, acc)
            for sem, val in b.r.items():
                self._need(e, (sem, val), acc)
        return list(acc.items())

    def _emit(self, e, fn, waits):
        for sem, val in waits[:-1]:
            e.h.wait_ge(sem, val)
            self.n_wait += 1
        inst = fn()
        if waits:
            sem, val = waits[-1]
            inst._wait_ge(sem, val)
        return inst

    def op(self, e, fn, reads=(), writes=()):
        waits = self._deps(e, reads, writes)
        inst = self._emit(e, fn, waits)
        e.cnt += 1
        inst.then_inc(e.sem, 1)
        tok = (e.sem, e.cnt)
        for b in writes:
            b.w = tok
            b.r = {}
        for b in reads:
            if b not in writes:
                b.r[e.sem] = e.cnt
        self.n_inst += 1
        return inst

    def dma(self, q, out, in_, buf, load, **kw):
        if load:
            waits = self._deps(q, (), (buf,))
        else:
            waits = self._deps(q, (buf,), ())
        inst = self._emit(q, lambda: q.h.dma_start(out=out, in_=in_, **kw), waits)
        inst.then_inc(buf.dsem, 16)
        buf.dcnt += 16
        tok = (buf.dsem, buf.dcnt)
        if load:
            buf.w = tok
            buf.r = {}
        else:
            buf.r[buf.dsem] = buf.dcnt
        self.n_inst += 1
        return inst


class Cfg:
    def __init__(self, n_prompt, n_seq, seq_len, depth=DEPTH, T=512, mixer=True, stage=99):
        self.mixer = mixer
        self.stage = stage
        self.n_prompt = n_prompt
        self.n_seq = n_seq
        self.seq_len = seq_len
        self.depth = depth
        self.T = T


def weight_blocks():
    blks = []
    for f in (1, 2):
        for i in range(5):
            blks.append((f"up{f}_{i}", f"ffn{f}_w_up", 1024, [(512 * i, 512), (2816 + 512 * i, 512)]))
        blks.append((f"up{f}_5", f"ffn{f}_w_up", 1024, [(2560, 256), (5376, 256)]))
        for i in range(4):
            blks.append((f"dn{f}_{i}", f"ffn{f}_w_down", 2816, [(256 * i, 256)]))
    for hd in range(4):
        blks.append((f"a_qfg{hd}", "w_in", 1024,
                     [(hd * 128, 128), (512 + hd * 128, 128), (1536 + hd * 128, 128)]))
    blks.append(("a_v", "w_in", 1024, [(1024, 512)]))
    for hd in range(4):
        blks.append((f"b_qko{hd}", "w_in", 1024,
                     [(2048 + hd * 128, 128), (2560 + hd * 128, 128), (3584 + hd * 128, 128)]))
    blks.append(("b_v", "w_in", 1024, [(3072, 512)]))
    blks.append(("small", "w_in", 1024, [(4096, 8), (5640, 16)]))
    for pr in range(2):
        blks.append((f"c_qk{pr}", "w_in", 1024, [(4104 + pr * 128, 128), (4360 + pr * 128, 128)]))
    blks.append(("c_v", "w_in", 1024, [(4616, 512)]))
    blks.append(("c_g", "w_in", 1024, [(5128, 512)]))
    for a in range(3):
        blks.append((f"mg{a}", "w_in", 1024, [(5656 + a * 1024, 1024)]))
    for a in range(3):
        blks.append((f"br{a}", "w_branch", 512, [(0, 1024)]))
    blks.append(("wout", "w_out", 1024, [(0, 1024)]))
    return blks


def build_program(cfg):
    nc = bass.Bass("TRN2", target_bir_lowering=False)
    es = ExitStack()
    with es:
        _build(nc, es, cfg)
    return nc


def host_consts():
    ident = np.eye(128, dtype=np.float32)
    maskT = np.triu(np.ones((64, 64), dtype=np.float32))
    cm = np.ones((128, 512 + 64 + 64), dtype=np.float32)
    cm[:, 0:512:64] = 0.0
    cm[:, 512:576:16] = 0.0
    cm[:, 576:640] = 0.0
    cm[:, 576:640:16] = NEG_BIG
    sel = np.zeros((4, 512), dtype=np.float32)
    for h in range(4):
        sel[h, h * 128:(h + 1) * 128] = 1.0
    return {"ident_in": ident, "maskT_in": maskT, "cmask_in": cm, "sel_in": sel}


def _build(nc, es, cfg):
    P = Prog(nc, es)
    T = cfg.T
    depth = cfg.depth
    NP = cfg.n_prompt
    NSEQ = cfg.n_seq
    SL = cfg.seq_len
    NS = NSEQ * SL
    NSET = max(depth, NSEQ, 1)
    PE, ACT, DVE, POOL, SP = P.PE, P.ACT, P.DVE, P.POOL, P.SP

    def din(name, shape):
        return nc.dram_tensor(name, list(shape), F32, kind="ExternalInput").ap()

    def dout(name, shape):
        return nc.dram_tensor(name, list(shape), F32, kind="ExternalOutput").ap()

    xp = din("x_prompt", [max(NP, 1), D_MODEL])
    yp = dout("y_prompt", [max(NP, 1), D_MODEL])
    xs = din("x_sample", [max(NS, 1), D_MODEL])
    ys = dout("y_sample", [max(NS, 1), D_MODEL])
    nsq = max(NSEQ, 1)
    st_hgrn = din("st_hgrn", [depth, nsq, 4, 128, 128])
    st_c = din("st_c", [depth, nsq, 4, 128, 128])
    st_n = din("st_n", [depth, nsq * 4, 128])
    st_m = din("st_m", [depth, nsq, 4])
    st_gla = din("st_gla", [depth, nsq, 4, 64, 128])
    o_hgrn_p = dout("hgrn_p", [depth, 4, 128, 128])
    o_c_p = dout("c_p", [depth, 4, 128, 128])
    o_n_p = dout("n_p", [depth, 4, 128])
    o_m_p = dout("m_p", [depth, 4])
    o_gla_p = dout("gla_p", [depth, 4, 64, 128])
    o_hgrn_s = dout("hgrn_s", [depth, nsq, 4, 128, 128])
    o_c_s = dout("c_s", [depth, nsq, 4, 128, 128])
    o_n_s = dout("n_s", [depth, nsq * 4, 128])
    o_m_s = dout("m_s", [depth, nsq, 4])
    o_gla_s = dout("gla_s", [depth, nsq, 4, 64, 128])
    w_dram = {
        "ffn1_w_up": din("ffn1_w_up", [depth, 1024, 2 * D_FF]),
        "ffn1_w_down": din("ffn1_w_down", [depth, D_FF, 1024]),
        "ffn2_w_up": din("ffn2_w_up", [depth, 1024, 2 * D_FF]),
        "ffn2_w_down": din("ffn2_w_down", [depth, D_FF, 1024]),
        "w_in": din("w_in", [depth, 1024, D_IN]),
        "w_branch": din("w_branch", [depth, 3, 512, 1024]),
        "w_out": din("w_out", [depth, 1024, 1024]),
    }
    norms = din("norms", [3 * depth + 1, 1024])
    rows512 = din("rows512", [4 * depth, 512])
    gla_b_a = din("gla_b_a", [depth, 256])
    gate_bias = din("mlstm_gate_bias", [depth, 8])
    gla_w2 = din("gla_w_a2", [depth, 16, 256])
    identd = din("ident_in", [128, 128])
    maskd = din("maskT_in", [64, 64])
    cmaskd = din("cmask_in", [128, 640])
    seld = din("sel_in", [4, 512])

    blks = weight_blocks()
    scratch = {}
    for l in range(depth):
        for key, src, K, cols in blks:
            kc = K // 128
            W = sum(n for _, n in cols)
            scratch[(l, key)] = (nc.dram_tensor(f"ws_{l}_{key}", [128, kc * W], BF16, kind="Internal").ap(), kc, W)

    cast_sem = P.sem("cast")
    n_cast = 0
    for l in range(depth):
        for key, src, K, cols in blks:
            sap, kc, W = scratch[(l, key)]
            sview = sap.rearrange("p (k w) -> p k w", k=kc)
            off = 0
            for c0, n in cols:
                if src == "w_branch":
                    a = int(key[2])
                    srcap = w_dram[src][l, a, :, c0:c0 + n]
                else:
                    srcap = w_dram[src][l, :, c0:c0 + n]
                srcap = srcap.rearrange("(k p) n -> p k n", p=128)
                POOL.h.dma_start(out=sview[:, :, off:off + n], in_=srcap).then_inc(cast_sem, 16)
                n_cast += 1
                off += n
    cast_tok = (cast_sem, 16 * n_cast)

    NSLOT = 3
    SLOTW = 8192
    slots = [P.sb(f"wslot{i}", [128, SLOTW], BF16, dma=True) for i in range(NSLOT)]
    slot_i = [0]

    def load_block(l, key):
        sap, kc, W = scratch[(l, key)]
        s = slots[slot_i[0] % NSLOT]
        slot_i[0] += 1
        P._wait(SP, cast_tok)
        P.dma(SP, s.t[:, 0:kc * W], sap[:, :], s, load=True)
        return s, s.t[:, 0:kc * W].rearrange("p (k w) -> p k w", k=kc)

    X = P.sb("X", [128, KC, T], F32)
    H = P.sb("H", [128, KC, T], BF16)
    ARENA = P.sb("ARENA", [128, FC * T // 2], F32)
    HIDv = ARENA.t[:, :].bitcast(BF16).rearrange("p (c t) -> p c t", c=FC)
    MERGEDv = ARENA.t[:, 0:KC * T].rearrange("p (c t) -> p c t", c=KC)
    SQ = P.sb("SQ", [128, KC, T], BF16)
    RSTD = P.sb("RSTD", [128, T], F32)
    SG = [P.sb(f"SG{i}", [128, T], F32) for i in range(2)]
    XIO = [P.sb(f"XIO{i}", [128, D_MODEL], F32, dma=True) for i in range(2)]
    csem = P.sem("consts")
    cbufs = []

    def cload(name, shape, dtype, src, q=None, **kw):
        b = P.sb(name, shape, dtype)
        b.dsem = csem
        cbufs.append(b)
        (q or SP).h.dma_start(out=b.t[:], in_=src, **kw).then_inc(csem, 16)
        return b

    ident = cload("ident", [128, 128], F32, identd[:, :])
    NROW = XIO[0]
    R5ROW = XIO[1]
    P.dma(POOL, XIO[0].t[0:3 * depth + 1, :], norms[:, :], XIO[0], load=True)
    P.dma(POOL, XIO[1].t[0:4 * depth, 0:512], rows512[:, :], XIO[1], load=True)
    P.dma(POOL, XIO[1].t[0:depth, 512:768], gla_b_a[:, :], XIO[1], load=True)
    maskT = cload("maskT", [64, 64], F32, maskd[:, :])
    CMASK = cload("CMASK", [128, 640], F32, cmaskd[:, :])
    SEL = cload("SEL", [4, 512], F32, seld[:, :])
    W2b = P.sb("W2b", [16, depth, 256], BF16, dma=True)
    P.dma(POOL, W2b.t[:], gla_w2.rearrange("l r c -> r l c"), W2b, load=True)
    GB = cload("GB", [4, depth, 2], F32, gate_bias.rearrange("l (g h) -> h l g", g=2), allow_slow_non_contiguous=True)
    for b in cbufs:
        b.w = (csem, 16 * len(cbufs))

    eps_c = P.sb("eps_c", [128, 1], F32)
    one_c = P.sb("one_c", [128, 1], F32)
    ones_m = P.sb("ones_m", [128, 128], BF16)
    ones_h = P.sb("ones_h", [128, 128], BF16)
    ones64 = P.sb("ones64", [64, 128], BF16)
    identb = P.sb("identb", [128, 128], BF16)
    maskS = P.sb("maskS", [64, 64], F32)
    NW = P.sb("NW", [128, KC, 16], F32)
    R5 = P.sb("R5", [128, 4, 16], F32)
    LB = P.sb("LB", [128, 4, 4], F32)
    OML = P.sb("OML", [128, 4, 4], F32)
    NBA = P.sb("NBA", [64, 4, 4], F32)
    NGBF = P.sb("NGBF", [4, 4], F32)
    ones4 = P.sb("ones4", [4, 512], F32)
    zeros4 = P.sb("zeros4", [4, 512], F32)

    QH = [P.sb(f"QH{i}", [128, T], BF16) for i in range(4)]
    KT = [P.sb(f"KT{i}", [128, T], BF16) for i in range(4)]
    G = [P.sb(f"G{i}", [128, T], BF16) for i in range(4)]
    Y = [P.sb(f"Y{i}", [128, T], BF16) for i in range(4)]
    V = P.sb("V", [64, 8, 512], BF16)
    TMP = [P.sb(f"TMP{i}", [128, T], F32) for i in range(6)]
    DL = [P.sb(f"DL{i}", [128, 8], F32) for i in range(4)]
    ATM = [P.sb(f"ATM{i}", [64, 64], BF16) for i in range(4)]
    KTOK = [P.sb(f"KTOK{i}", [64, 128], BF16) for i in range(4)]
    SQH = P.sb("SQH", [128, T], BF16)
    LR = P.sb("LR", [16, T], BF16)
    PTK = P.sb("PTK", [64, 32], F32)
    RB = P.sb("RB", [128, 4, 8], F32)
    R4 = P.sb("R4", [4, 8], F32)
    MP = P.sb("MP", [4, 8], F32)
    MO = P.sb("MO", [4, 8], F32, dma=True)
    M0 = P.sb("M0", [4, 8], F32, dma=True)
    NT = P.sb("NT", [128, 16], F32)
    NROWS = P.sb("NROWS", [16, 128], F32, dma=True)
    FCY = P.sb("FCY", [4, NSET], F32)
    MCY = P.sb("MCY", [4, NSET], F32)
    SAb = [P.sb(f"SAb{i}", [128, 128], BF16) for i in range(4)]
    CBb = [P.sb(f"CBb{i}", [128, 256], BF16) for i in range(4)]
    SA = [[P.sb(f"SA{s}_{i}", [128, 128], F32, dma=True) for i in range(4)] for s in range(NSET)]
    CB = [[P.sb(f"CB{s}_{i}", [128, 256], F32, dma=True) for i in range(4)] for s in range(NSET)]
    SC = [[P.sb(f"SC{s}_{i}", [64, 128], F32, dma=True) for i in range(4)] for s in range(NSET)]

    PSB = [P.ps(f"psb{i}", [128, 512], F32) for i in range(3)]
    OACC = [P.ps(f"oacc{i}", [128, 512], F32) for i in range(4)]
    PTB = es.enter_context(nc.psum_tensor("ptb", [128, 1024], BF16))
    PTR = [Buf(f"ptr{i}", PTB[:, i * 128:(i + 1) * 128]) for i in range(8)]
    ps_i = [0]
    pt_i = [0]

    def psum(all7=False):
        pool = PSB + OACC if all7 else PSB
        b = pool[ps_i[0] % len(pool)]
        ps_i[0] += 1
        return b

    def ptr():
        b = PTR[pt_i[0] % 8]
        pt_i[0] += 1
        return b

    rot = {"atm": 0, "ktok": 0}

    def act(out, in_, func, reads, writes, **kw):
        return P.op(ACT, lambda: nc.scalar.activation(out=out, in_=in_, func=func, **kw), reads=reads, writes=writes)

    def mm(out, lhsT, rhs, reads, writes, start=True, stop=True):
        return P.op(PE, lambda: nc.tensor.matmul(out, lhsT, rhs, start=start, stop=stop), reads=reads, writes=writes)

    def tr(out, in_, idn, reads, writes):
        return P.op(PE, lambda: nc.tensor.transpose(out, in_, idn), reads=reads, writes=writes)

    def tt(e, out, in0, in1, op, reads, writes):
        return P.op(e, lambda: e.h.tensor_tensor(out=out, in0=in0, in1=in1, op=op), reads=reads, writes=writes)

    def ts(e, out, in0, s1, s2, op0, op1, reads, writes):
        if s2 is None:
            return P.op(e, lambda: e.h.tensor_scalar(out=out, in0=in0, scalar1=s1, scalar2=None, op0=op0),
                        reads=reads, writes=writes)
        return P.op(e, lambda: e.h.tensor_scalar(out=out, in0=in0, scalar1=s1, scalar2=s2, op0=op0, op1=op1),
                    reads=reads, writes=writes)

    def stt(e, out, in0, scalar, in1, op0, op1, reads, writes):
        return P.op(e, lambda: e.h.scalar_tensor_tensor(out=out, in0=in0, scalar=scalar, in1=in1, op0=op0, op1=op1),
                    reads=reads, writes=writes)

    def cp(e, out, in_, reads, writes):
        if e is ACT:
            return act(out, in_, AF.Copy, reads, writes)
        return P.op(e, lambda: e.h.tensor_copy(out=out, in_=in_), reads=reads, writes=writes)

    def scan(out, d0, d1, init, op0, op1, reads, writes):
        return P.op(DVE, lambda: nc.vector.tensor_tensor_scan(out=out, data0=d0, data1=d1, initial=init, op0=op0, op1=op1),
                    reads=reads, writes=writes)

    def ms(e, buf, ap, val):
        return P.op(e, lambda: e.h.memset(ap, val), writes=(buf,))

    ms(DVE, ones_m, ones_m.t[:], 1.0 / 1024.0)
    ms(DVE, ones_h, ones_h.t[:], 1.0 / 128.0)
    ms(DVE, ones64, ones64.t[:], 1.0)
    ms(DVE, eps_c, eps_c.t[:], EPS)
    ms(DVE, one_c, one_c.t[:], 1.0)
    ms(DVE, ones4, ones4.t[:], 1.0)
    ms(DVE, zeros4, zeros4.t[:], 0.0)
    ms(DVE, FCY, FCY.t[:], 0.0)
    ms(DVE, MCY, MCY.t[:], 0.0)
    cp(DVE, identb.t[:], ident.t[:], (ident,), (identb,))
    ts(DVE, maskS.t[:], maskT.t[:], 128.0 ** -0.5, None, ALU.mult, None, (maskT,), (maskS,))
    nrows = 3 * depth + 1
    for kc in range(KC):
        pb = psum()
        tr(pb.t[:, 0:nrows], NROW.t[0:nrows, kc * 128:(kc + 1) * 128], ident.t[0:nrows, 0:nrows], (NROW, ident), (pb,))
        cp(DVE, NW.t[:, kc, 0:nrows], pb.t[:, 0:nrows], (pb,), (NW,))
    for hd in range(4):
        pb = psum()
        tr(pb.t[:, 0:4 * depth], R5ROW.t[0:4 * depth, hd * 128:(hd + 1) * 128], ident.t[0:4 * depth, 0:4 * depth],
           (R5ROW, ident), (pb,))
        cp(DVE, R5.t[:, hd, 0:4 * depth], pb.t[:, 0:4 * depth], (pb,), (R5,))
    for hd in range(4):
        pb = psum()
        tr(pb.t[0:64, 0:depth], R5ROW.t[0:depth, 512 + hd * 64:512 + (hd + 1) * 64], ident.t[0:depth, 0:depth],
           (R5ROW, ident), (pb,))
        ts(DVE, NBA.t[0:64, hd, 0:depth], pb.t[0:64, 0:depth], -1.0, None, ALU.mult, None, (pb,), (NBA,))
    ts(DVE, NGBF.t[:, 0:depth], GB.t[:, :, 1], -1.0, None, ALU.mult, None, (GB,), (NGBF,))
    EX = TMP[0]
    act(EX.t[:, 0:4 * depth].rearrange("p (h l) -> p h l", h=4), R5.t[:, :, 0:depth], AF.Exp, (R5,), (EX,))
    exv = EX.t[:, 0:4 * depth].rearrange("p (h l) -> p h l", h=4)
    P.op(DVE, lambda: nc.vector.reduce_sum(out=EX.t[:, 16:20], in_=exv, axis=AX.X), reads=(EX,), writes=(EX,))
    P.op(DVE, lambda: nc.vector.reciprocal(out=EX.t[:, 16:20], in_=EX.t[:, 16:20]), reads=(EX,), writes=(EX,))
    tt(DVE, exv, exv, EX.t[:, 16:20].unsqueeze(2).broadcast_to([128, 4, depth]), ALU.mult, (EX,), (EX,))
    ms(DVE, LB, LB.t[:], 0.0)
    for l in range(1, depth):
        tt(DVE, LB.t[:, :, l], LB.t[:, :, l - 1], exv[:, :, l], ALU.add, (LB, EX), (LB,))
    ts(DVE, OML.t[:], LB.t[:], -1.0, 1.0, ALU.mult, ALU.add, (LB,), (OML,))

    def rstd_from(pb, n, dst=None):
        dst = dst or RSTD
        act(dst.t[:, 0:n], pb.t[:, 0:n], AF.Ln, (pb, eps_c), (dst,), bias=eps_c.t[:, 0:1])
        act(dst.t[:, 0:n], dst.t[:, 0:n], AF.Exp, (dst,), (dst,), scale=-0.5)

    def x_stats(n):
        for kc in range(KC):
            act(SQ.t[:, kc, 0:n], X.t[:, kc, 0:n], AF.Square, (X,), (SQ,))
        pb = psum()
        for kc in range(KC):
            mm(pb.t[:, 0:n], ones_m.t[:], SQ.t[:, kc, 0:n], (ones_m, SQ), (pb,), start=(kc == 0), stop=(kc == KC - 1))
        rstd_from(pb, n)

    def rmsnorm(nidx, n):
        x_stats(n)
        for kc in range(KC):
            stt(DVE, H.t[:, kc, 0:n], X.t[:, kc, 0:n], NW.t[:, kc, nidx:nidx + 1], RSTD.t[:, 0:n], ALU.mult, ALU.mult,
                (X, NW, RSTD), (H,))

    def ffn(l, f, n):
        rmsnorm(3 * l + (0 if f == 1 else 2), n)
        for i in range(6):
            s, wv = load_block(l, f"up{f}_{i}")
            nch = 4 if i < 5 else 2
            half = nch * 128
            for j in range(nch):
                pg = psum(True)
                pu = psum(True)
                for kc in range(KC):
                    mm(pg.t[:, 0:n], wv[:, kc, j * 128:(j + 1) * 128], H.t[:, kc, 0:n], (s, H), (pg,),
                       start=(kc == 0), stop=(kc == KC - 1))
                for kc in range(KC):
                    mm(pu.t[:, 0:n], wv[:, kc, half + j * 128:half + (j + 1) * 128], H.t[:, kc, 0:n], (s, H), (pu,),
                       start=(kc == 0), stop=(kc == KC - 1))
                sg = SG[(i * 4 + j) % 2]
                act(sg.t[:, 0:n], pg.t[:, 0:n], AF.Silu, (pg,), (sg,))
                ch = i * 4 + j
                tt(DVE, HIDv[:, ch, 0:n], sg.t[:, 0:n], pu.t[:, 0:n], ALU.mult, (sg, pu), (ARENA,))
        for i in range(4):
            s, wv = load_block(l, f"dn{f}_{i}")
            for j in range(2):
                oc = i * 2 + j
                pb = psum(True)
                for kc in range(FC):
                    mm(pb.t[:, 0:n], wv[:, kc, j * 128:(j + 1) * 128], HIDv[:, kc, 0:n], (s, ARENA), (pb,),
                       start=(kc == 0), stop=(kc == FC - 1))
                stt(DVE, X.t[:, oc, 0:n], pb.t[:, 0:n], 0.5, X.t[:, oc, 0:n], ALU.mult, ALU.add, (pb, X), (X,))

    kio = [0]

    def load_x(src_ap, n):
        for tb in range((n + 127) // 128):
            r = min(128, n - tb * 128)
            xin = XIO[kio[0] % 2]
            kio[0] += 1
            P.dma(POOL, xin.t[0:r, :], src_ap[tb * 128:tb * 128 + r, :], xin, load=True)
            for g in range(2):
                pb = psum()
                for q in range(4):
                    kc = g * 4 + q
                    tr(pb.t[:, q * 128:q * 128 + r], xin.t[0:r, kc * 128:(kc + 1) * 128], ident.t[0:r, 0:r],
                       (xin, ident), (pb,))
                act(X.t[:, g * 4:(g + 1) * 4, tb * 128:tb * 128 + r],
                    pb.t[:, :].rearrange("p (q t) -> p q t", q=4)[:, :, 0:r], AF.Copy, (pb,), (X,))

    def store_y(dst_ap, n):
        x_stats(n)
        fi = 3 * depth
        for kc in range(KC):
            stt(DVE, X.t[:, kc, 0:n], X.t[:, kc, 0:n], NW.t[:, kc, fi:fi + 1], RSTD.t[:, 0:n], ALU.mult, ALU.mult,
                (X, NW, RSTD), (X,))
        for tb in range((n + 127) // 128):
            r = min(128, n - tb * 128)
            yo = XIO[kio[0] % 2]
            kio[0] += 1
            for g in range(2):
                pb = psum()
                for q in range(4):
                    kc = g * 4 + q
                    tr(pb.t[0:r, q * 128:(q + 1) * 128], X.t[:, kc, tb * 128:tb * 128 + r], ident.t[:, :], (X, ident), (pb,))
                act(yo.t[0:r, g * 512:(g + 1) * 512], pb.t[0:r, :], AF.Copy, (pb,), (yo,))
            P.dma(POOL, dst_ap[tb * 128:tb * 128 + r, :], yo.t[0:r, :], yo, load=False)

    def head_norm_scale(src_ap, src_bufs, n, wcol, out_ap, out_buf, gate=None):
        act(SQH.t[:, 0:n], src_ap, AF.Square, src_bufs, (SQH,))
        pm = psum()
        mm(pm.t[:, 0:n], ones_h.t[:], SQH.t[:, 0:n], (ones_h, SQH), (pm,))
        rstd_from(pm, n)
        if gate is None:
            stt(DVE, out_ap, src_ap, wcol, RSTD.t[:, 0:n], ALU.mult, ALU.mult, src_bufs + (R5, RSTD), (out_buf,))
        else:
            t = TMP[2]
            stt(DVE, t.t[:, 0:n], src_ap, wcol, RSTD.t[:, 0:n], ALU.mult, ALU.mult, src_bufs + (R5, RSTD), (t,))
            tt(DVE, out_ap, t.t[:, 0:n], gate.t[:, 0:n], ALU.mult, (t, gate), (out_buf,))

    def proj_v(l, key, n, L, NCH):
        s, wv = load_block(l, key)
        for c in range(NCH):
            pb = psum()
            for kc in range(KC):
                mm(pb.t[0:L, 0:512], H.t[:, kc, c * L:(c + 1) * L], wv[:, kc, 0:512], (s, H), (pb,),
                   start=(kc == 0), stop=(kc == KC - 1))
            cp(ACT, V.t[0:L, c, :], pb.t[0:L, 0:512], (pb,), (V,))

    def proj3(s, wv, n, ncol):
        outs = []
        for j in range(ncol):
            pb = psum()
            for kc in range(KC):
                mm(pb.t[:, 0:n], wv[:, kc, j * 128:(j + 1) * 128], H.t[:, kc, 0:n], (s, H), (pb,),
                   start=(kc == 0), stop=(kc == KC - 1))
            outs.append(pb)
        return outs

    def merge_branch(l, a, n):
        sbr, wbr = load_block(l, f"br{a}")
        smg, wmg = load_block(l, f"mg{a}")
        for oc in range(KC):
            pp = psum()
            for hd in range(4):
                mm(pp.t[:, 0:n], wbr[:, hd, oc * 128:(oc + 1) * 128], Y[hd].t[:, 0:n], (sbr, Y[hd]), (pp,),
                   start=(hd == 0), stop=(hd == 3))
            pg = psum()
            for kc in range(KC):
                mm(pg.t[:, 0:n], wmg[:, kc, oc * 128:(oc + 1) * 128], H.t[:, kc, 0:n], (smg, H), (pg,),
                   start=(kc == 0), stop=(kc == KC - 1))
            sg = SG[oc % 2]
            act(sg.t[:, 0:n], pg.t[:, 0:n], AF.Sigmoid, (pg,), (sg,))
            if a == 0:
                tt(DVE, MERGEDv[:, oc, 0:n], sg.t[:, 0:n], pp.t[:, 0:n], ALU.mult, (sg, pp), (ARENA,))
            else:
                tt(DVE, sg.t[:, 0:n], sg.t[:, 0:n], pp.t[:, 0:n], ALU.mult, (sg, pp), (sg,))
                tt(POOL, MERGEDv[:, oc, 0:n], MERGEDv[:, oc, 0:n], sg.t[:, 0:n], ALU.add, (ARENA, sg), (ARENA,))

    def mixer(l, n, L, NCH, sample, last):
        rmsnorm(3 * l + 1, n)
        RM = CMASK.t[:, 512:512 + n] if sample else CMASK.t[:, 0:n]

        def sset(c):
            return c if sample else l

        def chunked(ap):
            return ap.rearrange("p (c t) -> p c t", c=NCH)

        if cfg.stage <= 0:
            return
        proj_v(l, "a_v", n, L, NCH)
        for hd in range(4):
            s, wv = load_block(l, f"a_qfg{hd}")
            pq, pf, pg = proj3(s, wv, n, 3)
            t0, t1, t2, t3 = TMP[0], TMP[1], TMP[2], TMP[3]
            act(t0.t[:, 0:n], pf.t[:, 0:n], AF.Sigmoid, (pf,), (t0,))
            ts(DVE, t0.t[:, 0:n], t0.t[:, 0:n], OML.t[:, hd, l:l + 1], LB.t[:, hd, l:l + 1], ALU.mult, ALU.add,
               (t0, OML, LB), (t0,))
            ts(POOL, t1.t[:, 0:n], t0.t[:, 0:n], -1.0, 1.0, ALU.mult, ALU.add, (t0,), (t1,))
            act(t0.t[:, 0:n], t0.t[:, 0:n], AF.Ln, (t0,), (t0,))
            scan(t2.t[:, 0:n], RM, t0.t[:, 0:n], 0.0, ALU.mult, ALU.add, (CMASK, t0), (t2,))
            act(t0.t[:, 0:n], t2.t[:, 0:n], AF.Exp, (t2,), (t0,))
            act(t3.t[:, 0:n], t2.t[:, 0:n], AF.Exp, (t2,), (t3,), scale=-1.0)
            act(t2.t[:, 0:n], pq.t[:, 0:n], AF.Silu, (pq,), (t2,))
            tt(DVE, QH[hd].t[:, 0:n], t2.t[:, 0:n], t0.t[:, 0:n], ALU.mult, (t2, t0), (QH[hd],))
            tt(DVE, KT[hd].t[:, 0:n], t1.t[:, 0:n], t3.t[:, 0:n], ALU.mult, (t1, t3), (KT[hd],))
            cp(DVE, DL[hd].t[:, 0:NCH], chunked(t0.t[:, 0:n])[:, :, L - 1], (t0,), (DL[hd],))
            act(G[hd].t[:, 0:n], pg.t[:, 0:n], AF.Sigmoid, (pg,), (G[hd],))
        if cfg.stage <= 1:
            return
        for c in range(NCH):
            cs = slice(c * L, (c + 1) * L)
            st = sset(c)
            for hd in range(4):
                S = SA[st][hd]
                if c == 0 or sample:
                    cp(ACT, SAb[hd].t[:], S.t[:], (S,), (SAb[hd],))
                pa = psum()
                mm(pa.t[0:L, 0:L], KT[hd].t[:, cs], QH[hd].t[:, cs], (KT[hd], QH[hd]), (pa,))
                am = ATM[rot["atm"] % 4]
                rot["atm"] += 1
                tt(DVE, am.t[0:L, 0:L], pa.t[0:L, 0:L], maskT.t[0:L, 0:L], ALU.mult, (pa, maskT), (am,))
                mm(OACC[hd].t[:, cs], V.t[0:L, c, hd * 128:(hd + 1) * 128], am.t[0:L, 0:L], (V, am), (OACC[hd],),
                   start=True, stop=False)
                mm(OACC[hd].t[:, cs], SAb[hd].t[:], QH[hd].t[:, cs], (SAb[hd], QH[hd]), (OACC[hd],), start=False, stop=True)
                pt = ptr()
                tr(pt.t[0:L, 0:128], KT[hd].t[:, cs], identb.t[:], (KT[hd], identb), (pt,))
                kk = KTOK[rot["ktok"] % 4]
                rot["ktok"] += 1
                cp(ACT, kk.t[0:L, :], pt.t[0:L, 0:128], (pt,), (kk,))
                pu = psum()
                mm(pu.t[:, 0:128], kk.t[0:L, :], V.t[0:L, c, hd * 128:(hd + 1) * 128], (kk, V), (pu,))
                tt(DVE, S.t[:], S.t[:], pu.t[:, 0:128], ALU.add, (S, pu), (S,))
                if (not sample) and c < NCH - 1:
                    act(SAb[hd].t[:], S.t[:], AF.Copy, (S, DL[hd]), (SAb[hd],), scale=DL[hd].t[:, c:c + 1])
                ts(POOL, S.t[:], S.t[:], DL[hd].t[:, c:c + 1], None, ALU.mult, None, (S, DL[hd]), (S,))
                if sample:
                    P.dma(SP, o_hgrn_s[l, c, hd], S.t[:], S, load=False)
                elif last and c == NCH - 1:
                    P.dma(SP, o_hgrn_p[l, hd], S.t[:], S, load=False)
        if cfg.stage <= 2:
            return
        for hd in range(4):
            t4 = TMP[4]
            tt(DVE, t4.t[:, 0:n], OACC[hd].t[:, 0:n], G[hd].t[:, 0:n], ALU.mult, (OACC[hd], G[hd]), (t4,))
            head_norm_scale(t4.t[:, 0:n], (t4,), n, R5.t[:, hd, depth + l:depth + l + 1], Y[hd].t[:, 0:n], Y[hd])
        merge_branch(l, 0, n)

        if cfg.stage <= 3:
            return
        ssm, wsm = load_block(l, "small")
        pi = psum()
        pf = psum()
        plr = psum()
        for kc in range(KC):
            mm(pi.t[0:4, 0:n], wsm[:, kc, 0:4], H.t[:, kc, 0:n], (ssm, H), (pi,), start=(kc == 0), stop=(kc == KC - 1))
        for kc in range(KC):
            mm(pf.t[0:4, 0:n], wsm[:, kc, 4:8], H.t[:, kc, 0:n], (ssm, H), (pf,), start=(kc == 0), stop=(kc == KC - 1))
        for kc in range(KC):
            mm(plr.t[0:16, 0:n], wsm[:, kc, 8:24], H.t[:, kc, 0:n], (ssm, H), (plr,), start=(kc == 0), stop=(kc == KC - 1))
        cp(ACT, LR.t[0:16, 0:n], plr.t[0:16, 0:n], (plr,), (LR,))
        g0, g1, g2, g3 = TMP[0], TMP[1], TMP[2], TMP[3]
        act(g0.t[0:4, 0:n], pf.t[0:4, 0:n], AF.Exp, (pf, NGBF), (g0,), scale=-1.0, bias=NGBF.t[:, l:l + 1])
        act(g0.t[0:4, 0:n], g0.t[0:4, 0:n], AF.Ln, (g0,), (g0,), bias=one_c.t[0:4, 0:1])
        if sample:
            scan(g1.t[0:4, 0:n], CMASK.t[0:4, 512:512 + n], g0.t[0:4, 0:n], 0.0, ALU.mult, ALU.add, (CMASK, g0), (g1,))
        else:
            scan(g1.t[0:4, 0:n], ones4.t[0:4, 0:n], g0.t[0:4, 0:n], FCY.t[:, l:l + 1], ALU.mult, ALU.add,
                 (ones4, g0, FCY), (g1,))
        stt(DVE, g2.t[0:4, 0:n], pi.t[0:4, 0:n], GB.t[:, l, 0:1], g1.t[0:4, 0:n], ALU.add, ALU.add,
            (pi, GB, g1), (g2,))
        if sample:
            P.dma(SP, M0.t[0:4, 0:NCH], st_m[l].rearrange("s h -> h s"), M0, load=True, allow_slow_non_contiguous=True)
            ga = TMP[4]
            cp(DVE, ga.t[0:4, 0:n], g2.t[0:4, 0:n], (g2,), (ga,))
            a0 = chunked(ga.t[0:4, 0:n])[:, :, 0]
            tt(DVE, a0, a0, M0.t[0:4, 0:NCH], ALU.max, (ga, M0), (ga,))
            scan(g3.t[0:4, 0:n], CMASK.t[0:4, 576:576 + n], ga.t[0:4, 0:n], NEG_BIG, ALU.add, ALU.max, (CMASK, ga), (g3,))
        else:
            scan(g3.t[0:4, 0:n], zeros4.t[0:4, 0:n], g2.t[0:4, 0:n], MCY.t[:, l:l + 1], ALU.add, ALU.max,
                 (zeros4, g2, MCY), (g3,))
        mcv = chunked(g3.t[0:4, 0:n])[:, :, L - 1]
        if sample:
            cp(DVE, MP.t[0:4, 0:NCH], M0.t[0:4, 0:NCH], (M0,), (MP,))
        else:
            cp(DVE, MP.t[0:4, 0:1], MCY.t[:, l:l + 1], (MCY,), (MP,))
            if NCH > 1:
                cp(DVE, MP.t[0:4, 1:NCH], chunked(g3.t[0:4, 0:n])[:, 0:NCH - 1, L - 1], (g3,), (MP,))
        tt(DVE, R4.t[0:4, 0:NCH], MP.t[0:4, 0:NCH], mcv, ALU.subtract, (MP, g3), (R4,))
        act(R4.t[0:4, 0:NCH], R4.t[0:4, 0:NCH], AF.Exp, (R4,), (R4,))
        tt(DVE, MO.t[0:4, 0:NCH], mcv, chunked(g1.t[0:4, 0:n])[:, :, L - 1], ALU.subtract, (g3, g1), (MO,))
        mcb = mcv.unsqueeze(2).broadcast_to([4, NCH, L])
        g4, g5 = TMP[4], TMP[5]
        tt(DVE, chunked(g4.t[0:4, 0:n]), chunked(g2.t[0:4, 0:n]), mcb, ALU.subtract, (g2, g3), (g4,))
        act(g4.t[0:4, 0:n], g4.t[0:4, 0:n], AF.Exp, (g4,), (g4,))
        tt(DVE, chunked(g5.t[0:4, 0:n]), chunked(g1.t[0:4, 0:n]), mcb, ALU.subtract, (g1, g3), (g5,))
        act(g5.t[0:4, 0:n], g5.t[0:4, 0:n], AF.Exp, (g5,), (g5,))
        if not sample:
            cp(DVE, FCY.t[:, l:l + 1], g1.t[0:4, n - 1:n], (g1,), (FCY,))
            cp(DVE, MCY.t[:, l:l + 1], g3.t[0:4, n - 1:n], (g3,), (MCY,))
        if cfg.stage <= 4:
            return
        pp_ = psum()
        for c in range(NCH):
            tr(pp_.t[0:L, c * 4:(c + 1) * 4], g4.t[0:4, c * L:(c + 1) * L], ident.t[0:4, 0:4], (g4, ident), (pp_,))
        cp(ACT, PTK.t[0:L, 0:NCH * 4], pp_.t[0:L, 0:NCH * 4], (pp_,), (PTK,))
        for hd in range(4):
            pr_ = psum()
            mm(pr_.t[:, 0:NCH], SEL.t[0:4, hd * 128:(hd + 1) * 128], R4.t[0:4, 0:NCH], (SEL, R4), (pr_,))
            cp(ACT, RB.t[:, hd, 0:NCH], pr_.t[:, 0:NCH], (pr_,), (RB,))
        if sample:
            P.dma(SP, o_m_s[l].rearrange("s h -> h s"), MO.t[0:4, 0:NCH], MO, load=False, allow_slow_non_contiguous=True)
        elif last:
            P.dma(SP, o_m_p[l].rearrange("(h o) -> h o", o=1), MO.t[0:4, NCH - 1:NCH], MO, load=False)
        if sample:
            P.dma(SP, NROWS.t[0:4 * NCH, :], st_n[l], NROWS, load=True)
            pn = psum()
            tr(pn.t[:, 0:4 * NCH], NROWS.t[0:4 * NCH, :], ident.t[0:4 * NCH, 0:4 * NCH], (NROWS, ident), (pn,))
            cp(ACT, NT.t[:, 0:4 * NCH], pn.t[:, 0:4 * NCH], (pn,), (NT,))
        if cfg.stage <= 5:
            return
        proj_v(l, "b_v", n, L, NCH)
        for hd in range(4):
            s, wv = load_block(l, f"b_qko{hd}")
            pq, pk, po = proj3(s, wv, n, 3)
            cp(ACT, QH[hd].t[:, 0:n], pq.t[:, 0:n], (pq,), (QH[hd],))
            cp(DVE, KT[hd].t[:, 0:n], pk.t[:, 0:n], (pk,), (KT[hd],))
            act(G[hd].t[:, 0:n], po.t[:, 0:n], AF.Sigmoid, (po,), (G[hd],))
        if cfg.stage <= 6:
            return
        for hp in range(2):
            for c in range(NCH):
                cs = slice(c * L, (c + 1) * L)
                st = sset(c)
                for e in range(2):
                    hd = hp * 2 + e
                    C = CB[st][hd]
                    NUM = OACC[2 * e]
                    DEN = OACC[2 * e + 1]
                    if sample:
                        cp(DVE, C.t[:, 128:256], NT.t[:, c * 4 + hd:c * 4 + hd + 1].broadcast_to([128, 128]), (NT,), (C,))
                    if c == 0 or sample:
                        act(CBb[hd].t[:], C.t[:], AF.Copy, (C, RB), (CBb[hd],), scale=RB.t[:, hd, c:c + 1])
                    pa = psum()
                    mm(pa.t[0:L, 0:L], KT[hd].t[:, cs], QH[hd].t[:, cs], (KT[hd], QH[hd]), (pa,))
                    am = ATM[rot["atm"] % 4]
                    rot["atm"] += 1
                    pcol = PTK.t[0:L, c * 4 + hd:c * 4 + hd + 1]
                    stt(DVE, am.t[0:L, 0:L], pa.t[0:L, 0:L], pcol, maskS.t[0:L, 0:L], ALU.mult, ALU.mult,
                        (pa, PTK, maskS), (am,))
                    mm(NUM.t[:, cs], V.t[0:L, c, hd * 128:(hd + 1) * 128], am.t[0:L, 0:L], (V, am), (NUM,),
                       start=True, stop=False)
                    mm(NUM.t[:, cs], CBb[hd].t[:, 0:128], QH[hd].t[:, cs], (CBb[hd], QH[hd]), (NUM,), start=False, stop=True)
                    mm(DEN.t[:, cs], ones64.t[0:L, :], am.t[0:L, 0:L], (ones64, am), (DEN,), start=True, stop=False)
                    mm(DEN.t[:, cs], CBb[hd].t[:, 128:256], QH[hd].t[:, cs], (CBb[hd], QH[hd]), (DEN,), start=False, stop=True)
                    pt = ptr()
                    tr(pt.t[0:L, 0:128], KT[hd].t[:, cs], identb.t[:], (KT[hd], identb), (pt,))
                    kk = KTOK[rot["ktok"] % 4]
                    rot["ktok"] += 1
                    ts(DVE, kk.t[0:L, :], pt.t[0:L, 0:128], pcol, 128.0 ** -0.5, ALU.mult, ALU.mult, (pt, PTK), (kk,))
                    pu = psum()
                    mm(pu.t[:, 0:128], kk.t[0:L, :], V.t[0:L, c, hd * 128:(hd + 1) * 128], (kk, V), (pu,))
                    mm(pu.t[:, 128:256], kk.t[0:L, :], ones64.t[0:L, :], (kk, ones64), (pu,))
                    stt(DVE, C.t[:], C.t[:], RB.t[:, hd, c:c + 1], pu.t[:, 0:256], ALU.mult, ALU.add, (C, RB, pu), (C,))
                    if (not sample) and c < NCH - 1:
                        act(CBb[hd].t[:], C.t[:], AF.Copy, (C, RB), (CBb[hd],), scale=RB.t[:, hd, c + 1:c + 2])
                    fin = sample or (last and c == NCH - 1)
                    if fin:
                        idx = (c * 4 + hd) if sample else hd
                        cp(POOL, NT.t[:, idx:idx + 1], C.t[:, 128:129], (C,), (NT,))
                        dst = o_c_s[l, c, hd] if sample else o_c_p[l, hd]
                        P.dma(SP, dst, C.t[:, 0:128], C, load=False)
            for e in range(2):
                hd = hp * 2 + e
                NUM = OACC[2 * e]
                DEN = OACC[2 * e + 1]
                pth = psum()
                mm(pth.t[:, 0:n], SEL.t[0:4, hd * 128:(hd + 1) * 128], g5.t[0:4, 0:n], (SEL, g5), (pth,))
                t0, t1 = TMP[0], TMP[1]
                act(t0.t[:, 0:n], DEN.t[:, 0:n], AF.Abs, (DEN,), (t0,))
                tt(DVE, t0.t[:, 0:n], t0.t[:, 0:n], pth.t[:, 0:n], ALU.max, (t0, pth), (t0,))
                P.op(DVE, lambda: nc.vector.reciprocal(out=t0.t[:, 0:n], in_=t0.t[:, 0:n]), reads=(t0,), writes=(t0,))
                tt(DVE, t1.t[:, 0:n], NUM.t[:, 0:n], t0.t[:, 0:n], ALU.mult, (NUM, t0), (t1,))
                head_norm_scale(t1.t[:, 0:n], (t1,), n, R5.t[:, hd, 2 * depth + l:2 * depth + l + 1],
                                Y[hd].t[:, 0:n], Y[hd], gate=G[hd])
        if cfg.stage <= 7:
            return
        if sample or last:
            ncols = 4 * NCH if sample else 4
            pn = psum()
            tr(pn.t[0:ncols, 0:128], NT.t[:, 0:ncols], ident.t[:, :], (NT, ident), (pn,))
            cp(ACT, NROWS.t[0:ncols, :], pn.t[0:ncols, 0:128], (pn,), (NROWS,))
            P.dma(SP, (o_n_s[l] if sample else o_n_p[l]), NROWS.t[0:ncols, :], NROWS, load=False)
        merge_branch(l, 1, n)

        if cfg.stage <= 8:
            return
        proj_v(l, "c_v", n, L, NCH)
        for pr in range(2):
            s, wv = load_block(l, f"c_qk{pr}")
            for e in range(2):
                hd = 2 * pr + e
                pq = psum()
                pk = psum()
                pz = psum()
                for kc in range(KC):
                    mm(pq.t[0:64, 0:n], wv[:, kc, e * 64:(e + 1) * 64], H.t[:, kc, 0:n], (s, H), (pq,),
                       start=(kc == 0), stop=(kc == KC - 1))
                for kc in range(KC):
                    mm(pk.t[0:64, 0:n], wv[:, kc, 128 + e * 64:128 + (e + 1) * 64], H.t[:, kc, 0:n], (s, H), (pk,),
                       start=(kc == 0), stop=(kc == KC - 1))
                mm(pz.t[0:64, 0:n], W2b.t[0:16, l, hd * 64:(hd + 1) * 64], LR.t[0:16, 0:n], (W2b, LR), (pz,))
                t0, t2, t3 = TMP[0], TMP[2], TMP[3]
                act(t0.t[0:64, 0:n], pz.t[0:64, 0:n], AF.Exp, (pz, NBA), (t0,), scale=-1.0, bias=NBA.t[0:64, hd, l:l + 1])
                act(t0.t[0:64, 0:n], t0.t[0:64, 0:n], AF.Ln, (t0,), (t0,), bias=one_c.t[0:64, 0:1])
                scan(t2.t[0:64, 0:n], RM[0:64], t0.t[0:64, 0:n], 0.0, ALU.mult, ALU.add, (CMASK, t0), (t2,))
                act(t0.t[0:64, 0:n], t2.t[0:64, 0:n], AF.Exp, (t2,), (t0,), scale=-1.0 / 16.0)
                act(t3.t[0:64, 0:n], t2.t[0:64, 0:n], AF.Exp, (t2,), (t3,), scale=1.0 / 16.0)
                stt(DVE, QH[hd].t[0:64, 0:n], pq.t[0:64, 0:n], 0.125, t0.t[0:64, 0:n], ALU.mult, ALU.mult,
                    (pq, t0), (QH[hd],))
                tt(DVE, KT[hd].t[0:64, 0:n], pk.t[0:64, 0:n], t3.t[0:64, 0:n], ALU.mult, (pk, t3), (KT[hd],))
                cp(DVE, DL[hd].t[0:64, 0:NCH], chunked(t0.t[0:64, 0:n])[:, :, L - 1], (t0,), (DL[hd],))
        if cfg.stage <= 9:
            return
        s, wv = load_block(l, "c_g")
        for hd in range(4):
            pb = psum()
            for kc in range(KC):
                mm(pb.t[:, 0:n], wv[:, kc, hd * 128:(hd + 1) * 128], H.t[:, kc, 0:n], (s, H), (pb,),
                   start=(kc == 0), stop=(kc == KC - 1))
            act(G[hd].t[:, 0:n], pb.t[:, 0:n], AF.Silu, (pb,), (G[hd],))
        if cfg.stage <= 10:
            return
        for c in range(NCH):
            cs = slice(c * L, (c + 1) * L)
            st = sset(c)
            for hd in range(4):
                S = SC[st][hd]
                Sb = SAb[hd]
                if c == 0 or sample:
                    cp(ACT, Sb.t[0:64, :], S.t[:], (S,), (Sb,))
                pa = psum()
                mm(pa.t[0:L, 0:L], KT[hd].t[0:64, cs], QH[hd].t[0:64, cs], (KT[hd], QH[hd]), (pa,))
                am = ATM[rot["atm"] % 4]
                rot["atm"] += 1
                tt(DVE, am.t[0:L, 0:L], pa.t[0:L, 0:L], maskT.t[0:L, 0:L], ALU.mult, (pa, maskT), (am,))
                mm(OACC[hd].t[:, cs], V.t[0:L, c, hd * 128:(hd + 1) * 128], am.t[0:L, 0:L], (V, am), (OACC[hd],),
                   start=True, stop=False)
                mm(OACC[hd].t[:, cs], Sb.t[0:64, :], QH[hd].t[0:64, cs], (Sb, QH[hd]), (OACC[hd],), start=False, stop=True)
                pt = ptr()
                tr(pt.t[0:L, 0:64], KT[hd].t[0:64, cs], identb.t[0:64, 0:64], (KT[hd], identb), (pt,))
                kk = KTOK[rot["ktok"] % 4]
                rot["ktok"] += 1
                cp(ACT, kk.t[0:L, 0:64], pt.t[0:L, 0:64], (pt,), (kk,))
                pu = psum()
                mm(pu.t[0:64, 0:128], kk.t[0:L, 0:64], V.t[0:L, c, hd * 128:(hd + 1) * 128], (kk, V), (pu,))
                tt(DVE, S.t[:], S.t[:], pu.t[0:64, 0:128], ALU.add, (S, pu), (S,))
                if (not sample) and c < NCH - 1:
                    act(Sb.t[0:64, :], S.t[:], AF.Copy, (S, DL[hd]), (Sb,), scale=DL[hd].t[0:64, c:c + 1])
                ts(POOL, S.t[:], S.t[:], DL[hd].t[0:64, c:c + 1], None, ALU.mult, None, (S, DL[hd]), (S,))
                if sample:
                    P.dma(SP, o_gla_s[l, c, hd], S.t[:], S, load=False)
                elif last and c == NCH - 1:
                    P.dma(SP, o_gla_p[l, hd], S.t[:], S, load=False)
        if cfg.stage <= 11:
            return
        for hd in range(4):
            head_norm_scale(OACC[hd].t[:, 0:n], (OACC[hd],), n, R5.t[:, hd, 3 * depth + l:3 * depth + l + 1],
                            Y[hd].t[:, 0:n], Y[hd], gate=G[hd])
        merge_branch(l, 2, n)

        if cfg.stage <= 12:
            return
        for kc in range(KC):
            cp(ACT if kc % 2 else DVE, H.t[:, kc, 0:n], MERGEDv[:, kc, 0:n], (ARENA,), (H,))
        s, wv = load_block(l, "wout")
        for oc in range(KC):
            pb = psum()
            for kc in range(KC):
                mm(pb.t[:, 0:n], wv[:, kc, oc * 128:(oc + 1) * 128], H.t[:, kc, 0:n], (s, H), (pb,),
                   start=(kc == 0), stop=(kc == KC - 1))
            tt(DVE, X.t[:, oc, 0:n], X.t[:, oc, 0:n], pb.t[:, 0:n], ALU.add, (X, pb), (X,))

    def load_sample_states(l):
        for sq in range(NSEQ):
            for hd in range(4):
                P.dma(SP, SA[sq][hd].t[:], st_hgrn[l, sq, hd], SA[sq][hd], load=True)
                P.dma(SP, CB[sq][hd].t[:, 0:128], st_c[l, sq, hd], CB[sq][hd], load=True)
            for hd in range(4):
                P.dma(SP, SC[sq][hd].t[:], st_gla[l, sq, hd], SC[sq][hd], load=True)

    def zero_states():
        for st in range(NSET):
            for hd in range(4):
                ms(POOL, SA[st][hd], SA[st][hd].t[:], 0.0)
                ms(POOL, CB[st][hd], CB[st][hd].t[:], 0.0)
            for hd in range(4):
                ms(POOL, SC[st][hd], SC[st][hd].t[:], 0.0)

    do_mixer = cfg.mixer
    if NS > 0:
        load_x(xs[0:NS, :], NS)
        for l in range(depth):
            ffn(l, 1, NS)
            if do_mixer:
                load_sample_states(l)
                mixer(l, NS, SL, NSEQ, True, False)
            ffn(l, 2, NS)
        store_y(ys[0:NS, :], NS)
    if NP > 0:
        zero_states()
        ntile = NP // T
        for ti in range(ntile):
            load_x(xp[ti * T:(ti + 1) * T, :], T)
            for l in range(depth):
                ffn(l, 1, T)
                if do_mixer:
                    mixer(l, T, 64, T // 64, False, ti == ntile - 1)
                ffn(l, 2, T)
            store_y(yp[ti * T:(ti + 1) * T, :], T)

    for b in XIO + [MO, NROWS] + [x for st in SA + CB + SC for x in st]:
        if b.dcnt:
            P._wait(SP, (b.dsem, b.dcnt))
    for e in (PE, ACT, DVE, POOL):
        if e.cnt:
            P._wait(SP, (e.sem, e.cnt))
    print(f"[build] inst={P.n_inst} waits={P.n_wait} sems={P.nsem} sbuf_left={nc.sbuf_bytes_remaining}")


_NC_CACHE = {}


def kernel(**inputs):
    inp = {k: np.asarray(v) for k, v in inputs.items()}
    B, SEQ, _ = inp["x_prompt"].shape
    NSAMP, SL, _ = inp["x_sample"].shape
    n_cores = 8
    nseq = NSAMP // n_cores
    cfg = Cfg(SEQ, nseq, SL, depth=DEPTH)
    key = (SEQ, nseq, SL)
    if key not in _NC_CACHE:
        _NC_CACHE[key] = build_program(cfg)
    nc = _NC_CACHE[key]
    in_maps = [core_inputs(inp, cfg, c % B, c, list(range(c * nseq, (c + 1) * nseq))) for c in range(n_cores)]
    res = run_bass_kernel_spmd(nc, in_maps, core_ids=list(range(n_cores)))
    R = res.results
    f32 = np.float32
    y_prompt = np.stack([R[b]["y_prompt"] for b in range(B)]).astype(f32)
    y_sample = np.concatenate([R[c]["y_sample"].reshape(nseq, SL, D_MODEL) for c in range(n_cores)], 0).astype(f32)

    def pstate(name, shape):
        return np.stack([R[b][name].reshape(shape) for b in range(B)], 1).astype(f32)

    def sstate(name, shape):
        return np.concatenate([R[c][name].reshape((DEPTH, nseq) + shape) for c in range(n_cores)], 1).astype(f32)

    return (
        y_prompt, y_sample,
        pstate("hgrn_p", (DEPTH, 4, 128, 128)), pstate("c_p", (DEPTH, 4, 128, 128)), pstate("n_p", (DEPTH, 4, 128)),
        pstate("m_p", (DEPTH, 4)), pstate("gla_p", (DEPTH, 4, 64, 128)),
        sstate("hgrn_s", (4, 128, 128)), sstate("c_s", (4, 128, 128)), sstate("n_s", (4, 128)),
        sstate("m_s", (4,)), sstate("gla_s", (4, 64, 128)),
    )


def core_inputs(inp, cfg, b, core, seqs):
    depth = cfg.depth
    f = np.ascontiguousarray
    NP, NSEQ, SL = cfg.n_prompt, cfg.n_seq, cfg.seq_len
    nsq = max(NSEQ, 1)
    sq = list(seqs) if NSEQ > 0 else [0]
    m = {
        "x_prompt": f(inp["x_prompt"][b, :max(NP, 1)]),
        "x_sample": f(inp["x_sample"][sq].reshape(nsq * SL, D_MODEL)[:max(NSEQ * SL, 1)]),
        "st_hgrn": f(inp["state_hgrn"][:depth, sq]),
        "st_c": f(inp["state_mlstm_c"][:depth, sq]),
        "st_n": f(inp["state_mlstm_n"][:depth, sq].reshape(depth, nsq * 4, 128)),
        "st_m": f(inp["state_mlstm_m"][:depth, sq]),
        "st_gla": f(inp["state_gla"][:depth, sq]),
        "norms": f(np.concatenate([np.stack([inp["ffn1_norm"][l], inp["mix_norm"][l], inp["ffn2_norm"][l]])
                                   for l in range(depth)] + [inp["final_norm"][None]], 0)),
        "rows512": f(np.concatenate([inp["hgrn_lb_logits"][:depth], inp["hgrn_norm"][:depth],
                                     inp["mlstm_norm"][:depth], inp["gla_norm"][:depth]], 0)),
        "gla_b_a": f(inp["gla_b_a"][:depth]),
        "mlstm_gate_bias": f(inp["mlstm_gate_bias"][:depth]),
        "gla_w_a2": f(inp["gla_w_a2"][:depth]),
    }
    for k in ("ffn1_w_up", "ffn1_w_down", "ffn2_w_up", "ffn2_w_down", "w_in", "w_branch", "w_out"):
        m[k] = f(inp[k][:depth])
    m.update(host_consts())
    return m
```

```python
import math
from contextlib import ExitStack

import numpy as np
import concourse.bass as bass
import concourse.mybir as mybir
from concourse.bass_utils import run_bass_kernel_spmd

F32 = mybir.dt.float32
BF16 = mybir.dt.bfloat16
AF = mybir.ActivationFunctionType
ALU = mybir.AluOpType
AX = mybir.AxisListType

D_MODEL = 1024
DEPTH = 4
D_FF = 2816
D_IN = 8728
EPS = 1e-6
KC = 8
FC = 22
NEG_BIG = -1.0e30


PHASES = []


class Buf:
    def __init__(self, name, t, dsem=None):
        self.name = name
        self.t = t
        self.w = None
        self.r = {}
        self.dsem = dsem
        self.dcnt = 0

    def __getitem__(self, k):
        return self.t[k]


class Eng:
    def __init__(self, name, h, sem):
        self.name = name
        self.h = h
        self.sem = sem
        self.cnt = 0
        self.seen = {}


class Prog:
    def __init__(self, nc, es):
        self.nc = nc
        self.es = es
        self.nsem = 0
        self.PE = self._eng("pe", nc.tensor)
        self.ACT = self._eng("act", nc.scalar)
        self.DVE = self._eng("dve", nc.vector)
        self.POOL = self._eng("pool", nc.gpsimd)
        self.SP = self._eng("sp", nc.sync)
        self.engs = [self.PE, self.ACT, self.DVE, self.POOL, self.SP]
        self.n_inst = 0
        self.n_wait = 0

    def sem(self, name):
        self.nsem += 1
        return self.es.enter_context(self.nc.semaphore(name))

    def _eng(self, name, h):
        return Eng(name, h, self.sem("e_" + name))

    def sb(self, name, shape, dtype, dma=False):
        t = self.es.enter_context(self.nc.sbuf_tensor(name, list(shape), dtype))
        return Buf(name, t, self.sem("d_" + name) if dma else None)

    def ps(self, name, shape, dtype=F32):
        t = self.es.enter_context(self.nc.psum_tensor(name, list(shape), dtype))
        return Buf(name, t)

    def _need(self, e, tok, acc):
        sem, val = tok
        if e is self.PE and sem is self.PE.sem:
            return
        if e.seen.get(sem, 0) >= val:
            return
        e.seen[sem] = val
        acc[sem] = max(acc.get(sem, 0), val)

    def _wait(self, e, tok):
        acc = {}
        self._need(e, tok, acc)
        for sem, val in acc.items():
            e.h.wait_ge(sem, val)
            self.n_wait += 1

    def _deps(self, e, reads, writes):
        acc = {}
        for b in reads:
            if b.w is not None:
                self._need(e, b.w, acc)
        for b in writes:
            if b.w is not None:
                self._need(e, b.w, acc)
            for sem, val in b.r.items():
                self._need(e, (sem, val), acc)
        return list(acc.items())

    def _emit(self, e, fn, waits):
        for sem, val in waits[:-1]:
            e.h.wait_ge(sem, val)
            self.n_wait += 1
        inst = fn()
        if waits:
            sem, val = waits[-1]
            inst._wait_ge(sem, val)
        return inst

    def op(self, e, fn, reads=(), writes=()):
        waits = self._deps(e, reads, writes)
        inst = self._emit(e, fn, waits)
        e.cnt += 1
        inst.then_inc(e.sem, 1)
        tok = (e.sem, e.cnt)
        for b in writes:
            b.w = tok
            b.r = {}
        for b in reads:
            if b not in writes:
                b.r[e.sem] = e.cnt
        self.n_inst += 1
        return inst

    def dma(self, q, out, in_, buf, load, **kw):
        if load:
            waits = self._deps(q, (), (buf,))
        else:
            waits = self._deps(q, (buf,), ())
        inst = self._emit(q, lambda: q.h.dma_start(out=out, in_=in_, **kw), waits)
        inst.then_inc(buf.dsem, 16)
        buf.dcnt += 16
        tok = (buf.dsem, buf.dcnt)
        if load:
            buf.w = tok
            buf.r = {}
        else:
            buf.r[buf.dsem] = buf.dcnt
        self.n_inst += 1
        return inst


class Cfg:
    def __init__(self, n_prompt, n_seq, seq_len, depth=DEPTH, T=512, mixer=True, stage=99):
        self.mixer = mixer
        self.stage = stage
        self.n_prompt = n_prompt
        self.n_seq = n_seq
        self.seq_len = seq_len
        self.depth = depth
        self.T = T


def weight_blocks():
    blks = []
    for f in (1, 2):
        for i in range(5):
            blks.append((f"up{f}_{i}", f"ffn{f}_w_up", 1024, [(512 * i, 512), (2816 + 512 * i, 512)]))
        blks.append((f"up{f}_5", f"ffn{f}_w_up", 1024, [(2560, 256), (5376, 256)]))
        for i in range(4):
            blks.append((f"dn{f}_{i}", f"ffn{f}_w_down", 2816, [(256 * i, 256)]))
    for hd in range(4):
        blks.append((f"a_qfg{hd}", "w_in", 1024,
                     [(hd * 128, 128), (512 + hd * 128, 128), (1536 + hd * 128, 128)]))
    blks.append(("a_v", "w_in", 1024, [(1024, 512)]))
    for hd in range(4):
        blks.append((f"b_qko{hd}", "w_in", 1024,
                     [(2048 + hd * 128, 128), (2560 + hd * 128, 128), (3584 + hd * 128, 128)]))
    blks.append(("b_v", "w_in", 1024, [(3072, 512)]))
    blks.append(("small", "w_in", 1024, [(4096, 8), (5640, 16)]))
    for pr in range(2):
        blks.append((f"c_qk{pr}", "w_in", 1024, [(4104 + pr * 128, 128), (4360 + pr * 128, 128)]))
    blks.append(("c_v", "w_in", 1024, [(4616, 512)]))
    blks.append(("c_g", "w_in", 1024, [(5128, 512)]))
    for a in range(3):
        blks.append((f"mg{a}", "w_in", 1024, [(5656 + a * 1024, 1024)]))
    for a in range(3):
        blks.append((f"br{a}", "w_branch", 512, [(0, 1024)]))
    blks.append(("wout", "w_out", 1024, [(0, 1024)]))
    return blks


def build_program(cfg):
    nc = bass.Bass("TRN2", target_bir_lowering=False)
    es = ExitStack()
    with es:
        _build(nc, es, cfg)
    return nc


def host_consts():
    ident = np.eye(128, dtype=np.float32)
    maskT = np.triu(np.ones((64, 64), dtype=np.float32))
    cm = np.ones((128, 512 + 64 + 64), dtype=np.float32)
    cm[:, 0:512:64] = 0.0
    cm[:, 512:576:16] = 0.0
    cm[:, 576:640] = 0.0
    cm[:, 576:640:16] = NEG_BIG
    sel = np.zeros((4, 512), dtype=np.float32)
    for h in range(4):
        sel[h, h * 128:(h + 1) * 128] = 1.0
    return {"ident_in": ident, "maskT_in": maskT, "cmask_in": cm, "sel_in": sel}


def _build(nc, es, cfg):
    P = Prog(nc, es)
    T = cfg.T
    depth = cfg.depth
    NP = cfg.n_prompt
    NSEQ = cfg.n_seq
    SL = cfg.seq_len
    NS = NSEQ * SL
    NSET = max(depth, NSEQ, 1)
    PE, ACT, DVE, POOL, SP = P.PE, P.ACT, P.DVE, P.POOL, P.SP

    def din(name, shape):
        return nc.dram_tensor(name, list(shape), F32, kind="ExternalInput").ap()

    def dout(name, shape):
        return nc.dram_tensor(name, list(shape), F32, kind="ExternalOutput").ap()

    xp = din("x_prompt", [max(NP, 1), D_MODEL])
    yp = dout("y_prompt", [max(NP, 1), D_MODEL])
    xs = din("x_sample", [max(NS, 1), D_MODEL])
    ys = dout("y_sample", [max(NS, 1), D_MODEL])
    nsq = max(NSEQ, 1)
    st_hgrn = din("st_hgrn", [depth, nsq, 4, 128, 128])
    st_c = din("st_c", [depth, nsq, 4, 128, 128])
    st_n = din("st_n", [depth, nsq * 4, 128])
    st_m = din("st_m", [depth, nsq, 4])
    st_gla = din("st_gla", [depth, nsq, 4, 64, 128])
    o_hgrn_p = dout("hgrn_p", [depth, 4, 128, 128])
    o_c_p = dout("c_p", [depth, 4, 128, 128])
    o_n_p = dout("n_p", [depth, 4, 128])
    o_m_p = dout("m_p", [depth, 4])
    o_gla_p = dout("gla_p", [depth, 4, 64, 128])
    o_hgrn_s = dout("hgrn_s", [depth, nsq, 4, 128, 128])
    o_c_s = dout("c_s", [depth, nsq, 4, 128, 128])
    o_n_s = dout("n_s", [depth, nsq * 4, 128])
    o_m_s = dout("m_s", [depth, nsq, 4])
    o_gla_s = dout("gla_s", [depth, nsq, 4, 64, 128])
    w_dram = {
        "ffn1_w_up": din("ffn1_w_up", [depth, 1024, 2 * D_FF]),
        "ffn1_w_down": din("ffn1_w_down", [depth, D_FF, 1024]),
        "ffn2_w_up": din("ffn2_w_up", [depth, 1024, 2 * D_FF]),
        "ffn2_w_down": din("ffn2_w_down", [depth, D_FF, 1024]),
        "w_in": din("w_in", [depth, 1024, D_IN]),
        "w_branch": din("w_branch", [depth, 3, 512, 1024]),
        "w_out": din("w_out", [depth, 1024, 1024]),
    }
    norms = din("norms", [3 * depth + 1, 1024])
    rows512 = din("rows512", [4 * depth, 512])
    gla_b_a = din("gla_b_a", [depth, 256])
    gate_bias = din("mlstm_gate_bias", [depth, 8])
    gla_w2 = din("gla_w_a2", [depth, 16, 256])
    identd = din("ident_in", [128, 128])
    maskd = din("maskT_in", [64, 64])
    cmaskd = din("cmask_in", [128, 640])
    seld = din("sel_in", [4, 512])

    blks = weight_blocks()
    scratch = {}
    for l in range(depth):
        for key, src, K, cols in blks:
            kc = K // 128
            W = sum(n for _, n in cols)
            scratch[(l, key)] = (nc.dram_tensor(f"ws_{l}_{key}", [128, kc * W], BF16, kind="Internal").ap(), kc, W)

    cast_sem = P.sem("cast")
    n_cast = 0
    for l in range(depth):
        for key, src, K, cols in blks:
            sap, kc, W = scratch[(l, key)]
            sview = sap.rearrange("p (k w) -> p k w", k=kc)
            off = 0
            for c0, n in cols:
                if src == "w_branch":
                    a = int(key[2])
                    srcap = w_dram[src][l, a, :, c0:c0 + n]
                else:
                    srcap = w_dram[src][l, :, c0:c0 + n]
                srcap = srcap.rearrange("(k p) n -> p k n", p=128)
                POOL.h.dma_start(out=sview[:, :, off:off + n], in_=srcap).then_inc(cast_sem, 16)
                n_cast += 1
                off += n
    cast_tok = (cast_sem, 16 * n_cast)

    NSLOT = 3
    SLOTW = 8192
    slots = [P.sb(f"wslot{i}", [128, SLOTW], BF16, dma=True) for i in range(NSLOT)]
    slot_i = [0]

    def load_block(l, key):
        sap, kc, W = scratch[(l, key)]
        s = slots[slot_i[0] % NSLOT]
        slot_i[0] += 1
        P._wait(SP, cast_tok)
        P.dma(SP, s.t[:, 0:kc * W], sap[:, :], s, load=True)
        return s, s.t[:, 0:kc * W].rearrange("p (k w) -> p k w", k=kc)

    X = P.sb("X", [128, KC, T], F32)
    H = P.sb("H", [128, KC, T], BF16)
    ARENA = P.sb("ARENA", [128, FC * T // 2], F32)
    HIDv = ARENA.t[:, :].bitcast(BF16).rearrange("p (c t) -> p c t", c=FC)
    MERGEDv = ARENA.t[:, 0:KC * T].rearrange("p (c t) -> p c t", c=KC)
    SQ = P.sb("SQ", [128, KC, T], BF16)
    RSTD = P.sb("RSTD", [128, T], F32)
    SG = [P.sb(f"SG{i}", [128, T], F32) for i in range(2)]
    XIO = [P.sb(f"XIO{i}", [128, D_MODEL], F32, dma=True) for i in range(2)]
    csem = P.sem("consts")
    cbufs = []

    def cload(name, shape, dtype, src, q=None, **kw):
        b = P.sb(name, shape, dtype)
        b.dsem = csem
        cbufs.append(b)
        (q or SP).h.dma_start(out=b.t[:], in_=src, **kw).then_inc(csem, 16)
        return b

    ident = cload("ident", [128, 128], F32, identd[:, :])
    NROW = XIO[0]
    R5ROW = XIO[1]
    P.dma(POOL, XIO[0].t[0:3 * depth + 1, :], norms[:, :], XIO[0], load=True)
    P.dma(POOL, XIO[1].t[0:4 * depth, 0:512], rows512[:, :], XIO[1], load=True)
    P.dma(POOL, XIO[1].t[0:depth, 512:768], gla_b_a[:, :], XIO[1], load=True)
    maskT = cload("maskT", [64, 64], F32, maskd[:, :])
    CMASK = cload("CMASK", [128, 640], F32, cmaskd[:, :])
    SEL = cload("SEL", [4, 512], F32, seld[:, :])
    W2b = P.sb("W2b", [16, depth, 256], BF16, dma=True)
    P.dma(POOL, W2b.t[:], gla_w2.rearrange("l r c -> r l c"), W2b, load=True)
    GB = cload("GB", [4, depth, 2], F32, gate_bias.rearrange("l (g h) -> h l g", g=2), allow_slow_non_contiguous=True)
    for b in cbufs:
        b.w = (csem, 16 * len(cbufs))

    eps_c = P.sb("eps_c", [128, 1], F32)
    one_c = P.sb("one_c", [128, 1], F32)
    ones_m = P.sb("ones_m", [128, 128], BF16)
    ones_h = P.sb("ones_h", [128, 128], BF16)
    ones64 = P.sb("ones64", [64, 128], BF16)
    identb = P.sb("identb", [128, 128], BF16)
    maskS = P.sb("maskS", [64, 64], F32)
    NW = P.sb("NW", [128, KC, 16], F32)
    R5 = P.sb("R5", [128, 4, 16], F32)
    LB = P.sb("LB", [128, 4, 4], F32)
    OML = P.sb("OML", [128, 4, 4], F32)
    NBA = P.sb("NBA", [64, 4, 4], F32)
    NGBF = P.sb("NGBF", [4, 4], F32)
    ones4 = P.sb("ones4", [4, 512], F32)
    zeros4 = P.sb("zeros4", [4, 512], F32)

    QH = [P.sb(f"QH{i}", [128, T], BF16) for i in range(4)]
    KT = [P.sb(f"KT{i}", [128, T], BF16) for i in range(4)]
    G = [P.sb(f"G{i}", [128, T], BF16) for i in range(4)]
    Y = [P.sb(f"Y{i}", [128, T], BF16) for i in range(4)]
    V = P.sb("V", [64, 8, 512], BF16)
    TMP = [P.sb(f"TMP{i}", [128, T], F32) for i in range(6)]
    DL = [P.sb(f"DL{i}", [128, 8], F32) for i in range(4)]
    ATM = [P.sb(f"ATM{i}", [64, 64], BF16) for i in range(8)]
    KTOK = [P.sb(f"KTOK{i}", [64, 128], BF16) for i in range(8)]
    SQH = P.sb("SQH", [128, T], BF16)
    LR = P.sb("LR", [16, T], BF16)
    PTK = P.sb("PTK", [64, 32], F32)
    RB = P.sb("RB", [128, 4, 8], F32)
    R4 = P.sb("R4", [4, 8], F32)
    MP = P.sb("MP", [4, 8], F32)
    MO = P.sb("MO", [4, 8], F32, dma=True)
    M0 = P.sb("M0", [4, 8], F32, dma=True)
    NT = P.sb("NT", [128, 16], F32)
    NROWS = P.sb("NROWS", [16, 128], F32, dma=True)
    FCY = P.sb("FCY", [4, NSET], F32)
    MCY = P.sb("MCY", [4, NSET], F32)
    SAb = [P.sb(f"SAb{i}", [128, 128], BF16) for i in range(4)]
    CBb = [P.sb(f"CBb{i}", [128, 256], BF16) for i in range(4)]
    SA = [[P.sb(f"SA{s}_{i}", [128, 128], F32, dma=True) for i in range(4)] for s in range(NSET)]
    CB = [[P.sb(f"CB{s}_{i}", [128, 256], F32, dma=True) for i in range(4)] for s in range(NSET)]
    SC = [[P.sb(f"SC{s}_{i}", [64, 128], F32, dma=True) for i in range(4)] for s in range(NSET)]

    PSB = [P.ps(f"psb{i}", [128, 512], F32) for i in range(3)]
    OACC = [P.ps(f"oacc{i}", [128, 512], F32) for i in range(4)]
    PTB = es.enter_context(nc.psum_tensor("ptb", [128, 1024], BF16))
    ps_i = [0]
    pt_i = [0]
    class View:
        def __init__(self, buf, t):
            self.buf = buf
            self.t = t

    PAR = [View(PSB[0], PSB[0].t[:, i * 64:(i + 1) * 64]) for i in range(8)]
    PUR = [View(PSB[1 + i // 2], PSB[1 + i // 2].t[:, (i % 2) * 256:(i % 2 + 1) * 256]) for i in range(4)]
    PTBb = Buf("ptbb", PTB)
    PTR = [View(PTBb, PTB[:, i * 128:(i + 1) * 128]) for i in range(8)]

    def to_regions():
        pass

    def to_banks():
        pass

    def psum(all7=False):
        pool = PSB + OACC if all7 else PSB
        b = pool[ps_i[0] % len(pool)]
        ps_i[0] += 1
        return b

    def ptr():
        b = PTR[pt_i[0] % 8]
        pt_i[0] += 1
        return b

    rot = {"atm": 0, "ktok": 0}

    def act(out, in_, func, reads, writes, **kw):
        return P.op(ACT, lambda: nc.scalar.activation(out=out, in_=in_, func=func, **kw), reads=reads, writes=writes)

    def mm(out, lhsT, rhs, reads, writes, start=True, stop=True):
        return P.op(PE, lambda: nc.tensor.matmul(out, lhsT, rhs, start=start, stop=stop), reads=reads, writes=writes)

    def tr(out, in_, idn, reads, writes):
        return P.op(PE, lambda: nc.tensor.transpose(out, in_, idn), reads=reads, writes=writes)

    def tt(e, out, in0, in1, op, reads, writes):
        return P.op(e, lambda: e.h.tensor_tensor(out=out, in0=in0, in1=in1, op=op), reads=reads, writes=writes)

    def ts(e, out, in0, s1, s2, op0, op1, reads, writes):
        if s2 is None:
            return P.op(e, lambda: e.h.tensor_scalar(out=out, in0=in0, scalar1=s1, scalar2=None, op0=op0),
                        reads=reads, writes=writes)
        return P.op(e, lambda: e.h.tensor_scalar(out=out, in0=in0, scalar1=s1, scalar2=s2, op0=op0, op1=op1),
                    reads=reads, writes=writes)

    def stt(e, out, in0, scalar, in1, op0, op1, reads, writes):
        return P.op(e, lambda: e.h.scalar_tensor_tensor(out=out, in0=in0, scalar=scalar, in1=in1, op0=op0, op1=op1),
                    reads=reads, writes=writes)

    def cp(e, out, in_, reads, writes):
        if e is ACT:
            return act(out, in_, AF.Copy, reads, writes)
        return P.op(e, lambda: e.h.tensor_copy(out=out, in_=in_), reads=reads, writes=writes)

    def scan(out, d0, d1, init, op0, op1, reads, writes):
        return P.op(DVE, lambda: nc.vector.tensor_tensor_scan(out=out, data0=d0, data1=d1, initial=init, op0=op0, op1=op1),
                    reads=reads, writes=writes)

    def ms(e, buf, ap, val):
        return P.op(e, lambda: e.h.memset(ap, val), writes=(buf,))

    ms(DVE, ones_m, ones_m.t[:], 1.0 / 1024.0)
    ms(DVE, ones_h, ones_h.t[:], 1.0 / 128.0)
    ms(DVE, ones64, ones64.t[:], 1.0)
    ms(DVE, eps_c, eps_c.t[:], EPS)
    ms(DVE, one_c, one_c.t[:], 1.0)
    ms(DVE, ones4, ones4.t[:], 1.0)
    ms(DVE, zeros4, zeros4.t[:], 0.0)
    ms(DVE, FCY, FCY.t[:], 0.0)
    ms(DVE, MCY, MCY.t[:], 0.0)
    cp(DVE, identb.t[:], ident.t[:], (ident,), (identb,))
    ts(DVE, maskS.t[:], maskT.t[:], 128.0 ** -0.5, None, ALU.mult, None, (maskT,), (maskS,))
    nrows = 3 * depth + 1
    for kc in range(KC):
        pb = psum()
        tr(pb.t[:, 0:nrows], NROW.t[0:nrows, kc * 128:(kc + 1) * 128], ident.t[0:nrows, 0:nrows], (NROW, ident), (pb,))
        cp(DVE, NW.t[:, kc, 0:nrows], pb.t[:, 0:nrows], (pb,), (NW,))
    for hd in range(4):
        pb = psum()
        tr(pb.t[:, 0:4 * depth], R5ROW.t[0:4 * depth, hd * 128:(hd + 1) * 128], ident.t[0:4 * depth, 0:4 * depth],
           (R5ROW, ident), (pb,))
        cp(DVE, R5.t[:, hd, 0:4 * depth], pb.t[:, 0:4 * depth], (pb,), (R5,))
    for hd in range(4):
        pb = psum()
        tr(pb.t[0:64, 0:depth], R5ROW.t[0:depth, 512 + hd * 64:512 + (hd + 1) * 64], ident.t[0:depth, 0:depth],
           (R5ROW, ident), (pb,))
        ts(DVE, NBA.t[0:64, hd, 0:depth], pb.t[0:64, 0:depth], -1.0, None, ALU.mult, None, (pb,), (NBA,))
    ts(DVE, NGBF.t[:, 0:depth], GB.t[:, :, 1], -1.0, None, ALU.mult, None, (GB,), (NGBF,))
    EX = TMP[0]
    act(EX.t[:, 0:4 * depth].rearrange("p (h l) -> p h l", h=4), R5.t[:, :, 0:depth], AF.Exp, (R5,), (EX,))
    exv = EX.t[:, 0:4 * depth].rearrange("p (h l) -> p h l", h=4)
    P.op(DVE, lambda: nc.vector.reduce_sum(out=EX.t[:, 16:20], in_=exv, axis=AX.X), reads=(EX,), writes=(EX,))
    P.op(DVE, lambda: nc.vector.reciprocal(out=EX.t[:, 16:20], in_=EX.t[:, 16:20]), reads=(EX,), writes=(EX,))
    tt(DVE, exv, exv, EX.t[:, 16:20].unsqueeze(2).broadcast_to([128, 4, depth]), ALU.mult, (EX,), (EX,))
    ms(DVE, LB, LB.t[:], 0.0)
    for l in range(1, depth):
        tt(DVE, LB.t[:, :, l], LB.t[:, :, l - 1], exv[:, :, l], ALU.add, (LB, EX), (LB,))
    ts(DVE, OML.t[:], LB.t[:], -1.0, 1.0, ALU.mult, ALU.add, (LB,), (OML,))

    def rstd_from(pb, n, dst=None):
        dst = dst or RSTD
        act(dst.t[:, 0:n], pb.t[:, 0:n], AF.Ln, (pb, eps_c), (dst,), bias=eps_c.t[:, 0:1])
        act(dst.t[:, 0:n], dst.t[:, 0:n], AF.Exp, (dst,), (dst,), scale=-0.5)

    def x_stats(n):
        for kc in range(KC):
            act(SQ.t[:, kc, 0:n], X.t[:, kc, 0:n], AF.Square, (X,), (SQ,))
        pb = psum()
        for kc in range(KC):
            mm(pb.t[:, 0:n], ones_m.t[:], SQ.t[:, kc, 0:n], (ones_m, SQ), (pb,), start=(kc == 0), stop=(kc == KC - 1))
        rstd_from(pb, n)

    def rmsnorm(nidx, n):
        x_stats(n)
        for kc in range(KC):
            stt(DVE, H.t[:, kc, 0:n], X.t[:, kc, 0:n], NW.t[:, kc, nidx:nidx + 1], RSTD.t[:, 0:n], ALU.mult, ALU.mult,
                (X, NW, RSTD), (H,))

    def ffn(l, f, n):
        PHASES.append(('ffn_norm', P.PE.cnt))
        rmsnorm(3 * l + (0 if f == 1 else 2), n)
        PHASES.append(('ffn_up', P.PE.cnt))
        for i in range(6):
            s, wv = load_block(l, f"up{f}_{i}")
            nch = 4 if i < 5 else 2
            half = nch * 128
            for j in range(nch):
                pg = psum(True)
                pu = psum(True)
                for kc in range(KC):
                    mm(pg.t[:, 0:n], wv[:, kc, j * 128:(j + 1) * 128], H.t[:, kc, 0:n], (s, H), (pg,),
                       start=(kc == 0), stop=(kc == KC - 1))
                for kc in range(KC):
                    mm(pu.t[:, 0:n], wv[:, kc, half + j * 128:half + (j + 1) * 128], H.t[:, kc, 0:n], (s, H), (pu,),
                       start=(kc == 0), stop=(kc == KC - 1))
                sg = SG[(i * 4 + j) % 2]
                act(sg.t[:, 0:n], pg.t[:, 0:n], AF.Silu, (pg,), (sg,))
                ch = i * 4 + j
                tt(DVE, HIDv[:, ch, 0:n], sg.t[:, 0:n], pu.t[:, 0:n], ALU.mult, (sg, pu), (ARENA,))
        PHASES.append(('ffn_down', P.PE.cnt))
        for i in range(4):
            s, wv = load_block(l, f"dn{f}_{i}")
            for j in range(2):
                oc = i * 2 + j
                pb = psum(True)
                for kc in range(FC):
                    mm(pb.t[:, 0:n], wv[:, kc, j * 128:(j + 1) * 128], HIDv[:, kc, 0:n], (s, ARENA), (pb,),
                       start=(kc == 0), stop=(kc == FC - 1))
                stt(DVE, X.t[:, oc, 0:n], pb.t[:, 0:n], 0.5, X.t[:, oc, 0:n], ALU.mult, ALU.add, (pb, X), (X,))

    kio = [0]

    def load_x(src_ap, n):
        for tb in range((n + 127) // 128):
            r = min(128, n - tb * 128)
            xin = XIO[kio[0] % 2]
            kio[0] += 1
            P.dma(POOL, xin.t[0:r, :], src_ap[tb * 128:tb * 128 + r, :], xin, load=True)
            for g in range(2):
                pb = psum()
                for q in range(4):
                    kc = g * 4 + q
                    tr(pb.t[:, q * 128:q * 128 + r], xin.t[0:r, kc * 128:(kc + 1) * 128], ident.t[0:r, 0:r],
                       (xin, ident), (pb,))
                act(X.t[:, g * 4:(g + 1) * 4, tb * 128:tb * 128 + r],
                    pb.t[:, :].rearrange("p (q t) -> p q t", q=4)[:, :, 0:r], AF.Copy, (pb,), (X,))

    def store_y(dst_ap, n):
        x_stats(n)
        fi = 3 * depth
        for kc in range(KC):
            stt(DVE, X.t[:, kc, 0:n], X.t[:, kc, 0:n], NW.t[:, kc, fi:fi + 1], RSTD.t[:, 0:n], ALU.mult, ALU.mult,
                (X, NW, RSTD), (X,))
        for tb in range((n + 127) // 128):
            r = min(128, n - tb * 128)
            yo = XIO[kio[0] % 2]
            kio[0] += 1
            for g in range(2):
                pb = psum()
                for q in range(4):
                    kc = g * 4 + q
                    tr(pb.t[0:r, q * 128:(q + 1) * 128], X.t[:, kc, tb * 128:tb * 128 + r], ident.t[:, :], (X, ident), (pb,))
                act(yo.t[0:r, g * 512:(g + 1) * 512], pb.t[0:r, :], AF.Copy, (pb,), (yo,))
            P.dma(POOL, dst_ap[tb * 128:tb * 128 + r, :], yo.t[0:r, :], yo, load=False)

    def head_norm_scale(src_ap, src_bufs, n, wcol, out_ap, out_buf, gate=None):
        act(SQH.t[:, 0:n], src_ap, AF.Square, src_bufs, (SQH,))
        pm = psum()
        mm(pm.t[:, 0:n], ones_h.t[:], SQH.t[:, 0:n], (ones_h, SQH), (pm,))
        rstd_from(pm, n)
        if gate is None:
            stt(DVE, out_ap, src_ap, wcol, RSTD.t[:, 0:n], ALU.mult, ALU.mult, src_bufs + (R5, RSTD), (out_buf,))
        else:
            t = TMP[2]
            stt(DVE, t.t[:, 0:n], src_ap, wcol, RSTD.t[:, 0:n], ALU.mult, ALU.mult, src_bufs + (R5, RSTD), (t,))
            tt(DVE, out_ap, t.t[:, 0:n], gate.t[:, 0:n], ALU.mult, (t, gate), (out_buf,))

    def proj_v(l, key, n, L, NCH):
        s, wv = load_block(l, key)
        for c in range(NCH):
            pb = psum(True)
            for kc in range(KC):
                mm(pb.t[0:L, 0:512], H.t[:, kc, c * L:(c + 1) * L], wv[:, kc, 0:512], (s, H), (pb,),
                   start=(kc == 0), stop=(kc == KC - 1))
            cp(ACT, V.t[0:L, c, :], pb.t[0:L, 0:512], (pb,), (V,))

    def proj3(s, wv, n, ncol):
        outs = []
        for j in range(ncol):
            pb = psum(True)
            for kc in range(KC):
                mm(pb.t[:, 0:n], wv[:, kc, j * 128:(j + 1) * 128], H.t[:, kc, 0:n], (s, H), (pb,),
                   start=(kc == 0), stop=(kc == KC - 1))
            outs.append(pb)
        return outs

    def merge_branch(l, a, n):
        sbr, wbr = load_block(l, f"br{a}")
        smg, wmg = load_block(l, f"mg{a}")
        for oc in range(KC):
            pp = psum(True)
            for hd in range(4):
                mm(pp.t[:, 0:n], wbr[:, hd, oc * 128:(oc + 1) * 128], Y[hd].t[:, 0:n], (sbr, Y[hd]), (pp,),
                   start=(hd == 0), stop=(hd == 3))
            pg = psum(True)
            for kc in range(KC):
                mm(pg.t[:, 0:n], wmg[:, kc, oc * 128:(oc + 1) * 128], H.t[:, kc, 0:n], (smg, H), (pg,),
                   start=(kc == 0), stop=(kc == KC - 1))
            sg = SG[oc % 2]
            act(sg.t[:, 0:n], pg.t[:, 0:n], AF.Sigmoid, (pg,), (sg,))
            if a == 0:
                tt(DVE, MERGEDv[:, oc, 0:n], sg.t[:, 0:n], pp.t[:, 0:n], ALU.mult, (sg, pp), (ARENA,))
            else:
                tt(DVE, sg.t[:, 0:n], sg.t[:, 0:n], pp.t[:, 0:n], ALU.mult, (sg, pp), (sg,))
                tt(DVE, MERGEDv[:, oc, 0:n], MERGEDv[:, oc, 0:n], sg.t[:, 0:n], ALU.add, (ARENA, sg), (ARENA,))

    def mixer(l, n, L, NCH, sample, last):
        PHASES.append(('mix_norm', P.PE.cnt))
        rmsnorm(3 * l + 1, n)
        RM = CMASK.t[:, 512:512 + n] if sample else CMASK.t[:, 0:n]

        def sset(c):
            return c if sample else l

        def chunked(ap):
            return ap.rearrange("p (c t) -> p c t", c=NCH)

        if cfg.stage <= 0:
            return
        PHASES.append(('A_proj', P.PE.cnt))
        proj_v(l, "a_v", n, L, NCH)
        for hd in range(4):
            s, wv = load_block(l, f"a_qfg{hd}")
            pq, pf, pg = proj3(s, wv, n, 3)
            t0, t1, t2, t3 = TMP[0], TMP[1], TMP[2], TMP[3]
            act(t0.t[:, 0:n], pf.t[:, 0:n], AF.Sigmoid, (pf,), (t0,))
            ts(DVE, t0.t[:, 0:n], t0.t[:, 0:n], OML.t[:, hd, l:l + 1], LB.t[:, hd, l:l + 1], ALU.mult, ALU.add,
               (t0, OML, LB), (t0,))
            ts(DVE, t1.t[:, 0:n], t0.t[:, 0:n], -1.0, 1.0, ALU.mult, ALU.add, (t0,), (t1,))
            act(t0.t[:, 0:n], t0.t[:, 0:n], AF.Ln, (t0,), (t0,))
            scan(t2.t[:, 0:n], RM, t0.t[:, 0:n], 0.0, ALU.mult, ALU.add, (CMASK, t0), (t2,))
            act(t0.t[:, 0:n], t2.t[:, 0:n], AF.Exp, (t2,), (t0,))
            act(t3.t[:, 0:n], t2.t[:, 0:n], AF.Exp, (t2,), (t3,), scale=-1.0)
            act(t2.t[:, 0:n], pq.t[:, 0:n], AF.Silu, (pq,), (t2,))
            tt(DVE, QH[hd].t[:, 0:n], t2.t[:, 0:n], t0.t[:, 0:n], ALU.mult, (t2, t0), (QH[hd],))
            tt(DVE, KT[hd].t[:, 0:n], t1.t[:, 0:n], t3.t[:, 0:n], ALU.mult, (t1, t3), (KT[hd],))
            cp(DVE, DL[hd].t[:, 0:NCH], chunked(t0.t[:, 0:n])[:, :, L - 1], (t0,), (DL[hd],))
            act(G[hd].t[:, 0:n], pg.t[:, 0:n], AF.Sigmoid, (pg,), (G[hd],))
        if cfg.stage <= 1:
            return
        PHASES.append(('A_chunks', P.PE.cnt))
        to_regions()
        for c in range(NCH):
            cs = slice(c * L, (c + 1) * L)
            st = sset(c)
            pas, pts, ams, kks, pus = [], [], [], [], []
            for hd in range(4):
                S = SA[st][hd]
                if c == 0 or sample:
                    cp(ACT, SAb[hd].t[:], S.t[:], (S,), (SAb[hd],))
                pa = PAR[(c % 2) * 4 + hd]
                mm(pa.t[0:L, 0:L], KT[hd].t[:, cs], QH[hd].t[:, cs], (KT[hd], QH[hd]), (pa.buf,))
                pt = ptr()
                tr(pt.t[0:L, 0:128], KT[hd].t[:, cs], identb.t[:], (KT[hd], identb), (pt.buf,))
                pas.append(pa)
                pts.append(pt)
            for hd in range(4):
                am = ATM[(c % 2) * 4 + hd]
                tt(DVE, am.t[0:L, 0:L], pas[hd].t[0:L, 0:L], maskT.t[0:L, 0:L], ALU.mult, (pas[hd].buf, maskT), (am,))
                kk = KTOK[(c % 2) * 4 + hd]
                cp(ACT, kk.t[0:L, :], pts[hd].t[0:L, 0:128], (pts[hd].buf,), (kk,))
                ams.append(am)
                kks.append(kk)
            for hd in range(4):
                mm(OACC[hd].t[:, cs], V.t[0:L, c, hd * 128:(hd + 1) * 128], ams[hd].t[0:L, 0:L], (V, ams[hd]), (OACC[hd],),
                   start=True, stop=False)
                mm(OACC[hd].t[:, cs], SAb[hd].t[:], QH[hd].t[:, cs], (SAb[hd], QH[hd]), (OACC[hd],), start=False, stop=True)
                pu = View(PSB[1 + c % 2], PSB[1 + c % 2].t[:, hd * 128:(hd + 1) * 128])
                mm(pu.t[:, 0:128], kks[hd].t[0:L, :], V.t[0:L, c, hd * 128:(hd + 1) * 128], (kks[hd], V), (pu.buf,))
                pus.append(pu)
            for hd in range(4):
                S = SA[st][hd]
                tt(DVE, S.t[:], S.t[:], pus[hd].t[:, 0:128], ALU.add, (S, pus[hd].buf), (S,))
                if (not sample) and c < NCH - 1:
                    act(SAb[hd].t[:], S.t[:], AF.Copy, (S, DL[hd]), (SAb[hd],), scale=DL[hd].t[:, c:c + 1])
                ts(DVE, S.t[:], S.t[:], DL[hd].t[:, c:c + 1], None, ALU.mult, None, (S, DL[hd]), (S,))
                if sample:
                    P.dma(SP, o_hgrn_s[l, c, hd], S.t[:], S, load=False)
                elif last and c == NCH - 1:
                    P.dma(SP, o_hgrn_p[l, hd], S.t[:], S, load=False)
        to_banks()
        if cfg.stage <= 2:
            return
        PHASES.append(('A_post', P.PE.cnt))
        for hd in range(4):
            t4 = TMP[4]
            tt(DVE, t4.t[:, 0:n], OACC[hd].t[:, 0:n], G[hd].t[:, 0:n], ALU.mult, (OACC[hd], G[hd]), (t4,))
            head_norm_scale(t4.t[:, 0:n], (t4,), n, R5.t[:, hd, depth + l:depth + l + 1], Y[hd].t[:, 0:n], Y[hd])
        PHASES.append(('A_merge', P.PE.cnt))
        merge_branch(l, 0, n)

        if cfg.stage <= 3:
            return
        PHASES.append(('B_gates', P.PE.cnt))
        ssm, wsm = load_block(l, "small")
        pi = psum(True)
        pf = psum(True)
        plr = psum(True)
        for kc in range(KC):
            mm(pi.t[0:4, 0:n], wsm[:, kc, 0:4], H.t[:, kc, 0:n], (ssm, H), (pi,), start=(kc == 0), stop=(kc == KC - 1))
        for kc in range(KC):
            mm(pf.t[0:4, 0:n], wsm[:, kc, 4:8], H.t[:, kc, 0:n], (ssm, H), (pf,), start=(kc == 0), stop=(kc == KC - 1))
        for kc in range(KC):
            mm(plr.t[0:16, 0:n], wsm[:, kc, 8:24], H.t[:, kc, 0:n], (ssm, H), (plr,), start=(kc == 0), stop=(kc == KC - 1))
        cp(ACT, LR.t[0:16, 0:n], plr.t[0:16, 0:n], (plr,), (LR,))
        g0, g1, g2, g3 = TMP[0], TMP[1], TMP[2], TMP[3]
        act(g0.t[0:4, 0:n], pf.t[0:4, 0:n], AF.Exp, (pf, NGBF), (g0,), scale=-1.0, bias=NGBF.t[:, l:l + 1])
        act(g0.t[0:4, 0:n], g0.t[0:4, 0:n], AF.Ln, (g0,), (g0,), bias=one_c.t[0:4, 0:1])
        if sample:
            scan(g1.t[0:4, 0:n], CMASK.t[0:4, 512:512 + n], g0.t[0:4, 0:n], 0.0, ALU.mult, ALU.add, (CMASK, g0), (g1,))
        else:
            scan(g1.t[0:4, 0:n], ones4.t[0:4, 0:n], g0.t[0:4, 0:n], FCY.t[:, l:l + 1], ALU.mult, ALU.add,
                 (ones4, g0, FCY), (g1,))
        stt(DVE, g2.t[0:4, 0:n], pi.t[0:4, 0:n], GB.t[:, l, 0:1], g1.t[0:4, 0:n], ALU.add, ALU.add,
            (pi, GB, g1), (g2,))
        if sample:
            P.dma(SP, M0.t[0:4, 0:NCH], st_m[l].rearrange("s h -> h s"), M0, load=True, allow_slow_non_contiguous=True)
            ga = TMP[4]
            cp(DVE, ga.t[0:4, 0:n], g2.t[0:4, 0:n], (g2,), (ga,))
            a0 = chunked(ga.t[0:4, 0:n])[:, :, 0]
            tt(DVE, a0, a0, M0.t[0:4, 0:NCH], ALU.max, (ga, M0), (ga,))
            scan(g3.t[0:4, 0:n], CMASK.t[0:4, 576:576 + n], ga.t[0:4, 0:n], NEG_BIG, ALU.add, ALU.max, (CMASK, ga), (g3,))
        else:
            scan(g3.t[0:4, 0:n], zeros4.t[0:4, 0:n], g2.t[0:4, 0:n], MCY.t[:, l:l + 1], ALU.add, ALU.max,
                 (zeros4, g2, MCY), (g3,))
        mcv = chunked(g3.t[0:4, 0:n])[:, :, L - 1]
        if sample:
            cp(DVE, MP.t[0:4, 0:NCH], M0.t[0:4, 0:NCH], (M0,), (MP,))
        else:
            cp(DVE, MP.t[0:4, 0:1], MCY.t[:, l:l + 1], (MCY,), (MP,))
            if NCH > 1:
                cp(DVE, MP.t[0:4, 1:NCH], chunked(g3.t[0:4, 0:n])[:, 0:NCH - 1, L - 1], (g3,), (MP,))
        tt(DVE, R4.t[0:4, 0:NCH], MP.t[0:4, 0:NCH], mcv, ALU.subtract, (MP, g3), (R4,))
        act(R4.t[0:4, 0:NCH], R4.t[0:4, 0:NCH], AF.Exp, (R4,), (R4,))
        tt(DVE, MO.t[0:4, 0:NCH], mcv, chunked(g1.t[0:4, 0:n])[:, :, L - 1], ALU.subtract, (g3, g1), (MO,))
        mcb = mcv.unsqueeze(2).broadcast_to([4, NCH, L])
        g4, g5 = TMP[4], TMP[5]
        tt(DVE, chunked(g4.t[0:4, 0:n]), chunked(g2.t[0:4, 0:n]), mcb, ALU.subtract, (g2, g3), (g4,))
        act(g4.t[0:4, 0:n], g4.t[0:4, 0:n], AF.Exp, (g4,), (g4,))
        tt(DVE, chunked(g5.t[0:4, 0:n]), chunked(g1.t[0:4, 0:n]), mcb, ALU.subtract, (g1, g3), (g5,))
        act(g5.t[0:4, 0:n], g5.t[0:4, 0:n], AF.Exp, (g5,), (g5,))
        if not sample:
            cp(DVE, FCY.t[:, l:l + 1], g1.t[0:4, n - 1:n], (g1,), (FCY,))
            cp(DVE, MCY.t[:, l:l + 1], g3.t[0:4, n - 1:n], (g3,), (MCY,))
        if cfg.stage <= 4:
            return
        pp_ = psum()
        for c in range(NCH):
            tr(pp_.t[0:L, c * 4:(c + 1) * 4], g4.t[0:4, c * L:(c + 1) * L], ident.t[0:4, 0:4], (g4, ident), (pp_,))
        cp(ACT, PTK.t[0:L, 0:NCH * 4], pp_.t[0:L, 0:NCH * 4], (pp_,), (PTK,))
        for hd in range(4):
            pr_ = psum()
            mm(pr_.t[:, 0:NCH], SEL.t[0:4, hd * 128:(hd + 1) * 128], R4.t[0:4, 0:NCH], (SEL, R4), (pr_,))
            cp(ACT, RB.t[:, hd, 0:NCH], pr_.t[:, 0:NCH], (pr_,), (RB,))
        if sample:
            P.dma(SP, o_m_s[l].rearrange("s h -> h s"), MO.t[0:4, 0:NCH], MO, load=False, allow_slow_non_contiguous=True)
        elif last:
            P.dma(SP, o_m_p[l].rearrange("(h o) -> h o", o=1), MO.t[0:4, NCH - 1:NCH], MO, load=False)
        if sample:
            P.dma(SP, NROWS.t[0:4 * NCH, :], st_n[l], NROWS, load=True)
            pn = psum()
            tr(pn.t[:, 0:4 * NCH], NROWS.t[0:4 * NCH, :], ident.t[0:4 * NCH, 0:4 * NCH], (NROWS, ident), (pn,))
            cp(ACT, NT.t[:, 0:4 * NCH], pn.t[:, 0:4 * NCH], (pn,), (NT,))
        if cfg.stage <= 5:
            return
        PHASES.append(('B_proj', P.PE.cnt))
        proj_v(l, "b_v", n, L, NCH)
        for hd in range(4):
            s, wv = load_block(l, f"b_qko{hd}")
            pq, pk, po = proj3(s, wv, n, 3)
            cp(ACT, QH[hd].t[:, 0:n], pq.t[:, 0:n], (pq,), (QH[hd],))
            cp(DVE, KT[hd].t[:, 0:n], pk.t[:, 0:n], (pk,), (KT[hd],))
            act(G[hd].t[:, 0:n], po.t[:, 0:n], AF.Sigmoid, (po,), (G[hd],))
        if cfg.stage <= 6:
            return
        PHASES.append(('B_chunks_post', P.PE.cnt))
        for hp in range(2):
            to_regions()
            for c in range(NCH):
                cs = slice(c * L, (c + 1) * L)
                st = sset(c)
                pas, pts, ams, kks, pus = [], [], [], [], []
                for e in range(2):
                    hd = hp * 2 + e
                    C = CB[st][hd]
                    if sample:
                        cp(DVE, C.t[:, 128:256], NT.t[:, c * 4 + hd:c * 4 + hd + 1].broadcast_to([128, 128]), (NT,), (C,))
                    if c == 0 or sample:
                        act(CBb[hd].t[:], C.t[:], AF.Copy, (C, RB), (CBb[hd],), scale=RB.t[:, hd, c:c + 1])
                    pa = PAR[(c % 2) * 4 + e]
                    mm(pa.t[0:L, 0:L], KT[hd].t[:, cs], QH[hd].t[:, cs], (KT[hd], QH[hd]), (pa.buf,))
                    pt = ptr()
                    tr(pt.t[0:L, 0:128], KT[hd].t[:, cs], identb.t[:], (KT[hd], identb), (pt.buf,))
                    pas.append(pa)
                    pts.append(pt)
                for e in range(2):
                    hd = hp * 2 + e
                    pcol = PTK.t[0:L, c * 4 + hd:c * 4 + hd + 1]
                    am = ATM[(c % 2) * 4 + e]
                    stt(DVE, am.t[0:L, 0:L], pas[e].t[0:L, 0:L], pcol, maskS.t[0:L, 0:L], ALU.mult, ALU.mult,
                        (pas[e].buf, PTK, maskS), (am,))
                    kk = KTOK[(c % 2) * 4 + e]
                    ts(DVE, kk.t[0:L, :], pts[e].t[0:L, 0:128], pcol, 128.0 ** -0.5, ALU.mult, ALU.mult, (pts[e].buf, PTK), (kk,))
                    ams.append(am)
                    kks.append(kk)
                for e in range(2):
                    hd = hp * 2 + e
                    NUM = OACC[2 * e]
                    DEN = OACC[2 * e + 1]
                    am = ams[e]
                    kk = kks[e]
                    mm(NUM.t[:, cs], V.t[0:L, c, hd * 128:(hd + 1) * 128], am.t[0:L, 0:L], (V, am), (NUM,),
                       start=True, stop=False)
                    mm(NUM.t[:, cs], CBb[hd].t[:, 0:128], QH[hd].t[:, cs], (CBb[hd], QH[hd]), (NUM,), start=False, stop=True)
                    mm(DEN.t[:, cs], ones64.t[0:L, :], am.t[0:L, 0:L], (ones64, am), (DEN,), start=True, stop=False)
                    mm(DEN.t[:, cs], CBb[hd].t[:, 128:256], QH[hd].t[:, cs], (CBb[hd], QH[hd]), (DEN,), start=False, stop=True)
                    pu = PUR[(c % 2) * 2 + e]
                    mm(pu.t[:, 0:128], kk.t[0:L, :], V.t[0:L, c, hd * 128:(hd + 1) * 128], (kk, V), (pu.buf,))
                    mm(pu.t[:, 128:256], kk.t[0:L, :], ones64.t[0:L, :], (kk, ones64), (pu.buf,))
                    pus.append(pu)
                for e in range(2):
                    hd = hp * 2 + e
                    C = CB[st][hd]
                    stt(DVE, C.t[:], C.t[:], RB.t[:, hd, c:c + 1], pus[e].t[:, 0:256], ALU.mult, ALU.add, (C, RB, pus[e].buf), (C,))
                    if (not sample) and c < NCH - 1:
                        act(CBb[hd].t[:], C.t[:], AF.Copy, (C, RB), (CBb[hd],), scale=RB.t[:, hd, c + 1:c + 2])
                    fin = sample or (last and c == NCH - 1)
                    if fin:
                        idx = (c * 4 + hd) if sample else hd
                        cp(DVE, NT.t[:, idx:idx + 1], C.t[:, 128:129], (C,), (NT,))
                        dst = o_c_s[l, c, hd] if sample else o_c_p[l, hd]
                        P.dma(SP, dst, C.t[:, 0:128], C, load=False)
            to_banks()
            for e in range(2):
                hd = hp * 2 + e
                NUM = OACC[2 * e]
                DEN = OACC[2 * e + 1]
                pth = psum()
                mm(pth.t[:, 0:n], SEL.t[0:4, hd * 128:(hd + 1) * 128], g5.t[0:4, 0:n], (SEL, g5), (pth,))
                t0, t1 = TMP[0], TMP[1]
                act(t0.t[:, 0:n], DEN.t[:, 0:n], AF.Abs, (DEN,), (t0,))
                tt(DVE, t0.t[:, 0:n], t0.t[:, 0:n], pth.t[:, 0:n], ALU.max, (t0, pth), (t0,))
                P.op(DVE, lambda: nc.vector.reciprocal(out=t0.t[:, 0:n], in_=t0.t[:, 0:n]), reads=(t0,), writes=(t0,))
                tt(DVE, t1.t[:, 0:n], NUM.t[:, 0:n], t0.t[:, 0:n], ALU.mult, (NUM, t0), (t1,))
                head_norm_scale(t1.t[:, 0:n], (t1,), n, R5.t[:, hd, 2 * depth + l:2 * depth + l + 1],
                                Y[hd].t[:, 0:n], Y[hd], gate=G[hd])
        if cfg.stage <= 7:
            return
        if sample or last:
            ncols = 4 * NCH if sample else 4
            pn = psum()
            tr(pn.t[0:ncols, 0:128], NT.t[:, 0:ncols], ident.t[:, :], (NT, ident), (pn,))
            cp(ACT, NROWS.t[0:ncols, :], pn.t[0:ncols, 0:128], (pn,), (NROWS,))
            P.dma(SP, (o_n_s[l] if sample else o_n_p[l]), NROWS.t[0:ncols, :], NROWS, load=False)
        PHASES.append(('B_merge', P.PE.cnt))
        merge_branch(l, 1, n)

        if cfg.stage <= 8:
            return
        PHASES.append(('C_proj', P.PE.cnt))
        proj_v(l, "c_v", n, L, NCH)
        for pr in range(2):
            s, wv = load_block(l, f"c_qk{pr}")
            for e in range(2):
                hd = 2 * pr + e
                pq = psum(True)
                pk = psum(True)
                pz = psum(True)
                for kc in range(KC):
                    mm(pq.t[0:64, 0:n], wv[:, kc, e * 64:(e + 1) * 64], H.t[:, kc, 0:n], (s, H), (pq,),
                       start=(kc == 0), stop=(kc == KC - 1))
                for kc in range(KC):
                    mm(pk.t[0:64, 0:n], wv[:, kc, 128 + e * 64:128 + (e + 1) * 64], H.t[:, kc, 0:n], (s, H), (pk,),
                       start=(kc == 0), stop=(kc == KC - 1))
                mm(pz.t[0:64, 0:n], W2b.t[0:16, l, hd * 64:(hd + 1) * 64], LR.t[0:16, 0:n], (W2b, LR), (pz,))
                t0, t2, t3 = TMP[0], TMP[2], TMP[3]
                act(t0.t[0:64, 0:n], pz.t[0:64, 0:n], AF.Exp, (pz, NBA), (t0,), scale=-1.0, bias=NBA.t[0:64, hd, l:l + 1])
                act(t0.t[0:64, 0:n], t0.t[0:64, 0:n], AF.Ln, (t0,), (t0,), bias=one_c.t[0:64, 0:1])
                scan(t2.t[0:64, 0:n], RM[0:64], t0.t[0:64, 0:n], 0.0, ALU.mult, ALU.add, (CMASK, t0), (t2,))
                act(t0.t[0:64, 0:n], t2.t[0:64, 0:n], AF.Exp, (t2,), (t0,), scale=-1.0 / 16.0)
                act(t3.t[0:64, 0:n], t2.t[0:64, 0:n], AF.Exp, (t2,), (t3,), scale=1.0 / 16.0)
                stt(DVE, QH[hd].t[0:64, 0:n], pq.t[0:64, 0:n], 0.125, t0.t[0:64, 0:n], ALU.mult, ALU.mult,
                    (pq, t0), (QH[hd],))
                tt(DVE, KT[hd].t[0:64, 0:n], pk.t[0:64, 0:n], t3.t[0:64, 0:n], ALU.mult, (pk, t3), (KT[hd],))
                cp(DVE, DL[hd].t[0:64, 0:NCH], chunked(t0.t[0:64, 0:n])[:, :, L - 1], (t0,), (DL[hd],))
        if cfg.stage <= 9:
            return
        s, wv = load_block(l, "c_g")
        for hd in range(4):
            pb = psum(True)
            for kc in range(KC):
                mm(pb.t[:, 0:n], wv[:, kc, hd * 128:(hd + 1) * 128], H.t[:, kc, 0:n], (s, H), (pb,),
                   start=(kc == 0), stop=(kc == KC - 1))
            act(G[hd].t[:, 0:n], pb.t[:, 0:n], AF.Silu, (pb,), (G[hd],))
        if cfg.stage <= 10:
            return
        PHASES.append(('C_chunks', P.PE.cnt))
        to_regions()
        for c in range(NCH):
            cs = slice(c * L, (c + 1) * L)
            st = sset(c)
            pas, pts, ams, kks, pus = [], [], [], [], []
            for hd in range(4):
                S = SC[st][hd]
                Sb = SAb[hd]
                if c == 0 or sample:
                    cp(ACT, Sb.t[0:64, :], S.t[:], (S,), (Sb,))
                pa = PAR[(c % 2) * 4 + hd]
                mm(pa.t[0:L, 0:L], KT[hd].t[0:64, cs], QH[hd].t[0:64, cs], (KT[hd], QH[hd]), (pa.buf,))
                pt = ptr()
                tr(pt.t[0:L, 0:64], KT[hd].t[0:64, cs], identb.t[0:64, 0:64], (KT[hd], identb), (pt.buf,))
                pas.append(pa)
                pts.append(pt)
            for hd in range(4):
                am = ATM[(c % 2) * 4 + hd]
                tt(DVE, am.t[0:L, 0:L], pas[hd].t[0:L, 0:L], maskT.t[0:L, 0:L], ALU.mult, (pas[hd].buf, maskT), (am,))
                kk = KTOK[(c % 2) * 4 + hd]
                cp(ACT, kk.t[0:L, 0:64], pts[hd].t[0:L, 0:64], (pts[hd].buf,), (kk,))
                ams.append(am)
                kks.append(kk)
            for hd in range(4):
                Sb = SAb[hd]
                mm(OACC[hd].t[:, cs], V.t[0:L, c, hd * 128:(hd + 1) * 128], ams[hd].t[0:L, 0:L], (V, ams[hd]), (OACC[hd],),
                   start=True, stop=False)
                mm(OACC[hd].t[:, cs], Sb.t[0:64, :], QH[hd].t[0:64, cs], (Sb, QH[hd]), (OACC[hd],), start=False, stop=True)
                pu = View(PSB[1 + c % 2], PSB[1 + c % 2].t[:, hd * 128:(hd + 1) * 128])
                mm(pu.t[0:64, 0:128], kks[hd].t[0:L, 0:64], V.t[0:L, c, hd * 128:(hd + 1) * 128], (kks[hd], V), (pu.buf,))
                pus.append(pu)
            for hd in range(4):
                S = SC[st][hd]
                Sb = SAb[hd]
                tt(DVE, S.t[:], S.t[:], pus[hd].t[0:64, 0:128], ALU.add, (S, pus[hd].buf), (S,))
                if (not sample) and c < NCH - 1:
                    act(Sb.t[0:64, :], S.t[:], AF.Copy, (S, DL[hd]), (Sb,), scale=DL[hd].t[0:64, c:c + 1])
                ts(DVE, S.t[:], S.t[:], DL[hd].t[0:64, c:c + 1], None, ALU.mult, None, (S, DL[hd]), (S,))
                if sample:
                    P.dma(SP, o_gla_s[l, c, hd], S.t[:], S, load=False)
                elif last and c == NCH - 1:
                    P.dma(SP, o_gla_p[l, hd], S.t[:], S, load=False)
        to_banks()
        if cfg.stage <= 11:
            return
        PHASES.append(('C_post', P.PE.cnt))
        for hd in range(4):
            head_norm_scale(OACC[hd].t[:, 0:n], (OACC[hd],), n, R5.t[:, hd, 3 * depth + l:3 * depth + l + 1],
                            Y[hd].t[:, 0:n], Y[hd], gate=G[hd])
        PHASES.append(('C_merge', P.PE.cnt))
        merge_branch(l, 2, n)

        if cfg.stage <= 12:
            return
        PHASES.append(('out_proj', P.PE.cnt))
        for kc in range(KC):
            cp(ACT if kc % 2 else DVE, H.t[:, kc, 0:n], MERGEDv[:, kc, 0:n], (ARENA,), (H,))
        s, wv = load_block(l, "wout")
        for oc in range(KC):
            pb = psum(True)
            for kc in range(KC):
                mm(pb.t[:, 0:n], wv[:, kc, oc * 128:(oc + 1) * 128], H.t[:, kc, 0:n], (s, H), (pb,),
                   start=(kc == 0), stop=(kc == KC - 1))
            tt(DVE, X.t[:, oc, 0:n], X.t[:, oc, 0:n], pb.t[:, 0:n], ALU.add, (X, pb), (X,))

    def load_sample_states(l):
        for sq in range(NSEQ):
            for hd in range(4):
                P.dma(SP, SA[sq][hd].t[:], st_hgrn[l, sq, hd], SA[sq][hd], load=True)
                P.dma(SP, CB[sq][hd].t[:, 0:128], st_c[l, sq, hd], CB[sq][hd], load=True)
            for hd in range(4):
                P.dma(SP, SC[sq][hd].t[:], st_gla[l, sq, hd], SC[sq][hd], load=True)

    def zero_states():
        for st in range(NSET):
            for hd in range(4):
                ms(POOL, SA[st][hd], SA[st][hd].t[:], 0.0)
                ms(POOL, CB[st][hd], CB[st][hd].t[:], 0.0)
            for hd in range(4):
                ms(POOL, SC[st][hd], SC[st][hd].t[:], 0.0)

    do_mixer = cfg.mixer
    if NS > 0:
        load_x(xs[0:NS, :], NS)
        for l in range(depth):
            ffn(l, 1, NS)
            if do_mixer:
                load_sample_states(l)
                mixer(l, NS, SL, NSEQ, True, False)
            ffn(l, 2, NS)
        store_y(ys[0:NS, :], NS)
    if NP > 0:
        zero_states()
        ntile = NP // T
        for ti in range(ntile):
            load_x(xp[ti * T:(ti + 1) * T, :], T)
            for l in range(depth):
                ffn(l, 1, T)
                if do_mixer:
                    mixer(l, T, 64, T // 64, False, ti == ntile - 1)
                ffn(l, 2, T)
            store_y(yp[ti * T:(ti + 1) * T, :], T)

    for b in XIO + [MO, NROWS] + [x for st in SA + CB + SC for x in st]:
        if b.dcnt:
            P._wait(SP, (b.dsem, b.dcnt))
    for e in (PE, ACT, DVE, POOL):
        if e.cnt:
            P._wait(SP, (e.sem, e.cnt))
    print(f"[build] inst={P.n_inst} waits={P.n_wait} sems={P.nsem} sbuf_left={nc.sbuf_bytes_remaining}")


_NC_CACHE = {}


def kernel(**inputs):
    inp = {k: np.asarray(v) for k, v in inputs.items()}
    B, SEQ, _ = inp["x_prompt"].shape
    NSAMP, SL, _ = inp["x_sample"].shape
    n_cores = 8
    nseq = NSAMP // n_cores
    cfg = Cfg(SEQ, nseq, SL, depth=DEPTH)
    key = (SEQ, nseq, SL)
    if key not in _NC_CACHE:
        _NC_CACHE[key] = build_program(cfg)
    nc = _NC_CACHE[key]
    in_maps = [core_inputs(inp, cfg, c % B, c, list(range(c * nseq, (c + 1) * nseq))) for c in range(n_cores)]
    res = run_bass_kernel_spmd(nc, in_maps, core_ids=list(range(n_cores)))
    R = res.results
    f32 = np.float32
    y_prompt = np.stack([R[b]["y_prompt"] for b in range(B)]).astype(f32)
    y_sample = np.concatenate([R[c]["y_sample"].reshape(nseq, SL, D_MODEL) for c in range(n_cores)], 0).astype(f32)

    def pstate(name, shape):
        return np.stack([R[b][name].reshape(shape) for b in range(B)], 1).astype(f32)

    def sstate(name, shape):
        return np.concatenate([R[c][name].reshape((DEPTH, nseq) + shape) for c in range(n_cores)], 1).astype(f32)

    return (
        y_prompt, y_sample,
        pstate("hgrn_p", (DEPTH, 4, 128, 128)), pstate("c_p", (DEPTH, 4, 128, 128)), pstate("n_p", (DEPTH, 4, 128)),
        pstate("m_p", (DEPTH, 4)), pstate("gla_p", (DEPTH, 4, 64, 128)),
        sstate("hgrn_s", (4, 128, 128)), sstate("c_s", (4, 128, 128)), sstate("n_s", (4, 128)),
        sstate("m_s", (4,)), sstate("gla_s", (4, 64, 128)),
    )


def core_inputs(inp, cfg, b, core, seqs):
    depth = cfg.depth
    f = np.ascontiguousarray
    NP, NSEQ, SL = cfg.n_prompt, cfg.n_seq, cfg.seq_len
    nsq = max(NSEQ, 1)
    sq = list(seqs) if NSEQ > 0 else [0]
    m = {
        "x_prompt": f(inp["x_prompt"][b, :max(NP, 1)]),
        "x_sample": f(inp["x_sample"][sq].reshape(nsq * SL, D_MODEL)[:max(NSEQ * SL, 1)]),
        "st_hgrn": f(inp["state_hgrn"][:depth, sq]),
        "st_c": f(inp["state_mlstm_c"][:depth, sq]),
        "st_n": f(inp["state_mlstm_n"][:depth, sq].reshape(depth, nsq * 4, 128)),
        "st_m": f(inp["state_mlstm_m"][:depth, sq]),
        "st_gla": f(inp["state_gla"][:depth, sq]),
        "norms": f(np.concatenate([np.stack([inp["ffn1_norm"][l], inp["mix_norm"][l], inp["ffn2_norm"][l]])
                                   for l in range(depth)] + [inp["final_norm"][None]], 0)),
        "rows512": f(np.concatenate([inp["hgrn_lb_logits"][:depth], inp["hgrn_norm"][:depth],
                                     inp["mlstm_norm"][:depth], inp["gla_norm"][:depth]], 0)),
        "gla_b_a": f(inp["gla_b_a"][:depth]),
        "mlstm_gate_bias": f(inp["mlstm_gate_bias"][:depth]),
        "gla_w_a2": f(inp["gla_w_a2"][:depth]),
    }
    for k in ("ffn1_w_up", "ffn1_w_down", "ffn2_w_up", "ffn2_w_down", "w_in", "w_branch", "w_out"):
        m[k] = f(inp[k][:depth])
    m.update(host_consts())
    return m
```

```python
import math
from contextlib import ExitStack

import numpy as np
import concourse.bass as bass
import concourse.mybir as mybir
from concourse.bass_utils import run_bass_kernel_spmd

F32 = mybir.dt.float32
BF16 = mybir.dt.bfloat16
AF = mybir.ActivationFunctionType
ALU = mybir.AluOpType
AX = mybir.AxisListType

D_MODEL = 1024
DEPTH = 4
D_FF = 2816
D_IN = 8728
EPS = 1e-6
KC = 8
FC = 22
NEG_BIG = -1.0e30


PHASES = []


class Buf:
    def __init__(self, name, t, dsem=None):
        self.name = name
        self.t = t
        self.w = None
        self.r = {}
        self.dsem = dsem
        self.dcnt = 0

    def __getitem__(self, k):
        return self.t[k]


class Eng:
    def __init__(self, name, h, sem):
        self.name = name
        self.h = h
        self.sem = sem
        self.cnt = 0
        self.seen = {}


class Prog:
    def __init__(self, nc, es):
        self.nc = nc
        self.es = es
        self.nsem = 0
        self.PE = self._eng("pe", nc.tensor)
        self.ACT = self._eng("act", nc.scalar)
        self.DVE = self._eng("dve", nc.vector)
        self.POOL = self._eng("pool", nc.gpsimd)
        self.SP = self._eng("sp", nc.sync)
        self.engs = [self.PE, self.ACT, self.DVE, self.POOL, self.SP]
        self.n_inst = 0
        self.n_wait = 0

    def sem(self, name):
        self.nsem += 1
        return self.es.enter_context(self.nc.semaphore(name))

    def _eng(self, name, h):
        return Eng(name, h, self.sem("e_" + name))

    def sb(self, name, shape, dtype, dma=False):
        t = self.es.enter_context(self.nc.sbuf_tensor(name, list(shape), dtype))
        return Buf(name, t, self.sem("d_" + name) if dma else None)

    def ps(self, name, shape, dtype=F32):
        t = self.es.enter_context(self.nc.psum_tensor(name, list(shape), dtype))
        return Buf(name, t)

    def _need(self, e, tok, acc):
        sem, val = tok
        if e is self.PE and sem is self.PE.sem:
            return
        if e.seen.get(sem, 0) >= val:
            return
        e.seen[sem] = val
        acc[sem] = max(acc.get(sem, 0), val)

    def _wait(self, e, tok):
        acc = {}
        self._need(e, tok, acc)
        for sem, val in acc.items():
            e.h.wait_ge(sem, val)
            self.n_wait += 1

    def _deps(self, e, reads, writes):
        acc = {}
        for b in reads:
            if b.w is not None:
                self._need(e, b.w, acc)
        for b in writes:
            if b.w is not None:
                self._need(e, b.w, acc)
            for sem, val in b.r.items():
                self._need(e, (sem, val), acc)
        return list(acc.items())

    def _emit(self, e, fn, waits):
        for sem, val in waits[:-1]:
            e.h.wait_ge(sem, val)
            self.n_wait += 1
        inst = fn()
        if waits:
            sem, val = waits[-1]
            inst._wait_ge(sem, val)
        return inst

    def op(self, e, fn, reads=(), writes=()):
        waits = self._deps(e, reads, writes)
        inst = self._emit(e, fn, waits)
        e.cnt += 1
        inst.then_inc(e.sem, 1)
        tok = (e.sem, e.cnt)
        for b in writes:
            b.w = tok
            b.r = {}
        for b in reads:
            if b not in writes:
                b.r[e.sem] = e.cnt
        self.n_inst += 1
        return inst

    def dma(self, q, out, in_, buf, load, **kw):
        if load:
            waits = self._deps(q, (), (buf,))
        else:
            waits = self._deps(q, (buf,), ())
        inst = self._emit(q, lambda: q.h.dma_start(out=out, in_=in_, **kw), waits)
        inst.then_inc(buf.dsem, 16)
        buf.dcnt += 16
        tok = (buf.dsem, buf.dcnt)
        if load:
            buf.w = tok
            buf.r = {}
        else:
            buf.r[buf.dsem] = buf.dcnt
        self.n_inst += 1
        return inst


class Cfg:
    def __init__(self, n_prompt, n_seq, seq_len, depth=DEPTH, T=512, mixer=True, stage=99):
        self.mixer = mixer
        self.stage = stage
        self.n_prompt = n_prompt
        self.n_seq = n_seq
        self.seq_len = seq_len
        self.depth = depth
        self.T = T


def weight_blocks():
    blks = []
    for f in (1, 2):
        for i in range(5):
            blks.append((f"up{f}_{i}", f"ffn{f}_w_up", 1024, [(512 * i, 512), (2816 + 512 * i, 512)]))
        blks.append((f"up{f}_5", f"ffn{f}_w_up", 1024, [(2560, 256), (5376, 256)]))
        for i in range(4):
            blks.append((f"dn{f}_{i}", f"ffn{f}_w_down", 2816, [(256 * i, 256)]))
    for hd in range(4):
        blks.append((f"a_qfg{hd}", "w_in", 1024,
                     [(hd * 128, 128), (512 + hd * 128, 128), (1536 + hd * 128, 128)]))
    blks.append(("a_v", "w_in", 1024, [(1024, 512)]))
    for hd in range(4):
        blks.append((f"b_qko{hd}", "w_in", 1024,
                     [(2048 + hd * 128, 128), (2560 + hd * 128, 128), (3584 + hd * 128, 128)]))
    blks.append(("b_v", "w_in", 1024, [(3072, 512)]))
    blks.append(("small", "w_in", 1024, [(4096, 8), (5640, 16)]))
    for pr in range(2):
        blks.append((f"c_qk{pr}", "w_in", 1024, [(4104 + pr * 128, 128), (4360 + pr * 128, 128)]))
    blks.append(("c_v", "w_in", 1024, [(4616, 512)]))
    blks.append(("c_g", "w_in", 1024, [(5128, 512)]))
    for a in range(3):
        blks.append((f"mg{a}", "w_in", 1024, [(5656 + a * 1024, 1024)]))
    for a in range(3):
        blks.append((f"br{a}", "w_branch", 512, [(0, 1024)]))
    blks.append(("wout", "w_out", 1024, [(0, 1024)]))
    return blks


def build_program(cfg):
    nc = bass.Bass("TRN2", target_bir_lowering=False)
    es = ExitStack()
    with es:
        _build(nc, es, cfg)
    return nc


def host_consts():
    ident = np.eye(128, dtype=np.float32)
    maskT = np.triu(np.ones((64, 64), dtype=np.float32))
    cm = np.ones((128, 512 + 64 + 64), dtype=np.float32)
    cm[:, 0:512:64] = 0.0
    cm[:, 512:576:16] = 0.0
    cm[:, 576:640] = 0.0
    cm[:, 576:640:16] = NEG_BIG
    sel = np.zeros((4, 512), dtype=np.float32)
    for h in range(4):
        sel[h, h * 128:(h + 1) * 128] = 1.0
    return {"ident_in": ident, "maskT_in": maskT, "cmask_in": cm, "sel_in": sel}


def _build(nc, es, cfg):
    P = Prog(nc, es)
    T = cfg.T
    depth = cfg.depth
    NP = cfg.n_prompt
    NSEQ = cfg.n_seq
    SL = cfg.seq_len
    NS = NSEQ * SL
    NSET = max(depth, NSEQ, 1)
    PE, ACT, DVE, POOL, SP = P.PE, P.ACT, P.DVE, P.POOL, P.SP

    def din(name, shape):
        return nc.dram_tensor(name, list(shape), F32, kind="ExternalInput").ap()

    def dout(name, shape):
        return nc.dram_tensor(name, list(shape), F32, kind="ExternalOutput").ap()

    xp = din("x_prompt", [max(NP, 1), D_MODEL])
    yp = dout("y_prompt", [max(NP, 1), D_MODEL])
    xs = din("x_sample", [max(NS, 1), D_MODEL])
    ys = dout("y_sample", [max(NS, 1), D_MODEL])
    nsq = max(NSEQ, 1)
    st_hgrn = din("st_hgrn", [depth, nsq, 4, 128, 128])
    st_c = din("st_c", [depth, nsq, 4, 128, 128])
    st_n = din("st_n", [depth, nsq * 4, 128])
    st_m = din("st_m", [depth, nsq, 4])
    st_gla = din("st_gla", [depth, nsq, 4, 64, 128])
    o_hgrn_p = dout("hgrn_p", [depth, 4, 128, 128])
    o_c_p = dout("c_p", [depth, 4, 128, 128])
    o_n_p = dout("n_p", [depth, 4, 128])
    o_m_p = dout("m_p", [depth, 4])
    o_gla_p = dout("gla_p", [depth, 4, 64, 128])
    o_hgrn_s = dout("hgrn_s", [depth, nsq, 4, 128, 128])
    o_c_s = dout("c_s", [depth, nsq, 4, 128, 128])
    o_n_s = dout("n_s", [depth, nsq * 4, 128])
    o_m_s = dout("m_s", [depth, nsq, 4])
    o_gla_s = dout("gla_s", [depth, nsq, 4, 64, 128])
    w_dram = {
        "ffn1_w_up": din("ffn1_w_up", [depth, 1024, 2 * D_FF]),
        "ffn1_w_down": din("ffn1_w_down", [depth, D_FF, 1024]),
        "ffn2_w_up": din("ffn2_w_up", [depth, 1024, 2 * D_FF]),
        "ffn2_w_down": din("ffn2_w_down", [depth, D_FF, 1024]),
        "w_in": din("w_in", [depth, 1024, D_IN]),
        "w_branch": din("w_branch", [depth, 3, 512, 1024]),
        "w_out": din("w_out", [depth, 1024, 1024]),
    }
    norms = din("norms", [3 * depth + 1, 1024])
    rows512 = din("rows512", [4 * depth, 512])
    gla_b_a = din("gla_b_a", [depth, 256])
    gate_bias = din("mlstm_gate_bias", [depth, 8])
    gla_w2 = din("gla_w_a2", [depth, 16, 256])
    identd = din("ident_in", [128, 128])
    maskd = din("maskT_in", [64, 64])
    cmaskd = din("cmask_in", [128, 640])
    seld = din("sel_in", [4, 512])

    blks = weight_blocks()
    scratch = {}
    for l in range(depth):
        for key, src, K, cols in blks:
            kc = K // 128
            W = sum(n for _, n in cols)
            scratch[(l, key)] = (nc.dram_tensor(f"ws_{l}_{key}", [128, kc * W], BF16, kind="Internal").ap(), kc, W)

    cast_sem = P.sem("cast")
    n_cast = 0
    for l in range(depth):
        for key, src, K, cols in blks:
            sap, kc, W = scratch[(l, key)]
            sview = sap.rearrange("p (k w) -> p k w", k=kc)
            off = 0
            for c0, n in cols:
                if src == "w_branch":
                    a = int(key[2])
                    srcap = w_dram[src][l, a, :, c0:c0 + n]
                else:
                    srcap = w_dram[src][l, :, c0:c0 + n]
                srcap = srcap.rearrange("(k p) n -> p k n", p=128)
                POOL.h.dma_start(out=sview[:, :, off:off + n], in_=srcap).then_inc(cast_sem, 16)
                n_cast += 1
                off += n
    cast_tok = (cast_sem, 16 * n_cast)

    NSLOT = 3
    SLOTW = 8192
    slots = [P.sb(f"wslot{i}", [128, SLOTW], BF16, dma=True) for i in range(NSLOT)]
    slot_i = [0]

    def load_block(l, key):
        sap, kc, W = scratch[(l, key)]
        s = slots[slot_i[0] % NSLOT]
        slot_i[0] += 1
        P._wait(SP, cast_tok)
        P.dma(SP, s.t[:, 0:kc * W], sap[:, :], s, load=True)
        return s, s.t[:, 0:kc * W].rearrange("p (k w) -> p k w", k=kc)

    X = P.sb("X", [128, KC, T], F32)
    H = P.sb("H", [128, KC, T], BF16)
    ARENA = P.sb("ARENA", [128, FC * T // 2], F32)
    HIDv = ARENA.t[:, :].bitcast(BF16).rearrange("p (c t) -> p c t", c=FC)
    MERGEDv = ARENA.t[:, 0:KC * T].rearrange("p (c t) -> p c t", c=KC)
    SQ = P.sb("SQ", [128, KC, T], BF16)
    RSTD = P.sb("RSTD", [128, T], F32)
    SG = [P.sb(f"SG{i}", [128, T], F32) for i in range(2)]
    XIO = [P.sb(f"XIO{i}", [128, D_MODEL], F32, dma=True) for i in range(2)]
    csem = P.sem("consts")
    cbufs = []

    def cload(name, shape, dtype, src, q=None, **kw):
        b = P.sb(name, shape, dtype)
        b.dsem = csem
        cbufs.append(b)
        (q or SP).h.dma_start(out=b.t[:], in_=src, **kw).then_inc(csem, 16)
        return b

    ident = cload("ident", [128, 128], F32, identd[:, :])
    NROW = XIO[0]
    R5ROW = XIO[1]
    P.dma(POOL, XIO[0].t[0:3 * depth + 1, :], norms[:, :], XIO[0], load=True)
    P.dma(POOL, XIO[1].t[0:4 * depth, 0:512], rows512[:, :], XIO[1], load=True)
    P.dma(POOL, XIO[1].t[0:depth, 512:768], gla_b_a[:, :], XIO[1], load=True)
    maskT = cload("maskT", [64, 64], F32, maskd[:, :])
    CMASK = cload("CMASK", [128, 640], F32, cmaskd[:, :])
    SEL = cload("SEL", [4, 512], F32, seld[:, :])
    W2b = P.sb("W2b", [16, depth, 256], BF16, dma=True)
    P.dma(POOL, W2b.t[:], gla_w2.rearrange("l r c -> r l c"), W2b, load=True)
    GB = cload("GB", [4, depth, 2], F32, gate_bias.rearrange("l (g h) -> h l g", g=2), allow_slow_non_contiguous=True)
    for b in cbufs:
        b.w = (csem, 16 * len(cbufs))

    eps_c = P.sb("eps_c", [128, 1], F32)
    one_c = P.sb("one_c", [128, 1], F32)
    ones_m = P.sb("ones_m", [128, 128], BF16)
    ones_h = P.sb("ones_h", [128, 128], BF16)
    ones64 = P.sb("ones64", [64, 128], BF16)
    identb = P.sb("identb", [128, 128], BF16)
    maskS = P.sb("maskS", [64, 64], F32)
    NW = P.sb("NW", [128, KC, 16], F32)
    R5 = P.sb("R5", [128, 4, 16], F32)
    LB = P.sb("LB", [128, 4, 4], F32)
    OML = P.sb("OML", [128, 4, 4], F32)
    NBA = P.sb("NBA", [64, 4, 4], F32)
    NGBF = P.sb("NGBF", [4, 4], F32)
    ones4 = P.sb("ones4", [4, 512], F32)
    zeros4 = P.sb("zeros4", [4, 512], F32)

    QH = [P.sb(f"QH{i}", [128, T], BF16) for i in range(4)]
    KT = [P.sb(f"KT{i}", [128, T], BF16) for i in range(4)]
    G = [P.sb(f"G{i}", [128, T], BF16) for i in range(4)]
    Y = [P.sb(f"Y{i}", [128, T], BF16) for i in range(4)]
    V = P.sb("V", [64, 8, 512], BF16)
    TMP = [P.sb(f"TMP{i}", [128, T], F32) for i in range(6)]
    DL = [P.sb(f"DL{i}", [128, 8], F32) for i in range(4)]
    ATM = [P.sb(f"ATM{i}", [64, 64], BF16) for i in range(8)]
    KTOK = [P.sb(f"KTOK{i}", [64, 128], BF16) for i in range(8)]
    SQH = P.sb("SQH", [128, T], BF16)
    LR = P.sb("LR", [16, T], BF16)
    PTK = P.sb("PTK", [64, 32], F32)
    RB = P.sb("RB", [128, 4, 8], F32)
    R4 = P.sb("R4", [4, 8], F32)
    MP = P.sb("MP", [4, 8], F32)
    MO = P.sb("MO", [4, 8], F32, dma=True)
    M0 = P.sb("M0", [4, 8], F32, dma=True)
    NT = P.sb("NT", [128, 16], F32)
    NROWS = P.sb("NROWS", [16, 128], F32, dma=True)
    FCY = P.sb("FCY", [4, NSET], F32)
    MCY = P.sb("MCY", [4, NSET], F32)
    SAb = [P.sb(f"SAb{i}", [128, 128], BF16) for i in range(4)]
    CBb = [P.sb(f"CBb{i}", [128, 256], BF16) for i in range(4)]
    SA = [[P.sb(f"SA{s}_{i}", [128, 128], F32, dma=True) for i in range(4)] for s in range(NSET)]
    CB = [[P.sb(f"CB{s}_{i}", [128, 256], F32, dma=True) for i in range(4)] for s in range(NSET)]
    SC = [[P.sb(f"SC{s}_{i}", [64, 128], F32, dma=True) for i in range(4)] for s in range(NSET)]

    PSB = [P.ps(f"psb{i}", [128, 512], F32) for i in range(3)]
    OACC = [P.ps(f"oacc{i}", [128, 512], F32) for i in range(4)]
    PTB = es.enter_context(nc.psum_tensor("ptb", [128, 1024], BF16))
    ps_i = [0]
    pt_i = [0]
    class View:
        def __init__(self, buf, t):
            self.buf = buf
            self.t = t

    PAR = [View(PSB[0], PSB[0].t[:, i * 64:(i + 1) * 64]) for i in range(8)]
    PUR = [View(PSB[1 + i // 2], PSB[1 + i // 2].t[:, (i % 2) * 256:(i % 2 + 1) * 256]) for i in range(4)]
    PTBb = Buf("ptbb", PTB)
    PTR = [View(PTBb, PTB[:, i * 128:(i + 1) * 128]) for i in range(8)]

    def to_regions():
        pass

    def to_banks():
        pass

    def psum(all7=False):
        pool = PSB + OACC if all7 else PSB
        b = pool[ps_i[0] % len(pool)]
        ps_i[0] += 1
        return b

    def ptr():
        b = PTR[pt_i[0] % 8]
        pt_i[0] += 1
        return b

    rot = {"atm": 0, "ktok": 0}

    def act(out, in_, func, reads, writes, **kw):
        return P.op(ACT, lambda: nc.scalar.activation(out=out, in_=in_, func=func, **kw), reads=reads, writes=writes)

    def mm(out, lhsT, rhs, reads, writes, start=True, stop=True):
        return P.op(PE, lambda: nc.tensor.matmul(out, lhsT, rhs, start=start, stop=stop), reads=reads, writes=writes)

    def tr(out, in_, idn, reads, writes):
        return P.op(PE, lambda: nc.tensor.transpose(out, in_, idn), reads=reads, writes=writes)

    def tt(e, out, in0, in1, op, reads, writes):
        return P.op(e, lambda: e.h.tensor_tensor(out=out, in0=in0, in1=in1, op=op), reads=reads, writes=writes)

    def ts(e, out, in0, s1, s2, op0, op1, reads, writes):
        if s2 is None:
            return P.op(e, lambda: e.h.tensor_scalar(out=out, in0=in0, scalar1=s1, scalar2=None, op0=op0),
                        reads=reads, writes=writes)
        return P.op(e, lambda: e.h.tensor_scalar(out=out, in0=in0, scalar1=s1, scalar2=s2, op0=op0, op1=op1),
                    reads=reads, writes=writes)

    def stt(e, out, in0, scalar, in1, op0, op1, reads, writes):
        return P.op(e, lambda: e.h.scalar_tensor_tensor(out=out, in0=in0, scalar=scalar, in1=in1, op0=op0, op1=op1),
                    reads=reads, writes=writes)

    def cp(e, out, in_, reads, writes):
        if e is ACT:
            return act(out, in_, AF.Copy, reads, writes)
        return P.op(e, lambda: e.h.tensor_copy(out=out, in_=in_), reads=reads, writes=writes)

    def scan(out, d0, d1, init, op0, op1, reads, writes):
        return P.op(DVE, lambda: nc.vector.tensor_tensor_scan(out=out, data0=d0, data1=d1, initial=init, op0=op0, op1=op1),
                    reads=reads, writes=writes)

    def ms(e, buf, ap, val):
        return P.op(e, lambda: e.h.memset(ap, val), writes=(buf,))

    ms(DVE, ones_m, ones_m.t[:], 1.0 / 1024.0)
    ms(DVE, ones_h, ones_h.t[:], 1.0 / 128.0)
    ms(DVE, ones64, ones64.t[:], 1.0)
    ms(DVE, eps_c, eps_c.t[:], EPS)
    ms(DVE, one_c, one_c.t[:], 1.0)
    ms(DVE, ones4, ones4.t[:], 1.0)
    ms(DVE, zeros4, zeros4.t[:], 0.0)
    ms(DVE, FCY, FCY.t[:], 0.0)
    ms(DVE, MCY, MCY.t[:], 0.0)
    cp(DVE, identb.t[:], ident.t[:], (ident,), (identb,))
    ts(DVE, maskS.t[:], maskT.t[:], 128.0 ** -0.5, None, ALU.mult, None, (maskT,), (maskS,))
    nrows = 3 * depth + 1
    for kc in range(KC):
        pb = psum()
        tr(pb.t[:, 0:nrows], NROW.t[0:nrows, kc * 128:(kc + 1) * 128], ident.t[0:nrows, 0:nrows], (NROW, ident), (pb,))
        cp(DVE, NW.t[:, kc, 0:nrows], pb.t[:, 0:nrows], (pb,), (NW,))
    for hd in range(4):
        pb = psum()
        tr(pb.t[:, 0:4 * depth], R5ROW.t[0:4 * depth, hd * 128:(hd + 1) * 128], ident.t[0:4 * depth, 0:4 * depth],
           (R5ROW, ident), (pb,))
        cp(DVE, R5.t[:, hd, 0:4 * depth], pb.t[:, 0:4 * depth], (pb,), (R5,))
    for hd in range(4):
        pb = psum()
        tr(pb.t[0:64, 0:depth], R5ROW.t[0:depth, 512 + hd * 64:512 + (hd + 1) * 64], ident.t[0:depth, 0:depth],
           (R5ROW, ident), (pb,))
        ts(DVE, NBA.t[0:64, hd, 0:depth], pb.t[0:64, 0:depth], -1.0, None, ALU.mult, None, (pb,), (NBA,))
    ts(DVE, NGBF.t[:, 0:depth], GB.t[:, :, 1], -1.0, None, ALU.mult, None, (GB,), (NGBF,))
    EX = TMP[0]
    act(EX.t[:, 0:4 * depth].rearrange("p (h l) -> p h l", h=4), R5.t[:, :, 0:depth], AF.Exp, (R5,), (EX,))
    exv = EX.t[:, 0:4 * depth].rearrange("p (h l) -> p h l", h=4)
    P.op(DVE, lambda: nc.vector.reduce_sum(out=EX.t[:, 16:20], in_=exv, axis=AX.X), reads=(EX,), writes=(EX,))
    P.op(DVE, lambda: nc.vector.reciprocal(out=EX.t[:, 16:20], in_=EX.t[:, 16:20]), reads=(EX,), writes=(EX,))
    tt(DVE, exv, exv, EX.t[:, 16:20].unsqueeze(2).broadcast_to([128, 4, depth]), ALU.mult, (EX,), (EX,))
    ms(DVE, LB, LB.t[:], 0.0)
    for l in range(1, depth):
        tt(DVE, LB.t[:, :, l], LB.t[:, :, l - 1], exv[:, :, l], ALU.add, (LB, EX), (LB,))
    ts(DVE, OML.t[:], LB.t[:], -1.0, 1.0, ALU.mult, ALU.add, (LB,), (OML,))

    def rstd_from(pb, n, dst=None):
        dst = dst or RSTD
        act(dst.t[:, 0:n], pb.t[:, 0:n], AF.Ln, (pb, eps_c), (dst,), bias=eps_c.t[:, 0:1])
        act(dst.t[:, 0:n], dst.t[:, 0:n], AF.Exp, (dst,), (dst,), scale=-0.5)

    def x_stats(n):
        for kc in range(KC):
            act(SQ.t[:, kc, 0:n], X.t[:, kc, 0:n], AF.Square, (X,), (SQ,))
        pb = psum()
        for kc in range(KC):
            mm(pb.t[:, 0:n], ones_m.t[:], SQ.t[:, kc, 0:n], (ones_m, SQ), (pb,), start=(kc == 0), stop=(kc == KC - 1))
        rstd_from(pb, n)

    def rmsnorm(nidx, n):
        x_stats(n)
        for kc in range(KC):
            stt(DVE, H.t[:, kc, 0:n], X.t[:, kc, 0:n], NW.t[:, kc, nidx:nidx + 1], RSTD.t[:, 0:n], ALU.mult, ALU.mult,
                (X, NW, RSTD), (H,))

    def ffn(l, f, n):
        PHASES.append(('ffn_norm', P.PE.cnt))
        rmsnorm(3 * l + (0 if f == 1 else 2), n)
        PHASES.append(('ffn_up', P.PE.cnt))
        for i in range(6):
            s, wv = load_block(l, f"up{f}_{i}")
            nch = 4 if i < 5 else 2
            half = nch * 128
            for j in range(nch):
                pg = psum(True)
                pu = psum(True)
                for kc in range(KC):
                    mm(pg.t[:, 0:n], wv[:, kc, j * 128:(j + 1) * 128], H.t[:, kc, 0:n], (s, H), (pg,),
                       start=(kc == 0), stop=(kc == KC - 1))
                for kc in range(KC):
                    mm(pu.t[:, 0:n], wv[:, kc, half + j * 128:half + (j + 1) * 128], H.t[:, kc, 0:n], (s, H), (pu,),
                       start=(kc == 0), stop=(kc == KC - 1))
                sg = SG[(i * 4 + j) % 2]
                act(sg.t[:, 0:n], pg.t[:, 0:n], AF.Silu, (pg,), (sg,))
                ch = i * 4 + j
                tt(DVE, HIDv[:, ch, 0:n], sg.t[:, 0:n], pu.t[:, 0:n], ALU.mult, (sg, pu), (ARENA,))
        PHASES.append(('ffn_down', P.PE.cnt))
        for i in range(4):
            s, wv = load_block(l, f"dn{f}_{i}")
            for j in range(2):
                oc = i * 2 + j
                pb = psum(True)
                for kc in range(FC):
                    mm(pb.t[:, 0:n], wv[:, kc, j * 128:(j + 1) * 128], HIDv[:, kc, 0:n], (s, ARENA), (pb,),
                       start=(kc == 0), stop=(kc == FC - 1))
                stt(DVE, X.t[:, oc, 0:n], pb.t[:, 0:n], 0.5, X.t[:, oc, 0:n], ALU.mult, ALU.add, (pb, X), (X,))

    kio = [0]

    def load_x(src_ap, n):
        for tb in range((n + 127) // 128):
            r = min(128, n - tb * 128)
            xin = XIO[kio[0] % 2]
            kio[0] += 1
            P.dma(POOL, xin.t[0:r, :], src_ap[tb * 128:tb * 128 + r, :], xin, load=True)
            for g in range(2):
                pb = psum()
                for q in range(4):
                    kc = g * 4 + q
                    tr(pb.t[:, q * 128:q * 128 + r], xin.t[0:r, kc * 128:(kc + 1) * 128], ident.t[0:r, 0:r],
                       (xin, ident), (pb,))
                act(X.t[:, g * 4:(g + 1) * 4, tb * 128:tb * 128 + r],
                    pb.t[:, :].rearrange("p (q t) -> p q t", q=4)[:, :, 0:r], AF.Copy, (pb,), (X,))

    def store_y(dst_ap, n):
        x_stats(n)
        fi = 3 * depth
        for kc in range(KC):
            stt(DVE, X.t[:, kc, 0:n], X.t[:, kc, 0:n], NW.t[:, kc, fi:fi + 1], RSTD.t[:, 0:n], ALU.mult, ALU.mult,
                (X, NW, RSTD), (X,))
        for tb in range((n + 127) // 128):
            r = min(128, n - tb * 128)
            yo = XIO[kio[0] % 2]
            kio[0] += 1
            for g in range(2):
                pb = psum()
                for q in range(4):
                    kc = g * 4 + q
                    tr(pb.t[0:r, q * 128:(q + 1) * 128], X.t[:, kc, tb * 128:tb * 128 + r], ident.t[:, :], (X, ident), (pb,))
                act(yo.t[0:r, g * 512:(g + 1) * 512], pb.t[0:r, :], AF.Copy, (pb,), (yo,))
            P.dma(POOL, dst_ap[tb * 128:tb * 128 + r, :], yo.t[0:r, :], yo, load=False)

    def head_norm_scale(src_ap, src_bufs, n, wcol, out_ap, out_buf, gate=None):
        act(SQH.t[:, 0:n], src_ap, AF.Square, src_bufs, (SQH,))
        pm = psum()
        mm(pm.t[:, 0:n], ones_h.t[:], SQH.t[:, 0:n], (ones_h, SQH), (pm,))
        rstd_from(pm, n)
        if gate is None:
            stt(DVE, out_ap, src_ap, wcol, RSTD.t[:, 0:n], ALU.mult, ALU.mult, src_bufs + (R5, RSTD), (out_buf,))
        else:
            t = TMP[2]
            stt(DVE, t.t[:, 0:n], src_ap, wcol, RSTD.t[:, 0:n], ALU.mult, ALU.mult, src_bufs + (R5, RSTD), (t,))
            tt(DVE, out_ap, t.t[:, 0:n], gate.t[:, 0:n], ALU.mult, (t, gate), (out_buf,))

    def proj_v(l, key, n, L, NCH):
        s, wv = load_block(l, key)
        for c in range(NCH):
            pb = psum(True)
            for kc in range(KC):
                mm(pb.t[0:L, 0:512], H.t[:, kc, c * L:(c + 1) * L], wv[:, kc, 0:512], (s, H), (pb,),
                   start=(kc == 0), stop=(kc == KC - 1))
            cp(ACT, V.t[0:L, c, :], pb.t[0:L, 0:512], (pb,), (V,))

    def proj3(s, wv, n, ncol):
        outs = []
        for j in range(ncol):
            pb = psum(True)
            for kc in range(KC):
                mm(pb.t[:, 0:n], wv[:, kc, j * 128:(j + 1) * 128], H.t[:, kc, 0:n], (s, H), (pb,),
                   start=(kc == 0), stop=(kc == KC - 1))
            outs.append(pb)
        return outs

    def mg_fill(wmg, smg, n, ocs):
        for oc in ocs:
            pg = PSB[2]
            for kc in range(KC):
                mm(pg.t[:, 0:n], wmg[:, kc, oc * 128:(oc + 1) * 128], H.t[:, kc, 0:n], (smg, H), (pg,),
                   start=(kc == 0), stop=(kc == KC - 1))
            act(SQ.t[:, oc, 0:n], pg.t[:, 0:n], AF.Sigmoid, (pg,), (SQ,))

    def fill_ocs(c, NCH):
        per = KC // NCH
        return list(range(c * per, (c + 1) * per))

    def merge_branch(l, a, n):
        sbr, wbr = load_block(l, f"br{a}")
        for oc in range(KC):
            pp = psum(True)
            for hd in range(4):
                mm(pp.t[:, 0:n], wbr[:, hd, oc * 128:(oc + 1) * 128], Y[hd].t[:, 0:n], (sbr, Y[hd]), (pp,),
                   start=(hd == 0), stop=(hd == 3))
            if a == 0:
                tt(DVE, MERGEDv[:, oc, 0:n], SQ.t[:, oc, 0:n], pp.t[:, 0:n], ALU.mult, (SQ, pp), (ARENA,))
            else:
                sg = SG[oc % 2]
                tt(DVE, sg.t[:, 0:n], SQ.t[:, oc, 0:n], pp.t[:, 0:n], ALU.mult, (SQ, pp), (sg,))
                tt(DVE, MERGEDv[:, oc, 0:n], MERGEDv[:, oc, 0:n], sg.t[:, 0:n], ALU.add, (ARENA, sg), (ARENA,))

    def mixer(l, n, L, NCH, sample, last):
        PHASES.append(('mix_norm', P.PE.cnt))
        rmsnorm(3 * l + 1, n)
        RM = CMASK.t[:, 512:512 + n] if sample else CMASK.t[:, 0:n]

        def sset(c):
            return c if sample else l

        def chunked(ap):
            return ap.rearrange("p (c t) -> p c t", c=NCH)

        if cfg.stage <= 0:
            return
        PHASES.append(('A_proj', P.PE.cnt))
        proj_v(l, "a_v", n, L, NCH)
        for hd in range(4):
            s, wv = load_block(l, f"a_qfg{hd}")
            pq, pf, pg = proj3(s, wv, n, 3)
            t0, t1, t2, t3 = TMP[0], TMP[1], TMP[2], TMP[3]
            act(t0.t[:, 0:n], pf.t[:, 0:n], AF.Sigmoid, (pf,), (t0,))
            ts(DVE, t0.t[:, 0:n], t0.t[:, 0:n], OML.t[:, hd, l:l + 1], LB.t[:, hd, l:l + 1], ALU.mult, ALU.add,
               (t0, OML, LB), (t0,))
            ts(DVE, t1.t[:, 0:n], t0.t[:, 0:n], -1.0, 1.0, ALU.mult, ALU.add, (t0,), (t1,))
            act(t0.t[:, 0:n], t0.t[:, 0:n], AF.Ln, (t0,), (t0,))
            scan(t2.t[:, 0:n], RM, t0.t[:, 0:n], 0.0, ALU.mult, ALU.add, (CMASK, t0), (t2,))
            act(t0.t[:, 0:n], t2.t[:, 0:n], AF.Exp, (t2,), (t0,))
            act(t3.t[:, 0:n], t2.t[:, 0:n], AF.Exp, (t2,), (t3,), scale=-1.0)
            act(t2.t[:, 0:n], pq.t[:, 0:n], AF.Silu, (pq,), (t2,))
            tt(DVE, QH[hd].t[:, 0:n], t2.t[:, 0:n], t0.t[:, 0:n], ALU.mult, (t2, t0), (QH[hd],))
            tt(DVE, KT[hd].t[:, 0:n], t1.t[:, 0:n], t3.t[:, 0:n], ALU.mult, (t1, t3), (KT[hd],))
            cp(DVE, DL[hd].t[:, 0:NCH], chunked(t0.t[:, 0:n])[:, :, L - 1], (t0,), (DL[hd],))
            act(G[hd].t[:, 0:n], pg.t[:, 0:n], AF.Sigmoid, (pg,), (G[hd],))
        if cfg.stage <= 1:
            return
        PHASES.append(('A_chunks', P.PE.cnt))
        def a_stage1(c):
            cs = slice(c * L, (c + 1) * L)
            pas, pts, ams, kks = [], [], [], []
            for hd in range(4):
                pa = PAR[(c % 2) * 4 + hd]
                mm(pa.t[0:L, 0:L], KT[hd].t[:, cs], QH[hd].t[:, cs], (KT[hd], QH[hd]), (pa.buf,))
                pt = ptr()
                tr(pt.t[0:L, 0:128], KT[hd].t[:, cs], identb.t[:], (KT[hd], identb), (pt.buf,))
                pas.append(pa)
                pts.append(pt)
            for hd in range(4):
                am = ATM[(c % 2) * 4 + hd]
                tt(DVE, am.t[0:L, 0:L], pas[hd].t[0:L, 0:L], maskT.t[0:L, 0:L], ALU.mult, (pas[hd].buf, maskT), (am,))
                kk = KTOK[(c % 2) * 4 + hd]
                cp(ACT, kk.t[0:L, :], pts[hd].t[0:L, 0:128], (pts[hd].buf,), (kk,))
                ams.append(am)
                kks.append(kk)
            return ams, kks

        def a_stage2(c, ams, kks):
            cs = slice(c * L, (c + 1) * L)
            st = sset(c)
            pus = []
            for hd in range(4):
                S = SA[st][hd]
                if c == 0 or sample:
                    cp(ACT, SAb[hd].t[:], S.t[:], (S,), (SAb[hd],))
                mm(OACC[hd].t[:, cs], V.t[0:L, c, hd * 128:(hd + 1) * 128], ams[hd].t[0:L, 0:L], (V, ams[hd]), (OACC[hd],),
                   start=True, stop=False)
                mm(OACC[hd].t[:, cs], SAb[hd].t[:], QH[hd].t[:, cs], (SAb[hd], QH[hd]), (OACC[hd],), start=False, stop=True)
                pu = View(PSB[1], PSB[1].t[:, hd * 128:(hd + 1) * 128])
                mm(pu.t[:, 0:128], kks[hd].t[0:L, :], V.t[0:L, c, hd * 128:(hd + 1) * 128], (kks[hd], V), (pu.buf,))
                pus.append(pu)
            for hd in range(4):
                S = SA[st][hd]
                tt(DVE, S.t[:], S.t[:], pus[hd].t[:, 0:128], ALU.add, (S, pus[hd].buf), (S,))
                if (not sample) and c < NCH - 1:
                    act(SAb[hd].t[:], S.t[:], AF.Copy, (S, DL[hd]), (SAb[hd],), scale=DL[hd].t[:, c:c + 1])
                ts(DVE, S.t[:], S.t[:], DL[hd].t[:, c:c + 1], None, ALU.mult, None, (S, DL[hd]), (S,))
                if sample:
                    P.dma(SP, o_hgrn_s[l, c, hd], S.t[:], S, load=False)
                elif last and c == NCH - 1:
                    P.dma(SP, o_hgrn_p[l, hd], S.t[:], S, load=False)

        smg, wmg = load_block(l, "mg0")
        nxt = a_stage1(0)
        for c in range(NCH):
            cur = nxt
            if c + 1 < NCH:
                nxt = a_stage1(c + 1)
            a_stage2(c, *cur)
            mg_fill(wmg, smg, n, fill_ocs(c, NCH))
        if cfg.stage <= 2:
            return
        PHASES.append(('A_post', P.PE.cnt))
        for hd in range(4):
            t4 = TMP[4]
            tt(DVE, t4.t[:, 0:n], OACC[hd].t[:, 0:n], G[hd].t[:, 0:n], ALU.mult, (OACC[hd], G[hd]), (t4,))
            head_norm_scale(t4.t[:, 0:n], (t4,), n, R5.t[:, hd, depth + l:depth + l + 1], Y[hd].t[:, 0:n], Y[hd])
        PHASES.append(('A_merge', P.PE.cnt))
        merge_branch(l, 0, n)

        if cfg.stage <= 3:
            return
        PHASES.append(('B_gates', P.PE.cnt))
        ssm, wsm = load_block(l, "small")
        pi = psum(True)
        pf = psum(True)
        plr = psum(True)
        for kc in range(KC):
            mm(pi.t[0:4, 0:n], wsm[:, kc, 0:4], H.t[:, kc, 0:n], (ssm, H), (pi,), start=(kc == 0), stop=(kc == KC - 1))
        for kc in range(KC):
            mm(pf.t[0:4, 0:n], wsm[:, kc, 4:8], H.t[:, kc, 0:n], (ssm, H), (pf,), start=(kc == 0), stop=(kc == KC - 1))
        for kc in range(KC):
            mm(plr.t[0:16, 0:n], wsm[:, kc, 8:24], H.t[:, kc, 0:n], (ssm, H), (plr,), start=(kc == 0), stop=(kc == KC - 1))
        cp(ACT, LR.t[0:16, 0:n], plr.t[0:16, 0:n], (plr,), (LR,))
        g0, g1, g2, g3 = TMP[0], TMP[1], TMP[2], TMP[3]
        act(g0.t[0:4, 0:n], pf.t[0:4, 0:n], AF.Exp, (pf, NGBF), (g0,), scale=-1.0, bias=NGBF.t[:, l:l + 1])
        act(g0.t[0:4, 0:n], g0.t[0:4, 0:n], AF.Ln, (g0,), (g0,), bias=one_c.t[0:4, 0:1])
        if sample:
            scan(g1.t[0:4, 0:n], CMASK.t[0:4, 512:512 + n], g0.t[0:4, 0:n], 0.0, ALU.mult, ALU.add, (CMASK, g0), (g1,))
        else:
            scan(g1.t[0:4, 0:n], ones4.t[0:4, 0:n], g0.t[0:4, 0:n], FCY.t[:, l:l + 1], ALU.mult, ALU.add,
                 (ones4, g0, FCY), (g1,))
        stt(DVE, g2.t[0:4, 0:n], pi.t[0:4, 0:n], GB.t[:, l, 0:1], g1.t[0:4, 0:n], ALU.add, ALU.add,
            (pi, GB, g1), (g2,))
        if sample:
            P.dma(SP, M0.t[0:4, 0:NCH], st_m[l].rearrange("s h -> h s"), M0, load=True, allow_slow_non_contiguous=True)
            ga = TMP[4]
            cp(DVE, ga.t[0:4, 0:n], g2.t[0:4, 0:n], (g2,), (ga,))
            a0 = chunked(ga.t[0:4, 0:n])[:, :, 0]
            tt(DVE, a0, a0, M0.t[0:4, 0:NCH], ALU.max, (ga, M0), (ga,))
            scan(g3.t[0:4, 0:n], CMASK.t[0:4, 576:576 + n], ga.t[0:4, 0:n], NEG_BIG, ALU.add, ALU.max, (CMASK, ga), (g3,))
        else:
            scan(g3.t[0:4, 0:n], zeros4.t[0:4, 0:n], g2.t[0:4, 0:n], MCY.t[:, l:l + 1], ALU.add, ALU.max,
                 (zeros4, g2, MCY), (g3,))
        mcv = chunked(g3.t[0:4, 0:n])[:, :, L - 1]
        if sample:
            cp(DVE, MP.t[0:4, 0:NCH], M0.t[0:4, 0:NCH], (M0,), (MP,))
        else:
            cp(DVE, MP.t[0:4, 0:1], MCY.t[:, l:l + 1], (MCY,), (MP,))
            if NCH > 1:
                cp(DVE, MP.t[0:4, 1:NCH], chunked(g3.t[0:4, 0:n])[:, 0:NCH - 1, L - 1], (g3,), (MP,))
        tt(DVE, R4.t[0:4, 0:NCH], MP.t[0:4, 0:NCH], mcv, ALU.subtract, (MP, g3), (R4,))
        act(R4.t[0:4, 0:NCH], R4.t[0:4, 0:NCH], AF.Exp, (R4,), (R4,))
        tt(DVE, MO.t[0:4, 0:NCH], mcv, chunked(g1.t[0:4, 0:n])[:, :, L - 1], ALU.subtract, (g3, g1), (MO,))
        mcb = mcv.unsqueeze(2).broadcast_to([4, NCH, L])
        g4, g5 = TMP[4], TMP[5]
        tt(DVE, chunked(g4.t[0:4, 0:n]), chunked(g2.t[0:4, 0:n]), mcb, ALU.subtract, (g2, g3), (g4,))
        act(g4.t[0:4, 0:n], g4.t[0:4, 0:n], AF.Exp, (g4,), (g4,))
        tt(DVE, chunked(g5.t[0:4, 0:n]), chunked(g1.t[0:4, 0:n]), mcb, ALU.subtract, (g1, g3), (g5,))
        act(g5.t[0:4, 0:n], g5.t[0:4, 0:n], AF.Exp, (g5,), (g5,))
        if not sample:
            cp(DVE, FCY.t[:, l:l + 1], g1.t[0:4, n - 1:n], (g1,), (FCY,))
            cp(DVE, MCY.t[:, l:l + 1], g3.t[0:4, n - 1:n], (g3,), (MCY,))
        if cfg.stage <= 4:
            return
        pp_ = psum()
        for c in range(NCH):
            tr(pp_.t[0:L, c * 4:(c + 1) * 4], g4.t[0:4, c * L:(c + 1) * L], ident.t[0:4, 0:4], (g4, ident), (pp_,))
        cp(ACT, PTK.t[0:L, 0:NCH * 4], pp_.t[0:L, 0:NCH * 4], (pp_,), (PTK,))
        for hd in range(4):
            pr_ = psum()
            mm(pr_.t[:, 0:NCH], SEL.t[0:4, hd * 128:(hd + 1) * 128], R4.t[0:4, 0:NCH], (SEL, R4), (pr_,))
            cp(ACT, RB.t[:, hd, 0:NCH], pr_.t[:, 0:NCH], (pr_,), (RB,))
        if sample:
            P.dma(SP, o_m_s[l].rearrange("s h -> h s"), MO.t[0:4, 0:NCH], MO, load=False, allow_slow_non_contiguous=True)
        elif last:
            P.dma(SP, o_m_p[l].rearrange("(h o) -> h o", o=1), MO.t[0:4, NCH - 1:NCH], MO, load=False)
        if sample:
            P.dma(SP, NROWS.t[0:4 * NCH, :], st_n[l], NROWS, load=True)
            pn = psum()
            tr(pn.t[:, 0:4 * NCH], NROWS.t[0:4 * NCH, :], ident.t[0:4 * NCH, 0:4 * NCH], (NROWS, ident), (pn,))
            cp(ACT, NT.t[:, 0:4 * NCH], pn.t[:, 0:4 * NCH], (pn,), (NT,))
        if cfg.stage <= 5:
            return
        PHASES.append(('B_proj', P.PE.cnt))
        proj_v(l, "b_v", n, L, NCH)
        for hd in range(4):
            s, wv = load_block(l, f"b_qko{hd}")
            pq, pk, po = proj3(s, wv, n, 3)
            cp(ACT, QH[hd].t[:, 0:n], pq.t[:, 0:n], (pq,), (QH[hd],))
            cp(DVE, KT[hd].t[:, 0:n], pk.t[:, 0:n], (pk,), (KT[hd],))
            act(G[hd].t[:, 0:n], po.t[:, 0:n], AF.Sigmoid, (po,), (G[hd],))
        if cfg.stage <= 6:
            return
        PHASES.append(('B_chunks_post', P.PE.cnt))
        for hp in range(2):
            def b_stage1(c):
                cs = slice(c * L, (c + 1) * L)
                pas, pts, ams, kks = [], [], [], []
                for e in range(2):
                    hd = hp * 2 + e
                    pa = PAR[(c % 2) * 4 + e]
                    mm(pa.t[0:L, 0:L], KT[hd].t[:, cs], QH[hd].t[:, cs], (KT[hd], QH[hd]), (pa.buf,))
                    pt = ptr()
                    tr(pt.t[0:L, 0:128], KT[hd].t[:, cs], identb.t[:], (KT[hd], identb), (pt.buf,))
                    pas.append(pa)
                    pts.append(pt)
                for e in range(2):
                    hd = hp * 2 + e
                    pcol = PTK.t[0:L, c * 4 + hd:c * 4 + hd + 1]
                    am = ATM[(c % 2) * 4 + e]
                    stt(DVE, am.t[0:L, 0:L], pas[e].t[0:L, 0:L], pcol, maskS.t[0:L, 0:L], ALU.mult, ALU.mult,
                        (pas[e].buf, PTK, maskS), (am,))
                    kk = KTOK[(c % 2) * 4 + e]
                    ts(DVE, kk.t[0:L, :], pts[e].t[0:L, 0:128], pcol, 128.0 ** -0.5, ALU.mult, ALU.mult, (pts[e].buf, PTK), (kk,))
                    ams.append(am)
                    kks.append(kk)
                return ams, kks

            def b_stage2(c, ams, kks):
                cs = slice(c * L, (c + 1) * L)
                st = sset(c)
                pus = []
                for e in range(2):
                    hd = hp * 2 + e
                    C = CB[st][hd]
                    NUM = OACC[2 * e]
                    DEN = OACC[2 * e + 1]
                    am = ams[e]
                    kk = kks[e]
                    if sample:
                        cp(DVE, C.t[:, 128:256], NT.t[:, c * 4 + hd:c * 4 + hd + 1].broadcast_to([128, 128]), (NT,), (C,))
                    if c == 0 or sample:
                        act(CBb[hd].t[:], C.t[:], AF.Copy, (C, RB), (CBb[hd],), scale=RB.t[:, hd, c:c + 1])
                    mm(NUM.t[:, cs], V.t[0:L, c, hd * 128:(hd + 1) * 128], am.t[0:L, 0:L], (V, am), (NUM,),
                       start=True, stop=False)
                    mm(NUM.t[:, cs], CBb[hd].t[:, 0:128], QH[hd].t[:, cs], (CBb[hd], QH[hd]), (NUM,), start=False, stop=True)
                    mm(DEN.t[:, cs], ones64.t[0:L, :], am.t[0:L, 0:L], (ones64, am), (DEN,), start=True, stop=False)
                    mm(DEN.t[:, cs], CBb[hd].t[:, 128:256], QH[hd].t[:, cs], (CBb[hd], QH[hd]), (DEN,), start=False, stop=True)
                    pu = PUR[e]
                    mm(pu.t[:, 0:128], kk.t[0:L, :], V.t[0:L, c, hd * 128:(hd + 1) * 128], (kk, V), (pu.buf,))
                    mm(pu.t[:, 128:256], kk.t[0:L, :], ones64.t[0:L, :], (kk, ones64), (pu.buf,))
                    pus.append(pu)
                for e in range(2):
                    hd = hp * 2 + e
                    C = CB[st][hd]
                    stt(DVE, C.t[:], C.t[:], RB.t[:, hd, c:c + 1], pus[e].t[:, 0:256], ALU.mult, ALU.add, (C, RB, pus[e].buf), (C,))
                    if (not sample) and c < NCH - 1:
                        act(CBb[hd].t[:], C.t[:], AF.Copy, (C, RB), (CBb[hd],), scale=RB.t[:, hd, c + 1:c + 2])
                    fin = sample or (last and c == NCH - 1)
                    if fin:
                        idx = (c * 4 + hd) if sample else hd
                        cp(DVE, NT.t[:, idx:idx + 1], C.t[:, 128:129], (C,), (NT,))
                        dst = o_c_s[l, c, hd] if sample else o_c_p[l, hd]
                        P.dma(SP, dst, C.t[:, 0:128], C, load=False)

            if hp == 0:
                smg, wmg = load_block(l, "mg1")
            nxt = b_stage1(0)
            for c in range(NCH):
                cur = nxt
                if c + 1 < NCH:
                    nxt = b_stage1(c + 1)
                b_stage2(c, *cur)
                if hp == 0:
                    mg_fill(wmg, smg, n, fill_ocs(c, NCH))
            for e in range(2):
                hd = hp * 2 + e
                NUM = OACC[2 * e]
                DEN = OACC[2 * e + 1]
                pth = psum()
                mm(pth.t[:, 0:n], SEL.t[0:4, hd * 128:(hd + 1) * 128], g5.t[0:4, 0:n], (SEL, g5), (pth,))
                t0, t1 = TMP[0], TMP[1]
                act(t0.t[:, 0:n], DEN.t[:, 0:n], AF.Abs, (DEN,), (t0,))
                tt(DVE, t0.t[:, 0:n], t0.t[:, 0:n], pth.t[:, 0:n], ALU.max, (t0, pth), (t0,))
                P.op(DVE, lambda: nc.vector.reciprocal(out=t0.t[:, 0:n], in_=t0.t[:, 0:n]), reads=(t0,), writes=(t0,))
                tt(DVE, t1.t[:, 0:n], NUM.t[:, 0:n], t0.t[:, 0:n], ALU.mult, (NUM, t0), (t1,))
                head_norm_scale(t1.t[:, 0:n], (t1,), n, R5.t[:, hd, 2 * depth + l:2 * depth + l + 1],
                                Y[hd].t[:, 0:n], Y[hd], gate=G[hd])
        if cfg.stage <= 7:
            return
        if sample or last:
            ncols = 4 * NCH if sample else 4
            pn = psum()
            tr(pn.t[0:ncols, 0:128], NT.t[:, 0:ncols], ident.t[:, :], (NT, ident), (pn,))
            cp(ACT, NROWS.t[0:ncols, :], pn.t[0:ncols, 0:128], (pn,), (NROWS,))
            P.dma(SP, (o_n_s[l] if sample else o_n_p[l]), NROWS.t[0:ncols, :], NROWS, load=False)
        PHASES.append(('B_merge', P.PE.cnt))
        merge_branch(l, 1, n)

        if cfg.stage <= 8:
            return
        PHASES.append(('C_proj', P.PE.cnt))
        proj_v(l, "c_v", n, L, NCH)
        for pr in range(2):
            s, wv = load_block(l, f"c_qk{pr}")
            for e in range(2):
                hd = 2 * pr + e
                pq = psum(True)
                pk = psum(True)
                pz = psum(True)
                for kc in range(KC):
                    mm(pq.t[0:64, 0:n], wv[:, kc, e * 64:(e + 1) * 64], H.t[:, kc, 0:n], (s, H), (pq,),
                       start=(kc == 0), stop=(kc == KC - 1))
                for kc in range(KC):
                    mm(pk.t[0:64, 0:n], wv[:, kc, 128 + e * 64:128 + (e + 1) * 64], H.t[:, kc, 0:n], (s, H), (pk,),
                       start=(kc == 0), stop=(kc == KC - 1))
                mm(pz.t[0:64, 0:n], W2b.t[0:16, l, hd * 64:(hd + 1) * 64], LR.t[0:16, 0:n], (W2b, LR), (pz,))
                t0, t2, t3 = TMP[0], TMP[2], TMP[3]
                act(t0.t[0:64, 0:n], pz.t[0:64, 0:n], AF.Exp, (pz, NBA), (t0,), scale=-1.0, bias=NBA.t[0:64, hd, l:l + 1])
                act(t0.t[0:64, 0:n], t0.t[0:64, 0:n], AF.Ln, (t0,), (t0,), bias=one_c.t[0:64, 0:1])
                scan(t2.t[0:64, 0:n], RM[0:64], t0.t[0:64, 0:n], 0.0, ALU.mult, ALU.add, (CMASK, t0), (t2,))
                act(t0.t[0:64, 0:n], t2.t[0:64, 0:n], AF.Exp, (t2,), (t0,), scale=-1.0 / 16.0)
                act(t3.t[0:64, 0:n], t2.t[0:64, 0:n], AF.Exp, (t2,), (t3,), scale=1.0 / 16.0)
                stt(DVE, QH[hd].t[0:64, 0:n], pq.t[0:64, 0:n], 0.125, t0.t[0:64, 0:n], ALU.mult, ALU.mult,
                    (pq, t0), (QH[hd],))
                tt(DVE, KT[hd].t[0:64, 0:n], pk.t[0:64, 0:n], t3.t[0:64, 0:n], ALU.mult, (pk, t3), (KT[hd],))
                cp(DVE, DL[hd].t[0:64, 0:NCH], chunked(t0.t[0:64, 0:n])[:, :, L - 1], (t0,), (DL[hd],))
        if cfg.stage <= 9:
            return
        s, wv = load_block(l, "c_g")
        for hd in range(4):
            pb = psum(True)
            for kc in range(KC):
                mm(pb.t[:, 0:n], wv[:, kc, hd * 128:(hd + 1) * 128], H.t[:, kc, 0:n], (s, H), (pb,),
                   start=(kc == 0), stop=(kc == KC - 1))
            act(G[hd].t[:, 0:n], pb.t[:, 0:n], AF.Silu, (pb,), (G[hd],))
        if cfg.stage <= 10:
            return
        PHASES.append(('C_chunks', P.PE.cnt))
        def c_stage1(c):
            cs = slice(c * L, (c + 1) * L)
            pas, pts, ams, kks = [], [], [], []
            for hd in range(4):
                pa = PAR[(c % 2) * 4 + hd]
                mm(pa.t[0:L, 0:L], KT[hd].t[0:64, cs], QH[hd].t[0:64, cs], (KT[hd], QH[hd]), (pa.buf,))
                pt = ptr()
                tr(pt.t[0:L, 0:64], KT[hd].t[0:64, cs], identb.t[0:64, 0:64], (KT[hd], identb), (pt.buf,))
                pas.append(pa)
                pts.append(pt)
            for hd in range(4):
                am = ATM[(c % 2) * 4 + hd]
                tt(DVE, am.t[0:L, 0:L], pas[hd].t[0:L, 0:L], maskT.t[0:L, 0:L], ALU.mult, (pas[hd].buf, maskT), (am,))
                kk = KTOK[(c % 2) * 4 + hd]
                cp(ACT, kk.t[0:L, 0:64], pts[hd].t[0:L, 0:64], (pts[hd].buf,), (kk,))
                ams.append(am)
                kks.append(kk)
            return ams, kks

        def c_stage2(c, ams, kks):
            cs = slice(c * L, (c + 1) * L)
            st = sset(c)
            pus = []
            for hd in range(4):
                S = SC[st][hd]
                Sb = SAb[hd]
                if c == 0 or sample:
                    cp(ACT, Sb.t[0:64, :], S.t[:], (S,), (Sb,))
                mm(OACC[hd].t[:, cs], V.t[0:L, c, hd * 128:(hd + 1) * 128], ams[hd].t[0:L, 0:L], (V, ams[hd]), (OACC[hd],),
                   start=True, stop=False)
                mm(OACC[hd].t[:, cs], Sb.t[0:64, :], QH[hd].t[0:64, cs], (Sb, QH[hd]), (OACC[hd],), start=False, stop=True)
                pu = View(PSB[1], PSB[1].t[:, hd * 128:(hd + 1) * 128])
                mm(pu.t[0:64, 0:128], kks[hd].t[0:L, 0:64], V.t[0:L, c, hd * 128:(hd + 1) * 128], (kks[hd], V), (pu.buf,))
                pus.append(pu)
            for hd in range(4):
                S = SC[st][hd]
                Sb = SAb[hd]
                tt(DVE, S.t[:], S.t[:], pus[hd].t[0:64, 0:128], ALU.add, (S, pus[hd].buf), (S,))
                if (not sample) and c < NCH - 1:
                    act(Sb.t[0:64, :], S.t[:], AF.Copy, (S, DL[hd]), (Sb,), scale=DL[hd].t[0:64, c:c + 1])
                ts(DVE, S.t[:], S.t[:], DL[hd].t[0:64, c:c + 1], None, ALU.mult, None, (S, DL[hd]), (S,))
                if sample:
                    P.dma(SP, o_gla_s[l, c, hd], S.t[:], S, load=False)
                elif last and c == NCH - 1:
                    P.dma(SP, o_gla_p[l, hd], S.t[:], S, load=False)

        smg, wmg = load_block(l, "mg2")
        nxt = c_stage1(0)
        for c in range(NCH):
            cur = nxt
            if c + 1 < NCH:
                nxt = c_stage1(c + 1)
            c_stage2(c, *cur)
            mg_fill(wmg, smg, n, fill_ocs(c, NCH))
        if cfg.stage <= 11:
            return
        PHASES.append(('C_post', P.PE.cnt))
        for hd in range(4):
            head_norm_scale(OACC[hd].t[:, 0:n], (OACC[hd],), n, R5.t[:, hd, 3 * depth + l:3 * depth + l + 1],
                            Y[hd].t[:, 0:n], Y[hd], gate=G[hd])
        PHASES.append(('C_merge', P.PE.cnt))
        merge_branch(l, 2, n)

        if cfg.stage <= 12:
            return
        PHASES.append(('out_proj', P.PE.cnt))
        for kc in range(KC):
            cp(ACT if kc % 2 else DVE, H.t[:, kc, 0:n], MERGEDv[:, kc, 0:n], (ARENA,), (H,))
        s, wv = load_block(l, "wout")
        for oc in range(KC):
            pb = psum(True)
            for kc in range(KC):
                mm(pb.t[:, 0:n], wv[:, kc, oc * 128:(oc + 1) * 128], H.t[:, kc, 0:n], (s, H), (pb,),
                   start=(kc == 0), stop=(kc == KC - 1))
            tt(DVE, X.t[:, oc, 0:n], X.t[:, oc, 0:n], pb.t[:, 0:n], ALU.add, (X, pb), (X,))

    def load_sample_states(l):
        for sq in range(NSEQ):
            for hd in range(4):
                P.dma(SP, SA[sq][hd].t[:], st_hgrn[l, sq, hd], SA[sq][hd], load=True)
                P.dma(SP, CB[sq][hd].t[:, 0:128], st_c[l, sq, hd], CB[sq][hd], load=True)
            for hd in range(4):
                P.dma(SP, SC[sq][hd].t[:], st_gla[l, sq, hd], SC[sq][hd], load=True)

    def zero_states():
        for st in range(NSET):
            for hd in range(4):
                ms(POOL, SA[st][hd], SA[st][hd].t[:], 0.0)
                ms(POOL, CB[st][hd], CB[st][hd].t[:], 0.0)
            for hd in range(4):
                ms(POOL, SC[st][hd], SC[st][hd].t[:], 0.0)

    do_mixer = cfg.mixer
    if NS > 0:
        load_x(xs[0:NS, :], NS)
        for l in range(depth):
            ffn(l, 1, NS)
            if do_mixer:
                load_sample_states(l)
                mixer(l, NS, SL, NSEQ, True, False)
            ffn(l, 2, NS)
        store_y(ys[0:NS, :], NS)
    if NP > 0:
        zero_states()
        ntile = NP // T
        for ti in range(ntile):
            load_x(xp[ti * T:(ti + 1) * T, :], T)
            for l in range(depth):
                ffn(l, 1, T)
                if do_mixer:
                    mixer(l, T, 64, T // 64, False, ti == ntile - 1)
                ffn(l, 2, T)
            store_y(yp[ti * T:(ti + 1) * T, :], T)

    for b in XIO + [MO, NROWS] + [x for st in SA + CB + SC for x in st]:
        if b.dcnt:
            P._wait(SP, (b.dsem, b.dcnt))
    for e in (PE, ACT, DVE, POOL):
        if e.cnt:
            P._wait(SP, (e.sem, e.cnt))
    print(f"[build] inst={P.n_inst} waits={P.n_wait} sems={P.nsem} sbuf_left={nc.sbuf_bytes_remaining}")


_NC_CACHE = {}


def kernel(**inputs):
    inp = {k: np.asarray(v) for k, v in inputs.items()}
    B, SEQ, _ = inp["x_prompt"].shape
    NSAMP, SL, _ = inp["x_sample"].shape
    n_cores = 8
    nseq = NSAMP // n_cores
    cfg = Cfg(SEQ, nseq, SL, depth=DEPTH)
    key = (SEQ, nseq, SL)
    if key not in _NC_CACHE:
        _NC_CACHE[key] = build_program(cfg)
    nc = _NC_CACHE[key]
    in_maps = [core_inputs(inp, cfg, c % B, c, list(range(c * nseq, (c + 1) * nseq))) for c in range(n_cores)]
    res = run_bass_kernel_spmd(nc, in_maps, core_ids=list(range(n_cores)))
    R = res.results
    f32 = np.float32
    y_prompt = np.stack([R[b]["y_prompt"] for b in range(B)]).astype(f32)
    y_sample = np.concatenate([R[c]["y_sample"].reshape(nseq, SL, D_MODEL) for c in range(n_cores)], 0).astype(f32)

    def pstate(name, shape):
        return np.stack([R[b][name].reshape(shape) for b in range(B)], 1).astype(f32)

    def sstate(name, shape):
        return np.concatenate([R[c][name].reshape((DEPTH, nseq) + shape) for c in range(n_cores)], 1).astype(f32)

    return (
        y_prompt, y_sample,
        pstate("hgrn_p", (DEPTH, 4, 128, 128)), pstate("c_p", (DEPTH, 4, 128, 128)), pstate("n_p", (DEPTH, 4, 128)),
        pstate("m_p", (DEPTH, 4)), pstate("gla_p", (DEPTH, 4, 64, 128)),
        sstate("hgrn_s", (4, 128, 128)), sstate("c_s", (4, 128, 128)), sstate("n_s", (4, 128)),
        sstate("m_s", (4,)), sstate("gla_s", (4, 64, 128)),
    )


def core_inputs(inp, cfg, b, core, seqs):
    depth = cfg.depth
    f = np.ascontiguousarray
    NP, NSEQ, SL = cfg.n_prompt, cfg.n_seq, cfg.seq_len
    nsq = max(NSEQ, 1)
    sq = list(seqs) if NSEQ > 0 else [0]
    m = {
        "x_prompt": f(inp["x_prompt"][b, :max(NP, 1)]),
        "x_sample": f(inp["x_sample"][sq].reshape(nsq * SL, D_MODEL)[:max(NSEQ * SL, 1)]),
        "st_hgrn": f(inp["state_hgrn"][:depth, sq]),
        "st_c": f(inp["state_mlstm_c"][:depth, sq]),
        "st_n": f(inp["state_mlstm_n"][:depth, sq].reshape(depth, nsq * 4, 128)),
        "st_m": f(inp["state_mlstm_m"][:depth, sq]),
        "st_gla": f(inp["state_gla"][:depth, sq]),
        "norms": f(np.concatenate([np.stack([inp["ffn1_norm"][l], inp["mix_norm"][l], inp["ffn2_norm"][l]])
                                   for l in range(depth)] + [inp["final_norm"][None]], 0)),
        "rows512": f(np.concatenate([inp["hgrn_lb_logits"][:depth], inp["hgrn_norm"][:depth],
                                     inp["mlstm_norm"][:depth], inp["gla_norm"][:depth]], 0)),
        "gla_b_a": f(inp["gla_b_a"][:depth]),
        "mlstm_gate_bias": f(inp["mlstm_gate_bias"][:depth]),
        "gla_w_a2": f(inp["gla_w_a2"][:depth]),
    }
    for k in ("ffn1_w_up", "ffn1_w_down", "ffn2_w_up", "ffn2_w_down", "w_in", "w_branch", "w_out"):
        m[k] = f(inp[k][:depth])
    m.update(host_consts())
    return m
```

```python
import math
from contextlib import ExitStack

import numpy as np
import concourse.bass as bass
import concourse.mybir as mybir
from concourse.bass_utils import run_bass_kernel_spmd

F32 = mybir.dt.float32
BF16 = mybir.dt.bfloat16
AF = mybir.ActivationFunctionType
ALU = mybir.AluOpType
AX = mybir.AxisListType

D_MODEL = 1024
DEPTH = 4
D_FF = 2816
D_IN = 8728
EPS = 1e-6
KC = 8
FC = 22
NEG_BIG = -1.0e30


PHASES = []


class Buf:
    def __init__(self, name, t, dsem=None):
        self.name = name
        self.t = t
        self.w = None
        self.r = {}
        self.dsem = dsem
        self.dcnt = 0

    def __getitem__(self, k):
        return self.t[k]


class Eng:
    def __init__(self, name, h, sem):
        self.name = name
        self.h = h
        self.sem = sem
        self.cnt = 0
        self.seen = {}


class Prog:
    def __init__(self, nc, es):
        self.nc = nc
        self.es = es
        self.nsem = 0
        self.PE = self._eng("pe", nc.tensor)
        self.ACT = self._eng("act", nc.scalar)
        self.DVE = self._eng("dve", nc.vector)
        self.POOL = self._eng("pool", nc.gpsimd)
        self.SP = self._eng("sp", nc.sync)
        self.engs = [self.PE, self.ACT, self.DVE, self.POOL, self.SP]
        self.n_inst = 0
        self.n_wait = 0

    def sem(self, name):
        self.nsem += 1
        return self.es.enter_context(self.nc.semaphore(name))

    def _eng(self, name, h):
        return Eng(name, h, self.sem("e_" + name))

    def sb(self, name, shape, dtype, dma=False):
        t = self.es.enter_context(self.nc.sbuf_tensor(name, list(shape), dtype))
        return Buf(name, t, self.sem("d_" + name) if dma else None)

    def ps(self, name, shape, dtype=F32):
        t = self.es.enter_context(self.nc.psum_tensor(name, list(shape), dtype))
        return Buf(name, t)

    def _need(self, e, tok, acc):
        sem, val = tok
        if e is self.PE and sem is self.PE.sem:
            return
        if e.seen.get(sem, 0) >= val:
            return
        e.seen[sem] = val
        acc[sem] = max(acc.get(sem, 0), val)

    def _wait(self, e, tok):
        acc = {}
        self._need(e, tok, acc)
        for sem, val in acc.items():
            e.h.wait_ge(sem, val)
            self.n_wait += 1

    def _deps(self, e, reads, writes):
        acc = {}
        for b in reads:
            if b.w is not None:
                self._need(e, b.w, acc)
        for b in writes:
            if b.w is not None:
                self._need(e, b.w, acc)
            for sem, val in b.r.items():
                self._need(e, (sem, val), acc)
        return list(acc.items())

    def _emit(self, e, fn, waits):
        for sem, val in waits[:-1]:
            e.h.wait_ge(sem, val)
            self.n_wait += 1
        inst = fn()
        if waits:
            sem, val = waits[-1]
            inst._wait_ge(sem, val)
        return inst

    def op(self, e, fn, reads=(), writes=()):
        waits = self._deps(e, reads, writes)
        inst = self._emit(e, fn, waits)
        e.cnt += 1
        inst.then_inc(e.sem, 1)
        tok = (e.sem, e.cnt)
        for b in writes:
            b.w = tok
            b.r = {}
        for b in reads:
            if b not in writes:
                b.r[e.sem] = e.cnt
        self.n_inst += 1
        return inst

    def dma(self, q, out, in_, buf, load, **kw):
        if load:
            waits = self._deps(q, (), (buf,))
        else:
            waits = self._deps(q, (buf,), ())
        inst = self._emit(q, lambda: q.h.dma_start(out=out, in_=in_, **kw), waits)
        inst.then_inc(buf.dsem, 16)
        buf.dcnt += 16
        tok = (buf.dsem, buf.dcnt)
        if load:
            buf.w = tok
            buf.r = {}
        else:
            buf.r[buf.dsem] = buf.dcnt
        self.n_inst += 1
        return inst


class Cfg:
    def __init__(self, n_prompt, n_seq, seq_len, depth=DEPTH, T=512, mixer=True, stage=99):
        self.mixer = mixer
        self.stage = stage
        self.n_prompt = n_prompt
        self.n_seq = n_seq
        self.seq_len = seq_len
        self.depth = depth
        self.T = T


def weight_blocks():
    blks = []
    for f in (1, 2):
        for i in range(5):
            blks.append((f"up{f}_{i}", f"ffn{f}_w_up", 1024, [(512 * i, 512), (2816 + 512 * i, 512)]))
        blks.append((f"up{f}_5", f"ffn{f}_w_up", 1024, [(2560, 256), (5376, 256)]))
        for i in range(4):
            blks.append((f"dn{f}_{i}", f"ffn{f}_w_down", 2816, [(256 * i, 256)]))
    for hd in range(4):
        blks.append((f"a_qfg{hd}", "w_in", 1024,
                     [(hd * 128, 128), (512 + hd * 128, 128), (1536 + hd * 128, 128)]))
    blks.append(("a_v", "w_in", 1024, [(1024, 512)]))
    for hd in range(4):
        blks.append((f"b_qko{hd}", "w_in", 1024,
                     [(2048 + hd * 128, 128), (2560 + hd * 128, 128), (3584 + hd * 128, 128)]))
    blks.append(("b_v", "w_in", 1024, [(3072, 512)]))
    blks.append(("small", "w_in", 1024, [(4096, 8), (5640, 16)]))
    for pr in range(2):
        blks.append((f"c_qk{pr}", "w_in", 1024, [(4104 + pr * 128, 128), (4360 + pr * 128, 128)]))
    blks.append(("c_v", "w_in", 1024, [(4616, 512)]))
    blks.append(("c_g", "w_in", 1024, [(5128, 512)]))
    for a in range(3):
        blks.append((f"mg{a}", "w_in", 1024, [(5656 + a * 1024, 1024)]))
    for a in range(3):
        blks.append((f"br{a}", "w_branch", 512, [(0, 1024)]))
    blks.append(("wout", "w_out", 1024, [(0, 1024)]))
    return blks


def build_program(cfg):
    nc = bass.Bass("TRN2", target_bir_lowering=False)
    es = ExitStack()
    with es:
        _build(nc, es, cfg)
    return nc


def host_consts():
    ident = np.eye(128, dtype=np.float32)
    maskT = np.triu(np.ones((64, 64), dtype=np.float32))
    cm = np.ones((128, 512 + 64 + 64), dtype=np.float32)
    cm[:, 0:512:64] = 0.0
    cm[:, 512:576:16] = 0.0
    cm[:, 576:640] = 0.0
    cm[:, 576:640:16] = NEG_BIG
    sel = np.zeros((4, 512), dtype=np.float32)
    for h in range(4):
        sel[h, h * 128:(h + 1) * 128] = 1.0
    return {"ident_in": ident, "maskT_in": maskT, "cmask_in": cm, "sel_in": sel}


def _build(nc, es, cfg):
    P = Prog(nc, es)
    T = cfg.T
    depth = cfg.depth
    NP = cfg.n_prompt
    NSEQ = cfg.n_seq
    SL = cfg.seq_len
    NS = NSEQ * SL
    NSET = max(depth, NSEQ, 1)
    PE, ACT, DVE, POOL, SP = P.PE, P.ACT, P.DVE, P.POOL, P.SP

    def din(name, shape):
        return nc.dram_tensor(name, list(shape), F32, kind="ExternalInput").ap()

    def dout(name, shape):
        return nc.dram_tensor(name, list(shape), F32, kind="ExternalOutput").ap()

    xp = din("x_prompt", [max(NP, 1), D_MODEL])
    yp = dout("y_prompt", [max(NP, 1), D_MODEL])
    xs = din("x_sample", [max(NS, 1), D_MODEL])
    ys = dout("y_sample", [max(NS, 1), D_MODEL])
    nsq = max(NSEQ, 1)
    st_hgrn = din("st_hgrn", [depth, nsq, 4, 128, 128])
    st_c = din("st_c", [depth, nsq, 4, 128, 128])
    st_n = din("st_n", [depth, nsq * 4, 128])
    st_m = din("st_m", [depth, nsq, 4])
    st_gla = din("st_gla", [depth, nsq, 4, 64, 128])
    o_hgrn_p = dout("hgrn_p", [depth, 4, 128, 128])
    o_c_p = dout("c_p", [depth, 4, 128, 128])
    o_n_p = dout("n_p", [depth, 4, 128])
    o_m_p = dout("m_p", [depth, 4])
    o_gla_p = dout("gla_p", [depth, 4, 64, 128])
    o_hgrn_s = dout("hgrn_s", [depth, nsq, 4, 128, 128])
    o_c_s = dout("c_s", [depth, nsq, 4, 128, 128])
    o_n_s = dout("n_s", [depth, nsq * 4, 128])
    o_m_s = dout("m_s", [depth, nsq, 4])
    o_gla_s = dout("gla_s", [depth, nsq, 4, 64, 128])
    w_dram = {
        "ffn1_w_up": din("ffn1_w_up", [depth, 1024, 2 * D_FF]),
        "ffn1_w_down": din("ffn1_w_down", [depth, D_FF, 1024]),
        "ffn2_w_up": din("ffn2_w_up", [depth, 1024, 2 * D_FF]),
        "ffn2_w_down": din("ffn2_w_down", [depth, D_FF, 1024]),
        "w_in": din("w_in", [depth, 1024, D_IN]),
        "w_branch": din("w_branch", [depth, 3, 512, 1024]),
        "w_out": din("w_out", [depth, 1024, 1024]),
    }
    norms = din("norms", [3 * depth + 1, 1024])
    rows512 = din("rows512", [4 * depth, 512])
    gla_b_a = din("gla_b_a", [depth, 256])
    gate_bias = din("mlstm_gate_bias", [depth, 8])
    gla_w2 = din("gla_w_a2", [depth, 16, 256])
    identd = din("ident_in", [128, 128])
    maskd = din("maskT_in", [64, 64])
    cmaskd = din("cmask_in", [128, 640])
    seld = din("sel_in", [4, 512])

    blks = weight_blocks()
    scratch = {}
    for l in range(depth):
        for key, src, K, cols in blks:
            kc = K // 128
            W = sum(n for _, n in cols)
            scratch[(l, key)] = (nc.dram_tensor(f"ws_{l}_{key}", [128, kc * W], BF16, kind="Internal").ap(), kc, W)

    cast_sem = P.sem("cast")
    n_cast = 0
    for l in range(depth):
        for key, src, K, cols in blks:
            sap, kc, W = scratch[(l, key)]
            sview = sap.rearrange("p (k w) -> p k w", k=kc)
            off = 0
            for c0, n in cols:
                if src == "w_branch":
                    a = int(key[2])
                    srcap = w_dram[src][l, a, :, c0:c0 + n]
                else:
                    srcap = w_dram[src][l, :, c0:c0 + n]
                srcap = srcap.rearrange("(k p) n -> p k n", p=128)
                POOL.h.dma_start(out=sview[:, :, off:off + n], in_=srcap).then_inc(cast_sem, 16)
                n_cast += 1
                off += n
    cast_tok = (cast_sem, 16 * n_cast)

    NSLOT = 3
    SLOTW = 8192
    slots = [P.sb(f"wslot{i}", [128, SLOTW], BF16, dma=True) for i in range(NSLOT)]
    slot_i = [0]

    def load_block(l, key):
        sap, kc, W = scratch[(l, key)]
        s = slots[slot_i[0] % NSLOT]
        slot_i[0] += 1
        P._wait(SP, cast_tok)
        P.dma(SP, s.t[:, 0:kc * W], sap[:, :], s, load=True)
        return s, s.t[:, 0:kc * W].rearrange("p (k w) -> p k w", k=kc)

    X = P.sb("X", [128, KC, T], F32)
    H = P.sb("H", [128, KC, T], BF16)
    HB = [Buf(f"H{k}", None) for k in range(KC)]
    ARENA = P.sb("ARENA", [128, FC * T // 2], F32)
    HIDv = ARENA.t[:, :].bitcast(BF16).rearrange("p (c t) -> p c t", c=FC)
    MERGEDv = ARENA.t[:, 0:KC * T].rearrange("p (c t) -> p c t", c=KC)
    SQ = P.sb("SQ", [128, KC, T], BF16)
    RSTD = P.sb("RSTD", [128, T], F32)
    SG = [P.sb(f"SG{i}", [128, T], F32) for i in range(2)]
    XIO = [P.sb(f"XIO{i}", [128, D_MODEL], F32, dma=True) for i in range(2)]
    csem = P.sem("consts")
    cbufs = []

    def cload(name, shape, dtype, src, q=None, **kw):
        b = P.sb(name, shape, dtype)
        b.dsem = csem
        cbufs.append(b)
        (q or SP).h.dma_start(out=b.t[:], in_=src, **kw).then_inc(csem, 16)
        return b

    ident = cload("ident", [128, 128], F32, identd[:, :])
    NROW = XIO[0]
    R5ROW = XIO[1]
    P.dma(POOL, XIO[0].t[0:3 * depth + 1, :], norms[:, :], XIO[0], load=True)
    P.dma(POOL, XIO[1].t[0:4 * depth, 0:512], rows512[:, :], XIO[1], load=True)
    P.dma(POOL, XIO[1].t[0:depth, 512:768], gla_b_a[:, :], XIO[1], load=True)
    maskT = cload("maskT", [64, 64], F32, maskd[:, :])
    CMASK = cload("CMASK", [128, 640], F32, cmaskd[:, :])
    SEL = cload("SEL", [4, 512], F32, seld[:, :])
    W2b = P.sb("W2b", [16, depth, 256], BF16, dma=True)
    P.dma(POOL, W2b.t[:], gla_w2.rearrange("l r c -> r l c"), W2b, load=True)
    GB = cload("GB", [4, depth, 2], F32, gate_bias.rearrange("l (g h) -> h l g", g=2), allow_slow_non_contiguous=True)
    for b in cbufs:
        b.w = (csem, 16 * len(cbufs))

    eps_c = P.sb("eps_c", [128, 1], F32)
    one_c = P.sb("one_c", [128, 1], F32)
    ones_m = P.sb("ones_m", [128, 128], BF16)
    ones_h = P.sb("ones_h", [128, 128], BF16)
    ones64 = P.sb("ones64", [64, 128], BF16)
    identb = P.sb("identb", [128, 128], BF16)
    maskS = P.sb("maskS", [64, 64], F32)
    NW = P.sb("NW", [128, KC, 16], F32)
    R5 = P.sb("R5", [128, 4, 16], F32)
    LB = P.sb("LB", [128, 4, 4], F32)
    OML = P.sb("OML", [128, 4, 4], F32)
    NBA = P.sb("NBA", [64, 4, 4], F32)
    NGBF = P.sb("NGBF", [4, 4], F32)
    ones4 = P.sb("ones4", [4, 512], F32)
    zeros4 = P.sb("zeros4", [4, 512], F32)

    QH = [P.sb(f"QH{i}", [128, T], BF16) for i in range(4)]
    KT = [P.sb(f"KT{i}", [128, T], BF16) for i in range(4)]
    G = [P.sb(f"G{i}", [128, T], BF16) for i in range(4)]
    Y = [P.sb(f"Y{i}", [128, T], BF16) for i in range(4)]
    V = P.sb("V", [64, 8, 512], BF16)
    TMP = [P.sb(f"TMP{i}", [128, T], F32) for i in range(6)]
    DL = [P.sb(f"DL{i}", [128, 8], F32) for i in range(4)]
    ATM = [P.sb(f"ATM{i}", [64, 64], BF16) for i in range(8)]
    KTOK = [P.sb(f"KTOK{i}", [64, 128], BF16) for i in range(8)]
    SQH = P.sb("SQH", [128, T], BF16)
    LR = P.sb("LR", [16, T], BF16)
    PTK = P.sb("PTK", [64, 32], F32)
    RB = P.sb("RB", [128, 4, 8], F32)
    R4 = P.sb("R4", [4, 8], F32)
    MP = P.sb("MP", [4, 8], F32)
    MO = P.sb("MO", [4, 8], F32, dma=True)
    M0 = P.sb("M0", [4, 8], F32, dma=True)
    NT = P.sb("NT", [128, 16], F32)
    NROWS = P.sb("NROWS", [16, 128], F32, dma=True)
    FCY = P.sb("FCY", [4, NSET], F32)
    MCY = P.sb("MCY", [4, NSET], F32)
    SAb = [P.sb(f"SAb{i}", [128, 128], BF16) for i in range(4)]
    CBb = [P.sb(f"CBb{i}", [128, 256], BF16) for i in range(4)]
    SA = [[P.sb(f"SA{s}_{i}", [128, 128], F32, dma=True) for i in range(4)] for s in range(NSET)]
    CB = [[P.sb(f"CB{s}_{i}", [128, 256], F32, dma=True) for i in range(4)] for s in range(NSET)]
    SC = [[P.sb(f"SC{s}_{i}", [64, 128], F32, dma=True) for i in range(4)] for s in range(NSET)]

    PSB = [P.ps(f"psb{i}", [128, 512], F32) for i in range(3)]
    OACC = [P.ps(f"oacc{i}", [128, 512], F32) for i in range(4)]
    PTB = es.enter_context(nc.psum_tensor("ptb", [128, 1024], BF16))
    ps_i = [0]
    pt_i = [0]
    class View:
        def __init__(self, buf, t):
            self.buf = buf
            self.t = t

    PAR = [View(PSB[0], PSB[0].t[:, i * 64:(i + 1) * 64]) for i in range(8)]
    PUR = [View(PSB[1 + i // 2], PSB[1 + i // 2].t[:, (i % 2) * 256:(i % 2 + 1) * 256]) for i in range(4)]
    PTBb = Buf("ptbb", PTB)
    PTR = [View(PTBb, PTB[:, i * 128:(i + 1) * 128]) for i in range(8)]

    def to_regions():
        pass

    def to_banks():
        pass

    def psum(all7=False):
        pool = PSB + OACC if all7 else PSB
        b = pool[ps_i[0] % len(pool)]
        ps_i[0] += 1
        return b

    def ptr():
        b = PTR[pt_i[0] % 8]
        pt_i[0] += 1
        return b

    rot = {"atm": 0, "ktok": 0}

    def act(out, in_, func, reads, writes, **kw):
        return P.op(ACT, lambda: nc.scalar.activation(out=out, in_=in_, func=func, **kw), reads=reads, writes=writes)

    def mm(out, lhsT, rhs, reads, writes, start=True, stop=True):
        return P.op(PE, lambda: nc.tensor.matmul(out, lhsT, rhs, start=start, stop=stop), reads=reads, writes=writes)

    def tr(out, in_, idn, reads, writes):
        return P.op(PE, lambda: nc.tensor.transpose(out, in_, idn), reads=reads, writes=writes)

    def tt(e, out, in0, in1, op, reads, writes):
        return P.op(e, lambda: e.h.tensor_tensor(out=out, in0=in0, in1=in1, op=op), reads=reads, writes=writes)

    def ts(e, out, in0, s1, s2, op0, op1, reads, writes):
        if s2 is None:
            return P.op(e, lambda: e.h.tensor_scalar(out=out, in0=in0, scalar1=s1, scalar2=None, op0=op0),
                        reads=reads, writes=writes)
        return P.op(e, lambda: e.h.tensor_scalar(out=out, in0=in0, scalar1=s1, scalar2=s2, op0=op0, op1=op1),
                    reads=reads, writes=writes)

    def stt(e, out, in0, scalar, in1, op0, op1, reads, writes):
        return P.op(e, lambda: e.h.scalar_tensor_tensor(out=out, in0=in0, scalar=scalar, in1=in1, op0=op0, op1=op1),
                    reads=reads, writes=writes)

    def cp(e, out, in_, reads, writes):
        if e is ACT:
            return act(out, in_, AF.Copy, reads, writes)
        return P.op(e, lambda: e.h.tensor_copy(out=out, in_=in_), reads=reads, writes=writes)

    def scan(out, d0, d1, init, op0, op1, reads, writes):
        return P.op(DVE, lambda: nc.vector.tensor_tensor_scan(out=out, data0=d0, data1=d1, initial=init, op0=op0, op1=op1),
                    reads=reads, writes=writes)

    def ms(e, buf, ap, val):
        return P.op(e, lambda: e.h.memset(ap, val), writes=(buf,))

    ms(DVE, ones_m, ones_m.t[:], 1.0 / 1024.0)
    ms(DVE, ones_h, ones_h.t[:], 1.0 / 128.0)
    ms(DVE, ones64, ones64.t[:], 1.0)
    ms(DVE, eps_c, eps_c.t[:], EPS)
    ms(DVE, one_c, one_c.t[:], 1.0)
    ms(DVE, ones4, ones4.t[:], 1.0)
    ms(DVE, zeros4, zeros4.t[:], 0.0)
    ms(DVE, FCY, FCY.t[:], 0.0)
    ms(DVE, MCY, MCY.t[:], 0.0)
    cp(DVE, identb.t[:], ident.t[:], (ident,), (identb,))
    ts(DVE, maskS.t[:], maskT.t[:], 128.0 ** -0.5, None, ALU.mult, None, (maskT,), (maskS,))
    nrows = 3 * depth + 1
    for kc in range(KC):
        pb = psum()
        tr(pb.t[:, 0:nrows], NROW.t[0:nrows, kc * 128:(kc + 1) * 128], ident.t[0:nrows, 0:nrows], (NROW, ident), (pb,))
        cp(DVE, NW.t[:, kc, 0:nrows], pb.t[:, 0:nrows], (pb,), (NW,))
    for hd in range(4):
        pb = psum()
        tr(pb.t[:, 0:4 * depth], R5ROW.t[0:4 * depth, hd * 128:(hd + 1) * 128], ident.t[0:4 * depth, 0:4 * depth],
           (R5ROW, ident), (pb,))
        cp(DVE, R5.t[:, hd, 0:4 * depth], pb.t[:, 0:4 * depth], (pb,), (R5,))
    for hd in range(4):
        pb = psum()
        tr(pb.t[0:64, 0:depth], R5ROW.t[0:depth, 512 + hd * 64:512 + (hd + 1) * 64], ident.t[0:depth, 0:depth],
           (R5ROW, ident), (pb,))
        ts(DVE, NBA.t[0:64, hd, 0:depth], pb.t[0:64, 0:depth], -1.0, None, ALU.mult, None, (pb,), (NBA,))
    ts(DVE, NGBF.t[:, 0:depth], GB.t[:, :, 1], -1.0, None, ALU.mult, None, (GB,), (NGBF,))
    EX = TMP[0]
    act(EX.t[:, 0:4 * depth].rearrange("p (h l) -> p h l", h=4), R5.t[:, :, 0:depth], AF.Exp, (R5,), (EX,))
    exv = EX.t[:, 0:4 * depth].rearrange("p (h l) -> p h l", h=4)
    P.op(DVE, lambda: nc.vector.reduce_sum(out=EX.t[:, 16:20], in_=exv, axis=AX.X), reads=(EX,), writes=(EX,))
    P.op(DVE, lambda: nc.vector.reciprocal(out=EX.t[:, 16:20], in_=EX.t[:, 16:20]), reads=(EX,), writes=(EX,))
    tt(DVE, exv, exv, EX.t[:, 16:20].unsqueeze(2).broadcast_to([128, 4, depth]), ALU.mult, (EX,), (EX,))
    ms(DVE, LB, LB.t[:], 0.0)
    for l in range(1, depth):
        tt(DVE, LB.t[:, :, l], LB.t[:, :, l - 1], exv[:, :, l], ALU.add, (LB, EX), (LB,))
    ts(DVE, OML.t[:], LB.t[:], -1.0, 1.0, ALU.mult, ALU.add, (LB,), (OML,))

    def rstd_from(pb, n, dst=None):
        dst = dst or RSTD
        act(dst.t[:, 0:n], pb.t[:, 0:n], AF.Ln, (pb, eps_c), (dst,), bias=eps_c.t[:, 0:1])
        act(dst.t[:, 0:n], dst.t[:, 0:n], AF.Exp, (dst,), (dst,), scale=-0.5)

    def x_stats(n):
        for kc in range(KC):
            act(SQ.t[:, kc, 0:n], X.t[:, kc, 0:n], AF.Square, (X,), (SQ,))
        pb = psum()
        for kc in range(KC):
            mm(pb.t[:, 0:n], ones_m.t[:], SQ.t[:, kc, 0:n], (ones_m, SQ), (pb,), start=(kc == 0), stop=(kc == KC - 1))
        rstd_from(pb, n)

    def rmsnorm(nidx, n):
        x_stats(n)
        for kc in range(KC):
            stt(DVE, H.t[:, kc, 0:n], X.t[:, kc, 0:n], NW.t[:, kc, nidx:nidx + 1], RSTD.t[:, 0:n], ALU.mult, ALU.mult,
                (X, NW, RSTD), (HB[kc],))

    def ffn(l, f, n):
        PHASES.append(('ffn_norm', P.PE.cnt))
        rmsnorm(3 * l + (0 if f == 1 else 2), n)
        PHASES.append(('ffn_up', P.PE.cnt))
        for i in range(6):
            s, wv = load_block(l, f"up{f}_{i}")
            nch = 4 if i < 5 else 2
            half = nch * 128
            for j in range(nch):
                pg = psum(True)
                pu = psum(True)
                for kc in range(KC):
                    mm(pg.t[:, 0:n], wv[:, kc, j * 128:(j + 1) * 128], H.t[:, kc, 0:n], (s, HB[kc]), (pg,),
                       start=(kc == 0), stop=(kc == KC - 1))
                for kc in range(KC):
                    mm(pu.t[:, 0:n], wv[:, kc, half + j * 128:half + (j + 1) * 128], H.t[:, kc, 0:n], (s, HB[kc]), (pu,),
                       start=(kc == 0), stop=(kc == KC - 1))
                sg = SG[(i * 4 + j) % 2]
                act(sg.t[:, 0:n], pg.t[:, 0:n], AF.Silu, (pg,), (sg,))
                ch = i * 4 + j
                tt(DVE, HIDv[:, ch, 0:n], sg.t[:, 0:n], pu.t[:, 0:n], ALU.mult, (sg, pu), (ARENA,))
        PHASES.append(('ffn_down', P.PE.cnt))
        for i in range(4):
            s, wv = load_block(l, f"dn{f}_{i}")
            for j in range(2):
                oc = i * 2 + j
                pb = psum(True)
                for kc in range(FC):
                    mm(pb.t[:, 0:n], wv[:, kc, j * 128:(j + 1) * 128], HIDv[:, kc, 0:n], (s, ARENA), (pb,),
                       start=(kc == 0), stop=(kc == FC - 1))
                stt(DVE, X.t[:, oc, 0:n], pb.t[:, 0:n], 0.5, X.t[:, oc, 0:n], ALU.mult, ALU.add, (pb, X), (X,))

    kio = [0]

    def load_x(src_ap, n):
        for tb in range((n + 127) // 128):
            r = min(128, n - tb * 128)
            xin = XIO[kio[0] % 2]
            kio[0] += 1
            P.dma(POOL, xin.t[0:r, :], src_ap[tb * 128:tb * 128 + r, :], xin, load=True)
            for g in range(2):
                pb = psum()
                for q in range(4):
                    kc = g * 4 + q
                    tr(pb.t[:, q * 128:q * 128 + r], xin.t[0:r, kc * 128:(kc + 1) * 128], ident.t[0:r, 0:r],
                       (xin, ident), (pb,))
                act(X.t[:, g * 4:(g + 1) * 4, tb * 128:tb * 128 + r],
                    pb.t[:, :].rearrange("p (q t) -> p q t", q=4)[:, :, 0:r], AF.Copy, (pb,), (X,))

    def store_y(dst_ap, n):
        x_stats(n)
        fi = 3 * depth
        for kc in range(KC):
            stt(DVE, X.t[:, kc, 0:n], X.t[:, kc, 0:n], NW.t[:, kc, fi:fi + 1], RSTD.t[:, 0:n], ALU.mult, ALU.mult,
                (X, NW, RSTD), (X,))
        for tb in range((n + 127) // 128):
            r = min(128, n - tb * 128)
            yo = XIO[kio[0] % 2]
            kio[0] += 1
            for g in range(2):
                pb = psum()
                for q in range(4):
                    kc = g * 4 + q
                    tr(pb.t[0:r, q * 128:(q + 1) * 128], X.t[:, kc, tb * 128:tb * 128 + r], ident.t[:, :], (X, ident), (pb,))
                act(yo.t[0:r, g * 512:(g + 1) * 512], pb.t[0:r, :], AF.Copy, (pb,), (yo,))
            P.dma(POOL, dst_ap[tb * 128:tb * 128 + r, :], yo.t[0:r, :], yo, load=False)

    def head_norm_scale(src_ap, src_bufs, n, wcol, out_ap, out_buf, gate=None):
        act(SQH.t[:, 0:n], src_ap, AF.Square, src_bufs, (SQH,))
        pm = psum()
        mm(pm.t[:, 0:n], ones_h.t[:], SQH.t[:, 0:n], (ones_h, SQH), (pm,))
        rstd_from(pm, n)
        if gate is None:
            stt(DVE, out_ap, src_ap, wcol, RSTD.t[:, 0:n], ALU.mult, ALU.mult, src_bufs + (R5, RSTD), (out_buf,))
        else:
            t = TMP[2]
            stt(DVE, t.t[:, 0:n], src_ap, wcol, RSTD.t[:, 0:n], ALU.mult, ALU.mult, src_bufs + (R5, RSTD), (t,))
            tt(DVE, out_ap, t.t[:, 0:n], gate.t[:, 0:n], ALU.mult, (t, gate), (out_buf,))

    def proj_v(l, key, n, L, NCH):
        s, wv = load_block(l, key)
        for c in range(NCH):
            pb = psum(True)
            for kc in range(KC):
                mm(pb.t[0:L, 0:512], H.t[:, kc, c * L:(c + 1) * L], wv[:, kc, 0:512], (s, HB[kc]), (pb,),
                   start=(kc == 0), stop=(kc == KC - 1))
            cp(ACT, V.t[0:L, c, :], pb.t[0:L, 0:512], (pb,), (V,))

    def proj3(s, wv, n, ncol):
        outs = []
        for j in range(ncol):
            pb = psum(True)
            for kc in range(KC):
                mm(pb.t[:, 0:n], wv[:, kc, j * 128:(j + 1) * 128], H.t[:, kc, 0:n], (s, HB[kc]), (pb,),
                   start=(kc == 0), stop=(kc == KC - 1))
            outs.append(pb)
        return outs

    def mg_fill(wmg, smg, n, ocs):
        for oc in ocs:
            pg = PSB[2]
            for kc in range(KC):
                mm(pg.t[:, 0:n], wmg[:, kc, oc * 128:(oc + 1) * 128], H.t[:, kc, 0:n], (smg, HB[kc]), (pg,),
                   start=(kc == 0), stop=(kc == KC - 1))
            act(SQ.t[:, oc, 0:n], pg.t[:, 0:n], AF.Sigmoid, (pg,), (SQ,))

    def fill_ocs(c, NCH):
        per = KC // NCH
        return list(range(c * per, (c + 1) * per))

    def merge_branch(l, a, n):
        sbr, wbr = load_block(l, f"br{a}")
        for oc in range(KC):
            pp = psum(True)
            for hd in range(4):
                mm(pp.t[:, 0:n], wbr[:, hd, oc * 128:(oc + 1) * 128], Y[hd].t[:, 0:n], (sbr, Y[hd]), (pp,),
                   start=(hd == 0), stop=(hd == 3))
            if a == 0:
                tt(DVE, MERGEDv[:, oc, 0:n], SQ.t[:, oc, 0:n], pp.t[:, 0:n], ALU.mult, (SQ, pp), (ARENA,))
            else:
                sg = SG[oc % 2]
                tt(DVE, sg.t[:, 0:n], SQ.t[:, oc, 0:n], pp.t[:, 0:n], ALU.mult, (SQ, pp), (sg,))
                if a == 2:
                    tt(DVE, H.t[:, oc, 0:n], MERGEDv[:, oc, 0:n], sg.t[:, 0:n], ALU.add, (ARENA, sg), (HB[oc],))
                else:
                    tt(DVE, MERGEDv[:, oc, 0:n], MERGEDv[:, oc, 0:n], sg.t[:, 0:n], ALU.add, (ARENA, sg), (ARENA,))

    def mixer(l, n, L, NCH, sample, last):
        PHASES.append(('mix_norm', P.PE.cnt))
        rmsnorm(3 * l + 1, n)
        RM = CMASK.t[:, 512:512 + n] if sample else CMASK.t[:, 0:n]

        def sset(c):
            return c if sample else l

        def chunked(ap):
            return ap.rearrange("p (c t) -> p c t", c=NCH)

        if cfg.stage <= 0:
            return
        PHASES.append(('A_proj', P.PE.cnt))
        proj_v(l, "a_v", n, L, NCH)
        for hd in range(4):
            s, wv = load_block(l, f"a_qfg{hd}")
            pq, pf, pg = proj3(s, wv, n, 3)
            t0, t1, t2, t3 = TMP[0], TMP[1], TMP[2], TMP[3]
            act(t0.t[:, 0:n], pf.t[:, 0:n], AF.Sigmoid, (pf,), (t0,))
            ts(DVE, t0.t[:, 0:n], t0.t[:, 0:n], OML.t[:, hd, l:l + 1], LB.t[:, hd, l:l + 1], ALU.mult, ALU.add,
               (t0, OML, LB), (t0,))
            ts(DVE, t1.t[:, 0:n], t0.t[:, 0:n], -1.0, 1.0, ALU.mult, ALU.add, (t0,), (t1,))
            act(t0.t[:, 0:n], t0.t[:, 0:n], AF.Ln, (t0,), (t0,))
            scan(t2.t[:, 0:n], RM, t0.t[:, 0:n], 0.0, ALU.mult, ALU.add, (CMASK, t0), (t2,))
            act(t0.t[:, 0:n], t2.t[:, 0:n], AF.Exp, (t2,), (t0,))
            act(t3.t[:, 0:n], t2.t[:, 0:n], AF.Exp, (t2,), (t3,), scale=-1.0)
            act(t2.t[:, 0:n], pq.t[:, 0:n], AF.Silu, (pq,), (t2,))
            tt(DVE, QH[hd].t[:, 0:n], t2.t[:, 0:n], t0.t[:, 0:n], ALU.mult, (t2, t0), (QH[hd],))
            tt(DVE, KT[hd].t[:, 0:n], t1.t[:, 0:n], t3.t[:, 0:n], ALU.mult, (t1, t3), (KT[hd],))
            cp(DVE, DL[hd].t[:, 0:NCH], chunked(t0.t[:, 0:n])[:, :, L - 1], (t0,), (DL[hd],))
            act(G[hd].t[:, 0:n], pg.t[:, 0:n], AF.Sigmoid, (pg,), (G[hd],))
        if cfg.stage <= 1:
            return
        PHASES.append(('A_chunks', P.PE.cnt))
        def a_stage1(c):
            cs = slice(c * L, (c + 1) * L)
            pas, pts, ams, kks = [], [], [], []
            for hd in range(4):
                pa = PAR[(c % 2) * 4 + hd]
                mm(pa.t[0:L, 0:L], KT[hd].t[:, cs], QH[hd].t[:, cs], (KT[hd], QH[hd]), (pa.buf,))
                pt = ptr()
                tr(pt.t[0:L, 0:128], KT[hd].t[:, cs], identb.t[:], (KT[hd], identb), (pt.buf,))
                pas.append(pa)
                pts.append(pt)
            for hd in range(4):
                am = ATM[(c % 2) * 4 + hd]
                tt(DVE, am.t[0:L, 0:L], pas[hd].t[0:L, 0:L], maskT.t[0:L, 0:L], ALU.mult, (pas[hd].buf, maskT), (am,))
                kk = KTOK[(c % 2) * 4 + hd]
                cp(ACT, kk.t[0:L, :], pts[hd].t[0:L, 0:128], (pts[hd].buf,), (kk,))
                ams.append(am)
                kks.append(kk)
            return ams, kks

        def a_stage2(c, ams, kks):
            cs = slice(c * L, (c + 1) * L)
            st = sset(c)
            pus = []
            for hd in range(4):
                S = SA[st][hd]
                if c == 0 or sample:
                    cp(ACT, SAb[hd].t[:], S.t[:], (S,), (SAb[hd],))
                mm(OACC[hd].t[:, cs], V.t[0:L, c, hd * 128:(hd + 1) * 128], ams[hd].t[0:L, 0:L], (V, ams[hd]), (OACC[hd],),
                   start=True, stop=False)
                mm(OACC[hd].t[:, cs], SAb[hd].t[:], QH[hd].t[:, cs], (SAb[hd], QH[hd]), (OACC[hd],), start=False, stop=True)
                pu = View(PSB[1], PSB[1].t[:, hd * 128:(hd + 1) * 128])
                mm(pu.t[:, 0:128], kks[hd].t[0:L, :], V.t[0:L, c, hd * 128:(hd + 1) * 128], (kks[hd], V), (pu.buf,))
                pus.append(pu)
            for hd in range(4):
                S = SA[st][hd]
                tt(DVE, S.t[:], S.t[:], pus[hd].t[:, 0:128], ALU.add, (S, pus[hd].buf), (S,))
                if (not sample) and c < NCH - 1:
                    act(SAb[hd].t[:], S.t[:], AF.Copy, (S, DL[hd]), (SAb[hd],), scale=DL[hd].t[:, c:c + 1])
                ts(DVE, S.t[:], S.t[:], DL[hd].t[:, c:c + 1], None, ALU.mult, None, (S, DL[hd]), (S,))
                if sample:
                    P.dma(SP, o_hgrn_s[l, c, hd], S.t[:], S, load=False)
                elif last and c == NCH - 1:
                    P.dma(SP, o_hgrn_p[l, hd], S.t[:], S, load=False)

        smg, wmg = load_block(l, "mg0")
        nxt = a_stage1(0)
        for c in range(NCH):
            cur = nxt
            if c + 1 < NCH:
                nxt = a_stage1(c + 1)
            a_stage2(c, *cur)
            mg_fill(wmg, smg, n, fill_ocs(c, NCH))
        if cfg.stage <= 2:
            return
        PHASES.append(('A_post', P.PE.cnt))
        for hd in range(4):
            t4 = TMP[4]
            tt(DVE, t4.t[:, 0:n], OACC[hd].t[:, 0:n], G[hd].t[:, 0:n], ALU.mult, (OACC[hd], G[hd]), (t4,))
            head_norm_scale(t4.t[:, 0:n], (t4,), n, R5.t[:, hd, depth + l:depth + l + 1], Y[hd].t[:, 0:n], Y[hd])
        PHASES.append(('A_merge', P.PE.cnt))
        merge_branch(l, 0, n)

        if cfg.stage <= 3:
            return
        PHASES.append(('B_gates', P.PE.cnt))
        ssm, wsm = load_block(l, "small")
        pi = psum(True)
        pf = psum(True)
        plr = psum(True)
        for kc in range(KC):
            mm(pi.t[0:4, 0:n], wsm[:, kc, 0:4], H.t[:, kc, 0:n], (ssm, HB[kc]), (pi,), start=(kc == 0), stop=(kc == KC - 1))
        for kc in range(KC):
            mm(pf.t[0:4, 0:n], wsm[:, kc, 4:8], H.t[:, kc, 0:n], (ssm, HB[kc]), (pf,), start=(kc == 0), stop=(kc == KC - 1))
        for kc in range(KC):
            mm(plr.t[0:16, 0:n], wsm[:, kc, 8:24], H.t[:, kc, 0:n], (ssm, HB[kc]), (plr,), start=(kc == 0), stop=(kc == KC - 1))
        cp(ACT, LR.t[0:16, 0:n], plr.t[0:16, 0:n], (plr,), (LR,))
        g0, g1, g2, g3 = TMP[0], TMP[1], TMP[2], TMP[3]
        act(g0.t[0:4, 0:n], pf.t[0:4, 0:n], AF.Exp, (pf, NGBF), (g0,), scale=-1.0, bias=NGBF.t[:, l:l + 1])
        act(g0.t[0:4, 0:n], g0.t[0:4, 0:n], AF.Ln, (g0,), (g0,), bias=one_c.t[0:4, 0:1])
        if sample:
            scan(g1.t[0:4, 0:n], CMASK.t[0:4, 512:512 + n], g0.t[0:4, 0:n], 0.0, ALU.mult, ALU.add, (CMASK, g0), (g1,))
        else:
            scan(g1.t[0:4, 0:n], ones4.t[0:4, 0:n], g0.t[0:4, 0:n], FCY.t[:, l:l + 1], ALU.mult, ALU.add,
                 (ones4, g0, FCY), (g1,))
        stt(DVE, g2.t[0:4, 0:n], pi.t[0:4, 0:n], GB.t[:, l, 0:1], g1.t[0:4, 0:n], ALU.add, ALU.add,
            (pi, GB, g1), (g2,))
        if sample:
            P.dma(SP, M0.t[0:4, 0:NCH], st_m[l].rearrange("s h -> h s"), M0, load=True, allow_slow_non_contiguous=True)
            ga = TMP[4]
            cp(DVE, ga.t[0:4, 0:n], g2.t[0:4, 0:n], (g2,), (ga,))
            a0 = chunked(ga.t[0:4, 0:n])[:, :, 0]
            tt(DVE, a0, a0, M0.t[0:4, 0:NCH], ALU.max, (ga, M0), (ga,))
            scan(g3.t[0:4, 0:n], CMASK.t[0:4, 576:576 + n], ga.t[0:4, 0:n], NEG_BIG, ALU.add, ALU.max, (CMASK, ga), (g3,))
        else:
            scan(g3.t[0:4, 0:n], zeros4.t[0:4, 0:n], g2.t[0:4, 0:n], MCY.t[:, l:l + 1], ALU.add, ALU.max,
                 (zeros4, g2, MCY), (g3,))
        mcv = chunked(g3.t[0:4, 0:n])[:, :, L - 1]
        if sample:
            cp(DVE, MP.t[0:4, 0:NCH], M0.t[0:4, 0:NCH], (M0,), (MP,))
        else:
            cp(DVE, MP.t[0:4, 0:1], MCY.t[:, l:l + 1], (MCY,), (MP,))
            if NCH > 1:
                cp(DVE, MP.t[0:4, 1:NCH], chunked(g3.t[0:4, 0:n])[:, 0:NCH - 1, L - 1], (g3,), (MP,))
        tt(DVE, R4.t[0:4, 0:NCH], MP.t[0:4, 0:NCH], mcv, ALU.subtract, (MP, g3), (R4,))
        act(R4.t[0:4, 0:NCH], R4.t[0:4, 0:NCH], AF.Exp, (R4,), (R4,))
        tt(DVE, MO.t[0:4, 0:NCH], mcv, chunked(g1.t[0:4, 0:n])[:, :, L - 1], ALU.subtract, (g3, g1), (MO,))
        mcb = mcv.unsqueeze(2).broadcast_to([4, NCH, L])
        g4, g5 = TMP[4], TMP[5]
        tt(DVE, chunked(g4.t[0:4, 0:n]), chunked(g2.t[0:4, 0:n]), mcb, ALU.subtract, (g2, g3), (g4,))
        act(g4.t[0:4, 0:n], g4.t[0:4, 0:n], AF.Exp, (g4,), (g4,))
        tt(DVE, chunked(g5.t[0:4, 0:n]), chunked(g1.t[0:4, 0:n]), mcb, ALU.subtract, (g1, g3), (g5,))
        act(g5.t[0:4, 0:n], g5.t[0:4, 0:n], AF.Exp, (g5,), (g5,))
        if not sample:
            cp(DVE, FCY.t[:, l:l + 1], g1.t[0:4, n - 1:n], (g1,), (FCY,))
            cp(DVE, MCY.t[:, l:l + 1], g3.t[0:4, n - 1:n], (g3,), (MCY,))
        if cfg.stage <= 4:
            return
        pp_ = psum()
        for c in range(NCH):
            tr(pp_.t[0:L, c * 4:(c + 1) * 4], g4.t[0:4, c * L:(c + 1) * L], ident.t[0:4, 0:4], (g4, ident), (pp_,))
        cp(ACT, PTK.t[0:L, 0:NCH * 4], pp_.t[0:L, 0:NCH * 4], (pp_,), (PTK,))
        for hd in range(4):
            pr_ = psum()
            mm(pr_.t[:, 0:NCH], SEL.t[0:4, hd * 128:(hd + 1) * 128], R4.t[0:4, 0:NCH], (SEL, R4), (pr_,))
            cp(ACT, RB.t[:, hd, 0:NCH], pr_.t[:, 0:NCH], (pr_,), (RB,))
        if sample:
            P.dma(SP, o_m_s[l].rearrange("s h -> h s"), MO.t[0:4, 0:NCH], MO, load=False, allow_slow_non_contiguous=True)
        elif last:
            P.dma(SP, o_m_p[l].rearrange("(h o) -> h o", o=1), MO.t[0:4, NCH - 1:NCH], MO, load=False)
        if sample:
            P.dma(SP, NROWS.t[0:4 * NCH, :], st_n[l], NROWS, load=True)
            pn = psum()
            tr(pn.t[:, 0:4 * NCH], NROWS.t[0:4 * NCH, :], ident.t[0:4 * NCH, 0:4 * NCH], (NROWS, ident), (pn,))
            cp(ACT, NT.t[:, 0:4 * NCH], pn.t[:, 0:4 * NCH], (pn,), (NT,))
        if cfg.stage <= 5:
            return
        PHASES.append(('B_proj', P.PE.cnt))
        proj_v(l, "b_v", n, L, NCH)
        for hd in range(4):
            s, wv = load_block(l, f"b_qko{hd}")
            pq, pk, po = proj3(s, wv, n, 3)
            cp(ACT, QH[hd].t[:, 0:n], pq.t[:, 0:n], (pq,), (QH[hd],))
            cp(DVE, KT[hd].t[:, 0:n], pk.t[:, 0:n], (pk,), (KT[hd],))
            act(G[hd].t[:, 0:n], po.t[:, 0:n], AF.Sigmoid, (po,), (G[hd],))
        if cfg.stage <= 6:
            return
        PHASES.append(('B_chunks_post', P.PE.cnt))
        for hp in range(2):
            def b_stage1(c):
                cs = slice(c * L, (c + 1) * L)
                pas, pts, ams, kks = [], [], [], []
                for e in range(2):
                    hd = hp * 2 + e
                    pa = PAR[(c % 2) * 4 + e]
                    mm(pa.t[0:L, 0:L], KT[hd].t[:, cs], QH[hd].t[:, cs], (KT[hd], QH[hd]), (pa.buf,))
                    pt = ptr()
                    tr(pt.t[0:L, 0:128], KT[hd].t[:, cs], identb.t[:], (KT[hd], identb), (pt.buf,))
                    pas.append(pa)
                    pts.append(pt)
                for e in range(2):
                    hd = hp * 2 + e
                    pcol = PTK.t[0:L, c * 4 + hd:c * 4 + hd + 1]
                    am = ATM[(c % 2) * 4 + e]
                    stt(DVE, am.t[0:L, 0:L], pas[e].t[0:L, 0:L], pcol, maskS.t[0:L, 0:L], ALU.mult, ALU.mult,
                        (pas[e].buf, PTK, maskS), (am,))
                    kk = KTOK[(c % 2) * 4 + e]
                    ts(DVE, kk.t[0:L, :], pts[e].t[0:L, 0:128], pcol, 128.0 ** -0.5, ALU.mult, ALU.mult, (pts[e].buf, PTK), (kk,))
                    ams.append(am)
                    kks.append(kk)
                return ams, kks

            def b_stage2(c, ams, kks):
                cs = slice(c * L, (c + 1) * L)
                st = sset(c)
                pus = []
                for e in range(2):
                    hd = hp * 2 + e
                    C = CB[st][hd]
                    NUM = OACC[2 * e]
                    DEN = OACC[2 * e + 1]
                    am = ams[e]
                    kk = kks[e]
                    if sample:
                        cp(DVE, C.t[:, 128:256], NT.t[:, c * 4 + hd:c * 4 + hd + 1].broadcast_to([128, 128]), (NT,), (C,))
                    if c == 0 or sample:
                        act(CBb[hd].t[:], C.t[:], AF.Copy, (C, RB), (CBb[hd],), scale=RB.t[:, hd, c:c + 1])
                    mm(NUM.t[:, cs], V.t[0:L, c, hd * 128:(hd + 1) * 128], am.t[0:L, 0:L], (V, am), (NUM,),
                       start=True, stop=False)
                    mm(NUM.t[:, cs], CBb[hd].t[:, 0:128], QH[hd].t[:, cs], (CBb[hd], QH[hd]), (NUM,), start=False, stop=True)
                    mm(DEN.t[:, cs], ones64.t[0:L, :], am.t[0:L, 0:L], (ones64, am), (DEN,), start=True, stop=False)
                    mm(DEN.t[:, cs], CBb[hd].t[:, 128:256], QH[hd].t[:, cs], (CBb[hd], QH[hd]), (DEN,), start=False, stop=True)
                    pu = PUR[e]
                    mm(pu.t[:, 0:128], kk.t[0:L, :], V.t[0:L, c, hd * 128:(hd + 1) * 128], (kk, V), (pu.buf,))
                    mm(pu.t[:, 128:256], kk.t[0:L, :], ones64.t[0:L, :], (kk, ones64), (pu.buf,))
                    pus.append(pu)
                for e in range(2):
                    hd = hp * 2 + e
                    C = CB[st][hd]
                    stt(DVE, C.t[:], C.t[:], RB.t[:, hd, c:c + 1], pus[e].t[:, 0:256], ALU.mult, ALU.add, (C, RB, pus[e].buf), (C,))
                    if (not sample) and c < NCH - 1:
                        act(CBb[hd].t[:], C.t[:], AF.Copy, (C, RB), (CBb[hd],), scale=RB.t[:, hd, c + 1:c + 2])
                    fin = sample or (last and c == NCH - 1)
                    if fin:
                        idx = (c * 4 + hd) if sample else hd
                        cp(DVE, NT.t[:, idx:idx + 1], C.t[:, 128:129], (C,), (NT,))
                        dst = o_c_s[l, c, hd] if sample else o_c_p[l, hd]
                        P.dma(SP, dst, C.t[:, 0:128], C, load=False)

            if hp == 0:
                smg, wmg = load_block(l, "mg1")
            nxt = b_stage1(0)
            for c in range(NCH):
                cur = nxt
                if c + 1 < NCH:
                    nxt = b_stage1(c + 1)
                b_stage2(c, *cur)
                if hp == 0:
                    mg_fill(wmg, smg, n, fill_ocs(c, NCH))
            for e in range(2):
                hd = hp * 2 + e
                NUM = OACC[2 * e]
                DEN = OACC[2 * e + 1]
                pth = psum()
                mm(pth.t[:, 0:n], SEL.t[0:4, hd * 128:(hd + 1) * 128], g5.t[0:4, 0:n], (SEL, g5), (pth,))
                t0, t1 = TMP[0], TMP[1]
                act(t0.t[:, 0:n], DEN.t[:, 0:n], AF.Abs, (DEN,), (t0,))
                tt(DVE, t0.t[:, 0:n], t0.t[:, 0:n], pth.t[:, 0:n], ALU.max, (t0, pth), (t0,))
                P.op(DVE, lambda: nc.vector.reciprocal(out=t0.t[:, 0:n], in_=t0.t[:, 0:n]), reads=(t0,), writes=(t0,))
                tt(DVE, t1.t[:, 0:n], NUM.t[:, 0:n], t0.t[:, 0:n], ALU.mult, (NUM, t0), (t1,))
                head_norm_scale(t1.t[:, 0:n], (t1,), n, R5.t[:, hd, 2 * depth + l:2 * depth + l + 1],
                                Y[hd].t[:, 0:n], Y[hd], gate=G[hd])
        if cfg.stage <= 7:
            return
        if sample or last:
            ncols = 4 * NCH if sample else 4
            pn = psum()
            tr(pn.t[0:ncols, 0:128], NT.t[:, 0:ncols], ident.t[:, :], (NT, ident), (pn,))
            cp(ACT, NROWS.t[0:ncols, :], pn.t[0:ncols, 0:128], (pn,), (NROWS,))
            P.dma(SP, (o_n_s[l] if sample else o_n_p[l]), NROWS.t[0:ncols, :], NROWS, load=False)
        PHASES.append(('B_merge', P.PE.cnt))
        merge_branch(l, 1, n)

        if cfg.stage <= 8:
            return
        PHASES.append(('C_proj', P.PE.cnt))
        proj_v(l, "c_v", n, L, NCH)
        for pr in range(2):
            s, wv = load_block(l, f"c_qk{pr}")
            for e in range(2):
                hd = 2 * pr + e
                pq = psum(True)
                pk = psum(True)
                pz = psum(True)
                for kc in range(KC):
                    mm(pq.t[0:64, 0:n], wv[:, kc, e * 64:(e + 1) * 64], H.t[:, kc, 0:n], (s, HB[kc]), (pq,),
                       start=(kc == 0), stop=(kc == KC - 1))
                for kc in range(KC):
                    mm(pk.t[0:64, 0:n], wv[:, kc, 128 + e * 64:128 + (e + 1) * 64], H.t[:, kc, 0:n], (s, HB[kc]), (pk,),
                       start=(kc == 0), stop=(kc == KC - 1))
                mm(pz.t[0:64, 0:n], W2b.t[0:16, l, hd * 64:(hd + 1) * 64], LR.t[0:16, 0:n], (W2b, LR), (pz,))
                t0, t2, t3 = TMP[0], TMP[2], TMP[3]
                act(t0.t[0:64, 0:n], pz.t[0:64, 0:n], AF.Exp, (pz, NBA), (t0,), scale=-1.0, bias=NBA.t[0:64, hd, l:l + 1])
                act(t0.t[0:64, 0:n], t0.t[0:64, 0:n], AF.Ln, (t0,), (t0,), bias=one_c.t[0:64, 0:1])
                scan(t2.t[0:64, 0:n], RM[0:64], t0.t[0:64, 0:n], 0.0, ALU.mult, ALU.add, (CMASK, t0), (t2,))
                act(t0.t[0:64, 0:n], t2.t[0:64, 0:n], AF.Exp, (t2,), (t0,), scale=-1.0 / 16.0)
                act(t3.t[0:64, 0:n], t2.t[0:64, 0:n], AF.Exp, (t2,), (t3,), scale=1.0 / 16.0)
                stt(DVE, QH[hd].t[0:64, 0:n], pq.t[0:64, 0:n], 0.125, t0.t[0:64, 0:n], ALU.mult, ALU.mult,
                    (pq, t0), (QH[hd],))
                tt(DVE, KT[hd].t[0:64, 0:n], pk.t[0:64, 0:n], t3.t[0:64, 0:n], ALU.mult, (pk, t3), (KT[hd],))
                cp(DVE, DL[hd].t[0:64, 0:NCH], chunked(t0.t[0:64, 0:n])[:, :, L - 1], (t0,), (DL[hd],))
        if cfg.stage <= 9:
            return
        s, wv = load_block(l, "c_g")
        for hd in range(4):
            pb = psum(True)
            for kc in range(KC):
                mm(pb.t[:, 0:n], wv[:, kc, hd * 128:(hd + 1) * 128], H.t[:, kc, 0:n], (s, HB[kc]), (pb,),
                   start=(kc == 0), stop=(kc == KC - 1))
            act(G[hd].t[:, 0:n], pb.t[:, 0:n], AF.Silu, (pb,), (G[hd],))
        if cfg.stage <= 10:
            return
        PHASES.append(('C_chunks', P.PE.cnt))
        def c_stage1(c):
            cs = slice(c * L, (c + 1) * L)
            pas, pts, ams, kks = [], [], [], []
            for hd in range(4):
                pa = PAR[(c % 2) * 4 + hd]
                mm(pa.t[0:L, 0:L], KT[hd].t[0:64, cs], QH[hd].t[0:64, cs], (KT[hd], QH[hd]), (pa.buf,))
                pt = ptr()
                tr(pt.t[0:L, 0:64], KT[hd].t[0:64, cs], identb.t[0:64, 0:64], (KT[hd], identb), (pt.buf,))
                pas.append(pa)
                pts.append(pt)
            for hd in range(4):
                am = ATM[(c % 2) * 4 + hd]
                tt(DVE, am.t[0:L, 0:L], pas[hd].t[0:L, 0:L], maskT.t[0:L, 0:L], ALU.mult, (pas[hd].buf, maskT), (am,))
                kk = KTOK[(c % 2) * 4 + hd]
                cp(ACT, kk.t[0:L, 0:64], pts[hd].t[0:L, 0:64], (pts[hd].buf,), (kk,))
                ams.append(am)
                kks.append(kk)
            return ams, kks

        def c_stage2(c, ams, kks):
            cs = slice(c * L, (c + 1) * L)
            st = sset(c)
            pus = []
            for hd in range(4):
                S = SC[st][hd]
                Sb = SAb[hd]
                if c == 0 or sample:
                    cp(ACT, Sb.t[0:64, :], S.t[:], (S,), (Sb,))
                mm(OACC[hd].t[:, cs], V.t[0:L, c, hd * 128:(hd + 1) * 128], ams[hd].t[0:L, 0:L], (V, ams[hd]), (OACC[hd],),
                   start=True, stop=False)
                mm(OACC[hd].t[:, cs], Sb.t[0:64, :], QH[hd].t[0:64, cs], (Sb, QH[hd]), (OACC[hd],), start=False, stop=True)
                pu = View(PSB[1], PSB[1].t[:, hd * 128:(hd + 1) * 128])
                mm(pu.t[0:64, 0:128], kks[hd].t[0:L, 0:64], V.t[0:L, c, hd * 128:(hd + 1) * 128], (kks[hd], V), (pu.buf,))
                pus.append(pu)
            for hd in range(4):
                S = SC[st][hd]
                Sb = SAb[hd]
                tt(DVE, S.t[:], S.t[:], pus[hd].t[0:64, 0:128], ALU.add, (S, pus[hd].buf), (S,))
                if (not sample) and c < NCH - 1:
                    act(Sb.t[0:64, :], S.t[:], AF.Copy, (S, DL[hd]), (Sb,), scale=DL[hd].t[0:64, c:c + 1])
                ts(DVE, S.t[:], S.t[:], DL[hd].t[0:64, c:c + 1], None, ALU.mult, None, (S, DL[hd]), (S,))
                if sample:
                    P.dma(SP, o_gla_s[l, c, hd], S.t[:], S, load=False)
                elif last and c == NCH - 1:
                    P.dma(SP, o_gla_p[l, hd], S.t[:], S, load=False)

        smg, wmg = load_block(l, "mg2")
        nxt = c_stage1(0)
        for c in range(NCH):
            cur = nxt
            if c + 1 < NCH:
                nxt = c_stage1(c + 1)
            c_stage2(c, *cur)
            mg_fill(wmg, smg, n, fill_ocs(c, NCH))
        if cfg.stage <= 11:
            return
        PHASES.append(('C_post', P.PE.cnt))
        for hd in range(4):
            head_norm_scale(OACC[hd].t[:, 0:n], (OACC[hd],), n, R5.t[:, hd, 3 * depth + l:3 * depth + l + 1],
                            Y[hd].t[:, 0:n], Y[hd], gate=G[hd])
        PHASES.append(('C_merge', P.PE.cnt))
        merge_branch(l, 2, n)

        if cfg.stage <= 12:
            return
        PHASES.append(('out_proj', P.PE.cnt))
        s, wv = load_block(l, "wout")
        for oc in range(KC):
            pb = psum(True)
            for kc in range(KC):
                mm(pb.t[:, 0:n], wv[:, kc, oc * 128:(oc + 1) * 128], H.t[:, kc, 0:n], (s, HB[kc]), (pb,),
                   start=(kc == 0), stop=(kc == KC - 1))
            tt(DVE, X.t[:, oc, 0:n], X.t[:, oc, 0:n], pb.t[:, 0:n], ALU.add, (X, pb), (X,))

    def load_sample_states(l):
        for sq in range(NSEQ):
            for hd in range(4):
                P.dma(SP, SA[sq][hd].t[:], st_hgrn[l, sq, hd], SA[sq][hd], load=True)
                P.dma(SP, CB[sq][hd].t[:, 0:128], st_c[l, sq, hd], CB[sq][hd], load=True)
            for hd in range(4):
                P.dma(SP, SC[sq][hd].t[:], st_gla[l, sq, hd], SC[sq][hd], load=True)

    def zero_states():
        for st in range(NSET):
            for hd in range(4):
                ms(POOL, SA[st][hd], SA[st][hd].t[:], 0.0)
                ms(POOL, CB[st][hd], CB[st][hd].t[:], 0.0)
            for hd in range(4):
                ms(POOL, SC[st][hd], SC[st][hd].t[:], 0.0)

    do_mixer = cfg.mixer
    if NS > 0:
        load_x(xs[0:NS, :], NS)
        for l in range(depth):
            ffn(l, 1, NS)
            if do_mixer:
                load_sample_states(l)
                mixer(l, NS, SL, NSEQ, True, False)
            ffn(l, 2, NS)
        store_y(ys[0:NS, :], NS)
    if NP > 0:
        zero_states()
        ntile = NP // T
        for ti in range(ntile):
            load_x(xp[ti * T:(ti + 1) * T, :], T)
            for l in range(depth):
                ffn(l, 1, T)
                if do_mixer:
                    mixer(l, T, 64, T // 64, False, ti == ntile - 1)
                ffn(l, 2, T)
            store_y(yp[ti * T:(ti + 1) * T, :], T)

    for b in XIO + [MO, NROWS] + [x for st in SA + CB + SC for x in st]:
        if b.dcnt:
            P._wait(SP, (b.dsem, b.dcnt))
    for e in (PE, ACT, DVE, POOL):
        if e.cnt:
            P._wait(SP, (e.sem, e.cnt))
    print(f"[build] inst={P.n_inst} waits={P.n_wait} sems={P.nsem} sbuf_left={nc.sbuf_bytes_remaining}")


_NC_CACHE = {}


def kernel(**inputs):
    inp = {k: np.asarray(v) for k, v in inputs.items()}
    B, SEQ, _ = inp["x_prompt"].shape
    NSAMP, SL, _ = inp["x_sample"].shape
    n_cores = 8
    nseq = NSAMP // n_cores
    cfg = Cfg(SEQ, nseq, SL, depth=DEPTH)
    key = (SEQ, nseq, SL)
    if key not in _NC_CACHE:
        _NC_CACHE[key] = build_program(cfg)
    nc = _NC_CACHE[key]
    in_maps = [core_inputs(inp, cfg, c % B, c, list(range(c * nseq, (c + 1) * nseq))) for c in range(n_cores)]
    res = run_bass_kernel_spmd(nc, in_maps, core_ids=list(range(n_cores)))
    R = res.results
    f32 = np.float32
    y_prompt = np.stack([R[b]["y_prompt"] for b in range(B)]).astype(f32)
    y_sample = np.concatenate([R[c]["y_sample"].reshape(nseq, SL, D_MODEL) for c in range(n_cores)], 0).astype(f32)

    def pstate(name, shape):
        return np.stack([R[b][name].reshape(shape) for b in range(B)], 1).astype(f32)

    def sstate(name, shape):
        return np.concatenate([R[c][name].reshape((DEPTH, nseq) + shape) for c in range(n_cores)], 1).astype(f32)

    return (
        y_prompt, y_sample,
        pstate("hgrn_p", (DEPTH, 4, 128, 128)), pstate("c_p", (DEPTH, 4, 128, 128)), pstate("n_p", (DEPTH, 4, 128)),
        pstate("m_p", (DEPTH, 4)), pstate("gla_p", (DEPTH, 4, 64, 128)),
        sstate("hgrn_s", (4, 128, 128)), sstate("c_s", (4, 128, 128)), sstate("n_s", (4, 128)),
        sstate("m_s", (4,)), sstate("gla_s", (4, 64, 128)),
    )


def core_inputs(inp, cfg, b, core, seqs):
    depth = cfg.depth
    f = np.ascontiguousarray
    NP, NSEQ, SL = cfg.n_prompt, cfg.n_seq, cfg.seq_len
    nsq = max(NSEQ, 1)
    sq = list(seqs) if NSEQ > 0 else [0]
    m = {
        "x_prompt": f(inp["x_prompt"][b, :max(NP, 1)]),
        "x_sample": f(inp["x_sample"][sq].reshape(nsq * SL, D_MODEL)[:max(NSEQ * SL, 1)]),
        "st_hgrn": f(inp["state_hgrn"][:depth, sq]),
        "st_c": f(inp["state_mlstm_c"][:depth, sq]),
        "st_n": f(inp["state_mlstm_n"][:depth, sq].reshape(depth, nsq * 4, 128)),
        "st_m": f(inp["state_mlstm_m"][:depth, sq]),
        "st_gla": f(inp["state_gla"][:depth, sq]),
        "norms": f(np.concatenate([np.stack([inp["ffn1_norm"][l], inp["mix_norm"][l], inp["ffn2_norm"][l]])
                                   for l in range(depth)] + [inp["final_norm"][None]], 0)),
        "rows512": f(np.concatenate([inp["hgrn_lb_logits"][:depth], inp["hgrn_norm"][:depth],
                                     inp["mlstm_norm"][:depth], inp["gla_norm"][:depth]], 0)),
        "gla_b_a": f(inp["gla_b_a"][:depth]),
        "mlstm_gate_bias": f(inp["mlstm_gate_bias"][:depth]),
        "gla_w_a2": f(inp["gla_w_a2"][:depth]),
    }
    for k in ("ffn1_w_up", "ffn1_w_down", "ffn2_w_up", "ffn2_w_down", "w_in", "w_branch", "w_out"):
        m[k] = f(inp[k][:depth])
    m.update(host_consts())
    return m
```

```python
import math
from contextlib import ExitStack

import numpy as np
import concourse.bass as bass
import concourse.mybir as mybir
from concourse.bass_utils import run_bass_kernel_spmd

F32 = mybir.dt.float32
BF16 = mybir.dt.bfloat16
AF = mybir.ActivationFunctionType
ALU = mybir.AluOpType
AX = mybir.AxisListType

D_MODEL = 1024
DEPTH = 4
D_FF = 2816
D_IN = 8728
EPS = 1e-6
KC = 8
FC = 22
NEG_BIG = -1.0e30


PHASES = []


class Buf:
    def __init__(self, name, t, dsem=None):
        self.name = name
        self.t = t
        self.w = None
        self.r = {}
        self.dsem = dsem
        self.dcnt = 0

    def __getitem__(self, k):
        return self.t[k]


class Eng:
    def __init__(self, name, h, sem):
        self.name = name
        self.h = h
        self.sem = sem
        self.cnt = 0
        self.seen = {}


class Prog:
    def __init__(self, nc, es):
        self.nc = nc
        self.es = es
        self.nsem = 0
        self.PE = self._eng("pe", nc.tensor)
        self.ACT = self._eng("act", nc.scalar)
        self.DVE = self._eng("dve", nc.vector)
        self.POOL = self._eng("pool", nc.gpsimd)
        self.SP = self._eng("sp", nc.sync)
        self.engs = [self.PE, self.ACT, self.DVE, self.POOL, self.SP]
        self.n_inst = 0
        self.n_wait = 0

    def sem(self, name):
        self.nsem += 1
        return self.es.enter_context(self.nc.semaphore(name))

    def _eng(self, name, h):
        return Eng(name, h, self.sem("e_" + name))

    def sb(self, name, shape, dtype, dma=False):
        t = self.es.enter_context(self.nc.sbuf_tensor(name, list(shape), dtype))
        return Buf(name, t, self.sem("d_" + name) if dma else None)

    def ps(self, name, shape, dtype=F32):
        t = self.es.enter_context(self.nc.psum_tensor(name, list(shape), dtype))
        return Buf(name, t)

    def _need(self, e, tok, acc):
        sem, val = tok
        if e is self.PE and sem is self.PE.sem:
            return
        if e.seen.get(sem, 0) >= val:
            return
        e.seen[sem] = val
        acc[sem] = max(acc.get(sem, 0), val)

    def _wait(self, e, tok):
        acc = {}
        self._need(e, tok, acc)
        for sem, val in acc.items():
            e.h.wait_ge(sem, val)
            self.n_wait += 1

    def _deps(self, e, reads, writes):
        acc = {}
        for b in reads:
            if b.w is not None:
                self._need(e, b.w, acc)
        for b in writes:
            if b.w is not None:
                self._need(e, b.w, acc)
            for sem, val in b.r.items():
                self._need(e, (sem, val), acc)
        return list(acc.items())

    def _emit(self, e, fn, waits):
        for sem, val in waits[:-1]:
            e.h.wait_ge(sem, val)
            self.n_wait += 1
        inst = fn()
        if waits:
            sem, val = waits[-1]
            inst._wait_ge(sem, val)
        return inst

    def op(self, e, fn, reads=(), writes=()):
        waits = self._deps(e, reads, writes)
        inst = self._emit(e, fn, waits)
        e.cnt += 1
        inst.then_inc(e.sem, 1)
        tok = (e.sem, e.cnt)
        for b in writes:
            b.w = tok
            b.r = {}
        for b in reads:
            if b not in writes:
                b.r[e.sem] = e.cnt
        self.n_inst += 1
        return inst

    def dma(self, q, out, in_, buf, load, **kw):
        if load:
            waits = self._deps(q, (), (buf,))
        else:
            waits = self._deps(q, (buf,), ())
        inst = self._emit(q, lambda: q.h.dma_start(out=out, in_=in_, **kw), waits)
        inst.then_inc(buf.dsem, 16)
        buf.dcnt += 16
        tok = (buf.dsem, buf.dcnt)
        if load:
            buf.w = tok
            buf.r = {}
        else:
            buf.r[buf.dsem] = buf.dcnt
        self.n_inst += 1
        return inst


class Cfg:
    def __init__(self, n_prompt, n_seq, seq_len, depth=DEPTH, T=512, mixer=True, stage=99):
        self.mixer = mixer
        self.stage = stage
        self.n_prompt = n_prompt
        self.n_seq = n_seq
        self.seq_len = seq_len
        self.depth = depth
        self.T = T


def weight_blocks():
    blks = []
    for f in (1, 2):
        for i in range(5):
            blks.append((f"up{f}_{i}", f"ffn{f}_w_up", 1024, [(512 * i, 512), (2816 + 512 * i, 512)]))
        blks.append((f"up{f}_5", f"ffn{f}_w_up", 1024, [(2560, 256), (5376, 256)]))
        for i in range(4):
            blks.append((f"dn{f}_{i}", f"ffn{f}_w_down", 2816, [(256 * i, 256)]))
    for hd in range(4):
        blks.append((f"a_qfg{hd}", "w_in", 1024,
                     [(hd * 128, 128), (512 + hd * 128, 128), (1536 + hd * 128, 128)]))
    blks.append(("a_v", "w_in", 1024, [(1024, 512)]))
    for hd in range(4):
        blks.append((f"b_qko{hd}", "w_in", 1024,
                     [(2048 + hd * 128, 128), (2560 + hd * 128, 128), (3584 + hd * 128, 128)]))
    blks.append(("b_v", "w_in", 1024, [(3072, 512)]))
    blks.append(("small", "w_in", 1024, [(4096, 8), (5640, 16)]))
    for pr in range(2):
        blks.append((f"c_qk{pr}", "w_in", 1024, [(4104 + pr * 128, 128), (4360 + pr * 128, 128)]))
    blks.append(("c_v", "w_in", 1024, [(4616, 512)]))
    blks.append(("c_g", "w_in", 1024, [(5128, 512)]))
    for a in range(3):
        blks.append((f"mg{a}", "w_in", 1024, [(5656 + a * 1024, 1024)]))
    for a in range(3):
        blks.append((f"br{a}", "w_branch", 512, [(0, 1024)]))
    blks.append(("wout", "w_out", 1024, [(0, 1024)]))
    return blks


def build_program(cfg):
    nc = bass.Bass("TRN2", target_bir_lowering=False)
    es = ExitStack()
    with es:
        _build(nc, es, cfg)
    return nc


def host_consts():
    ident = np.eye(128, dtype=np.float32)
    maskT = np.triu(np.ones((64, 64), dtype=np.float32))
    cm = np.ones((128, 512 + 64 + 64), dtype=np.float32)
    cm[:, 0:512:64] = 0.0
    cm[:, 512:576:16] = 0.0
    cm[:, 576:640] = 0.0
    cm[:, 576:640:16] = NEG_BIG
    sel = np.zeros((4, 512), dtype=np.float32)
    for h in range(4):
        sel[h, h * 128:(h + 1) * 128] = 1.0
    return {"ident_in": ident, "maskT_in": maskT, "cmask_in": cm, "sel_in": sel}


def _build(nc, es, cfg):
    P = Prog(nc, es)
    T = cfg.T
    depth = cfg.depth
    NP = cfg.n_prompt
    NSEQ = cfg.n_seq
    SL = cfg.seq_len
    NS = NSEQ * SL
    NSET = max(depth, NSEQ, 1)
    PE, ACT, DVE, POOL, SP = P.PE, P.ACT, P.DVE, P.POOL, P.SP

    def din(name, shape):
        return nc.dram_tensor(name, list(shape), F32, kind="ExternalInput").ap()

    def dout(name, shape):
        return nc.dram_tensor(name, list(shape), F32, kind="ExternalOutput").ap()

    xp = din("x_prompt", [max(NP, 1), D_MODEL])
    yp = dout("y_prompt", [max(NP, 1), D_MODEL])
    xs = din("x_sample", [max(NS, 1), D_MODEL])
    ys = dout("y_sample", [max(NS, 1), D_MODEL])
    nsq = max(NSEQ, 1)
    st_hgrn = din("st_hgrn", [depth, nsq, 4, 128, 128])
    st_c = din("st_c", [depth, nsq, 4, 128, 128])
    st_n = din("st_n", [depth, nsq * 4, 128])
    st_m = din("st_m", [depth, nsq, 4])
    st_gla = din("st_gla", [depth, nsq, 4, 64, 128])
    o_hgrn_p = dout("hgrn_p", [depth, 4, 128, 128])
    o_c_p = dout("c_p", [depth, 4, 128, 128])
    o_n_p = dout("n_p", [depth, 4, 128])
    o_m_p = dout("m_p", [depth, 4])
    o_gla_p = dout("gla_p", [depth, 4, 64, 128])
    o_hgrn_s = dout("hgrn_s", [depth, nsq, 4, 128, 128])
    o_c_s = dout("c_s", [depth, nsq, 4, 128, 128])
    o_n_s = dout("n_s", [depth, nsq * 4, 128])
    o_m_s = dout("m_s", [depth, nsq, 4])
    o_gla_s = dout("gla_s", [depth, nsq, 4, 64, 128])
    w_dram = {
        "ffn1_w_up": din("ffn1_w_up", [depth, 1024, 2 * D_FF]),
        "ffn1_w_down": din("ffn1_w_down", [depth, D_FF, 1024]),
        "ffn2_w_up": din("ffn2_w_up", [depth, 1024, 2 * D_FF]),
        "ffn2_w_down": din("ffn2_w_down", [depth, D_FF, 1024]),
        "w_in": din("w_in", [depth, 1024, D_IN]),
        "w_branch": din("w_branch", [depth, 3, 512, 1024]),
        "w_out": din("w_out", [depth, 1024, 1024]),
    }
    norms = din("norms", [3 * depth + 1, 1024])
    rows512 = din("rows512", [4 * depth, 512])
    gla_b_a = din("gla_b_a", [depth, 256])
    gate_bias = din("mlstm_gate_bias", [depth, 8])
    gla_w2 = din("gla_w_a2", [depth, 16, 256])
    identd = din("ident_in", [128, 128])
    maskd = din("maskT_in", [64, 64])
    cmaskd = din("cmask_in", [128, 640])
    seld = din("sel_in", [4, 512])

    blks = weight_blocks()
    scratch = {}
    for l in range(depth):
        for key, src, K, cols in blks:
            kc = K // 128
            W = sum(n for _, n in cols)
            scratch[(l, key)] = (nc.dram_tensor(f"ws_{l}_{key}", [128, kc * W], BF16, kind="Internal").ap(), kc, W)

    cast_sem = P.sem("cast")
    n_cast = 0
    for l in range(depth):
        for key, src, K, cols in blks:
            sap, kc, W = scratch[(l, key)]
            sview = sap.rearrange("p (k w) -> p k w", k=kc)
            off = 0
            for c0, n in cols:
                if src == "w_branch":
                    a = int(key[2])
                    srcap = w_dram[src][l, a, :, c0:c0 + n]
                else:
                    srcap = w_dram[src][l, :, c0:c0 + n]
                srcap = srcap.rearrange("(k p) n -> p k n", p=128)
                POOL.h.dma_start(out=sview[:, :, off:off + n], in_=srcap).then_inc(cast_sem, 16)
                n_cast += 1
                off += n
    cast_tok = (cast_sem, 16 * n_cast)

    NSLOT = 3
    SLOTW = 8192
    slots = [P.sb(f"wslot{i}", [128, SLOTW], BF16, dma=True) for i in range(NSLOT)]
    slot_i = [0]

    def load_block(l, key):
        sap, kc, W = scratch[(l, key)]
        s = slots[slot_i[0] % NSLOT]
        slot_i[0] += 1
        P._wait(SP, cast_tok)
        P.dma(SP, s.t[:, 0:kc * W], sap[:, :], s, load=True)
        return s, s.t[:, 0:kc * W].rearrange("p (k w) -> p k w", k=kc)

    X = P.sb("X", [128, KC, T], F32)
    XB = [Buf(f"X{k}", None) for k in range(KC)]
    H = P.sb("H", [128, KC, T], BF16)
    HB = [Buf(f"H{k}", None) for k in range(KC)]
    ARENA = P.sb("ARENA", [128, FC * T // 2], F32)
    HIDv = ARENA.t[:, :].bitcast(BF16).rearrange("p (c t) -> p c t", c=FC)
    MERGEDv = ARENA.t[:, 0:KC * T].rearrange("p (c t) -> p c t", c=KC)
    SQ = P.sb("SQ", [128, KC, T], BF16)
    RSTD = P.sb("RSTD", [128, T], F32)
    SG = [P.sb(f"SG{i}", [128, T], F32) for i in range(2)]
    XIO = [P.sb(f"XIO{i}", [128, D_MODEL], F32, dma=True) for i in range(2)]
    csem = P.sem("consts")
    cbufs = []

    def cload(name, shape, dtype, src, q=None, **kw):
        b = P.sb(name, shape, dtype)
        b.dsem = csem
        cbufs.append(b)
        (q or SP).h.dma_start(out=b.t[:], in_=src, **kw).then_inc(csem, 16)
        return b

    ident = cload("ident", [128, 128], F32, identd[:, :])
    NROW = XIO[0]
    R5ROW = XIO[1]
    P.dma(POOL, XIO[0].t[0:3 * depth + 1, :], norms[:, :], XIO[0], load=True)
    P.dma(POOL, XIO[1].t[0:4 * depth, 0:512], rows512[:, :], XIO[1], load=True)
    P.dma(POOL, XIO[1].t[0:depth, 512:768], gla_b_a[:, :], XIO[1], load=True)
    maskT = cload("maskT", [64, 64], F32, maskd[:, :])
    CMASK = cload("CMASK", [128, 640], F32, cmaskd[:, :])
    SEL = cload("SEL", [4, 512], F32, seld[:, :])
    W2b = P.sb("W2b", [16, depth, 256], BF16, dma=True)
    P.dma(POOL, W2b.t[:], gla_w2.rearrange("l r c -> r l c"), W2b, load=True)
    GB = cload("GB", [4, depth, 2], F32, gate_bias.rearrange("l (g h) -> h l g", g=2), allow_slow_non_contiguous=True)
    for b in cbufs:
        b.w = (csem, 16 * len(cbufs))

    eps_c = P.sb("eps_c", [128, 1], F32)
    one_c = P.sb("one_c", [128, 1], F32)
    ones_m = P.sb("ones_m", [128, 128], BF16)
    ones_h = P.sb("ones_h", [128, 128], BF16)
    ones64 = P.sb("ones64", [64, 128], BF16)
    identb = P.sb("identb", [128, 128], BF16)
    maskS = P.sb("maskS", [64, 64], F32)
    NW = P.sb("NW", [128, KC, 16], F32)
    R5 = P.sb("R5", [128, 4, 16], F32)
    LB = P.sb("LB", [128, 4, 4], F32)
    OML = P.sb("OML", [128, 4, 4], F32)
    NBA = P.sb("NBA", [64, 4, 4], F32)
    NGBF = P.sb("NGBF", [4, 4], F32)
    ones4 = P.sb("ones4", [4, 512], F32)
    zeros4 = P.sb("zeros4", [4, 512], F32)

    QH = [P.sb(f"QH{i}", [128, T], BF16) for i in range(4)]
    KT = [P.sb(f"KT{i}", [128, T], BF16) for i in range(4)]
    G = [P.sb(f"G{i}", [128, T], BF16) for i in range(4)]
    Y = [P.sb(f"Y{i}", [128, T], BF16) for i in range(4)]
    V = P.sb("V", [64, 8, 512], BF16)
    TMP = [P.sb(f"TMP{i}", [128, T], F32) for i in range(6)]
    DL = [P.sb(f"DL{i}", [128, 8], F32) for i in range(4)]
    ATM = [P.sb(f"ATM{i}", [64, 64], BF16) for i in range(8)]
    KTOK = [P.sb(f"KTOK{i}", [64, 128], BF16) for i in range(8)]
    SQH = P.sb("SQH", [128, T], BF16)
    LR = P.sb("LR", [16, T], BF16)
    PTK = P.sb("PTK", [64, 32], F32)
    RB = P.sb("RB", [128, 4, 8], F32)
    R4 = P.sb("R4", [4, 8], F32)
    MP = P.sb("MP", [4, 8], F32)
    MO = P.sb("MO", [4, 8], F32, dma=True)
    M0 = P.sb("M0", [4, 8], F32, dma=True)
    NT = P.sb("NT", [128, 16], F32)
    NROWS = P.sb("NROWS", [16, 128], F32, dma=True)
    FCY = P.sb("FCY", [4, NSET], F32)
    MCY = P.sb("MCY", [4, NSET], F32)
    SAb = [P.sb(f"SAb{i}", [128, 128], BF16) for i in range(4)]
    CBb = [P.sb(f"CBb{i}", [128, 256], BF16) for i in range(4)]
    SA = [[P.sb(f"SA{s}_{i}", [128, 128], F32, dma=True) for i in range(4)] for s in range(NSET)]
    CB = [[P.sb(f"CB{s}_{i}", [128, 256], F32, dma=True) for i in range(4)] for s in range(NSET)]
    SC = [[P.sb(f"SC{s}_{i}", [64, 128], F32, dma=True) for i in range(4)] for s in range(NSET)]

    PSB = [P.ps(f"psb{i}", [128, 512], F32) for i in range(3)]
    OACC = [P.ps(f"oacc{i}", [128, 512], F32) for i in range(4)]
    PTB = es.enter_context(nc.psum_tensor("ptb", [128, 1024], BF16))
    ps_i = [0]
    pt_i = [0]
    class View:
        def __init__(self, buf, t):
            self.buf = buf
            self.t = t

    PAR = [View(PSB[0], PSB[0].t[:, i * 64:(i + 1) * 64]) for i in range(8)]
    PUR = [View(PSB[1 + i // 2], PSB[1 + i // 2].t[:, (i % 2) * 256:(i % 2 + 1) * 256]) for i in range(4)]
    PTBb = Buf("ptbb", PTB)
    PTR = [View(PTBb, PTB[:, i * 128:(i + 1) * 128]) for i in range(8)]

    def to_regions():
        pass

    def to_banks():
        pass

    def psum(all7=False):
        pool = PSB + OACC if all7 else PSB
        b = pool[ps_i[0] % len(pool)]
        ps_i[0] += 1
        return b

    def ptr():
        b = PTR[pt_i[0] % 8]
        pt_i[0] += 1
        return b

    rot = {"atm": 0, "ktok": 0}

    def act(out, in_, func, reads, writes, **kw):
        return P.op(ACT, lambda: nc.scalar.activation(out=out, in_=in_, func=func, **kw), reads=reads, writes=writes)

    def mm(out, lhsT, rhs, reads, writes, start=True, stop=True):
        return P.op(PE, lambda: nc.tensor.matmul(out, lhsT, rhs, start=start, stop=stop), reads=reads, writes=writes)

    def tr(out, in_, idn, reads, writes):
        return P.op(PE, lambda: nc.tensor.transpose(out, in_, idn), reads=reads, writes=writes)

    def tt(e, out, in0, in1, op, reads, writes):
        return P.op(e, lambda: e.h.tensor_tensor(out=out, in0=in0, in1=in1, op=op), reads=reads, writes=writes)

    def ts(e, out, in0, s1, s2, op0, op1, reads, writes):
        if s2 is None:
            return P.op(e, lambda: e.h.tensor_scalar(out=out, in0=in0, scalar1=s1, scalar2=None, op0=op0),
                        reads=reads, writes=writes)
        return P.op(e, lambda: e.h.tensor_scalar(out=out, in0=in0, scalar1=s1, scalar2=s2, op0=op0, op1=op1),
                    reads=reads, writes=writes)

    def stt(e, out, in0, scalar, in1, op0, op1, reads, writes):
        return P.op(e, lambda: e.h.scalar_tensor_tensor(out=out, in0=in0, scalar=scalar, in1=in1, op0=op0, op1=op1),
                    reads=reads, writes=writes)

    def cp(e, out, in_, reads, writes):
        if e is ACT:
            return act(out, in_, AF.Copy, reads, writes)
        return P.op(e, lambda: e.h.tensor_copy(out=out, in_=in_), reads=reads, writes=writes)

    def scan(out, d0, d1, init, op0, op1, reads, writes):
        return P.op(DVE, lambda: nc.vector.tensor_tensor_scan(out=out, data0=d0, data1=d1, initial=init, op0=op0, op1=op1),
                    reads=reads, writes=writes)

    def ms(e, buf, ap, val):
        return P.op(e, lambda: e.h.memset(ap, val), writes=(buf,))

    ms(DVE, ones_m, ones_m.t[:], 1.0 / 1024.0)
    ms(DVE, ones_h, ones_h.t[:], 1.0 / 128.0)
    ms(DVE, ones64, ones64.t[:], 1.0)
    ms(DVE, eps_c, eps_c.t[:], EPS)
    ms(DVE, one_c, one_c.t[:], 1.0)
    ms(DVE, ones4, ones4.t[:], 1.0)
    ms(DVE, zeros4, zeros4.t[:], 0.0)
    ms(DVE, FCY, FCY.t[:], 0.0)
    ms(DVE, MCY, MCY.t[:], 0.0)
    cp(DVE, identb.t[:], ident.t[:], (ident,), (identb,))
    ts(DVE, maskS.t[:], maskT.t[:], 128.0 ** -0.5, None, ALU.mult, None, (maskT,), (maskS,))
    nrows = 3 * depth + 1
    for kc in range(KC):
        pb = psum()
        tr(pb.t[:, 0:nrows], NROW.t[0:nrows, kc * 128:(kc + 1) * 128], ident.t[0:nrows, 0:nrows], (NROW, ident), (pb,))
        cp(DVE, NW.t[:, kc, 0:nrows], pb.t[:, 0:nrows], (pb,), (NW,))
    for hd in range(4):
        pb = psum()
        tr(pb.t[:, 0:4 * depth], R5ROW.t[0:4 * depth, hd * 128:(hd + 1) * 128], ident.t[0:4 * depth, 0:4 * depth],
           (R5ROW, ident), (pb,))
        cp(DVE, R5.t[:, hd, 0:4 * depth], pb.t[:, 0:4 * depth], (pb,), (R5,))
    for hd in range(4):
        pb = psum()
        tr(pb.t[0:64, 0:depth], R5ROW.t[0:depth, 512 + hd * 64:512 + (hd + 1) * 64], ident.t[0:depth, 0:depth],
           (R5ROW, ident), (pb,))
        ts(DVE, NBA.t[0:64, hd, 0:depth], pb.t[0:64, 0:depth], -1.0, None, ALU.mult, None, (pb,), (NBA,))
    ts(DVE, NGBF.t[:, 0:depth], GB.t[:, :, 1], -1.0, None, ALU.mult, None, (GB,), (NGBF,))
    EX = TMP[0]
    act(EX.t[:, 0:4 * depth].rearrange("p (h l) -> p h l", h=4), R5.t[:, :, 0:depth], AF.Exp, (R5,), (EX,))
    exv = EX.t[:, 0:4 * depth].rearrange("p (h l) -> p h l", h=4)
    P.op(DVE, lambda: nc.vector.reduce_sum(out=EX.t[:, 16:20], in_=exv, axis=AX.X), reads=(EX,), writes=(EX,))
    P.op(DVE, lambda: nc.vector.reciprocal(out=EX.t[:, 16:20], in_=EX.t[:, 16:20]), reads=(EX,), writes=(EX,))
    tt(DVE, exv, exv, EX.t[:, 16:20].unsqueeze(2).broadcast_to([128, 4, depth]), ALU.mult, (EX,), (EX,))
    ms(DVE, LB, LB.t[:], 0.0)
    for l in range(1, depth):
        tt(DVE, LB.t[:, :, l], LB.t[:, :, l - 1], exv[:, :, l], ALU.add, (LB, EX), (LB,))
    ts(DVE, OML.t[:], LB.t[:], -1.0, 1.0, ALU.mult, ALU.add, (LB,), (OML,))

    def rstd_from(pb, n, dst=None):
        dst = dst or RSTD
        act(dst.t[:, 0:n], pb.t[:, 0:n], AF.Ln, (pb, eps_c), (dst,), bias=eps_c.t[:, 0:1])
        act(dst.t[:, 0:n], dst.t[:, 0:n], AF.Exp, (dst,), (dst,), scale=-0.5)

    def x_stats(n):
        for kc in range(KC):
            act(SQ.t[:, kc, 0:n], X.t[:, kc, 0:n], AF.Square, (XB[kc],), (SQ,))
        pb = psum()
        for kc in range(KC):
            mm(pb.t[:, 0:n], ones_m.t[:], SQ.t[:, kc, 0:n], (ones_m, SQ), (pb,), start=(kc == 0), stop=(kc == KC - 1))
        rstd_from(pb, n)

    def rmsnorm(nidx, n):
        x_stats(n)
        for kc in range(KC):
            stt(DVE, H.t[:, kc, 0:n], X.t[:, kc, 0:n], NW.t[:, kc, nidx:nidx + 1], RSTD.t[:, 0:n], ALU.mult, ALU.mult,
                (XB[kc], NW, RSTD), (HB[kc],))

    def ffn(l, f, n):
        PHASES.append(('ffn_norm', P.PE.cnt))
        rmsnorm(3 * l + (0 if f == 1 else 2), n)
        PHASES.append(('ffn_up', P.PE.cnt))
        for i in range(6):
            s, wv = load_block(l, f"up{f}_{i}")
            nch = 4 if i < 5 else 2
            half = nch * 128
            for j in range(nch):
                pg = psum(True)
                pu = psum(True)
                for kc in range(KC):
                    mm(pg.t[:, 0:n], wv[:, kc, j * 128:(j + 1) * 128], H.t[:, kc, 0:n], (s, HB[kc]), (pg,),
                       start=(kc == 0), stop=(kc == KC - 1))
                for kc in range(KC):
                    mm(pu.t[:, 0:n], wv[:, kc, half + j * 128:half + (j + 1) * 128], H.t[:, kc, 0:n], (s, HB[kc]), (pu,),
                       start=(kc == 0), stop=(kc == KC - 1))
                sg = SG[(i * 4 + j) % 2]
                act(sg.t[:, 0:n], pg.t[:, 0:n], AF.Silu, (pg,), (sg,))
                ch = i * 4 + j
                tt(DVE, HIDv[:, ch, 0:n], sg.t[:, 0:n], pu.t[:, 0:n], ALU.mult, (sg, pu), (ARENA,))
        PHASES.append(('ffn_down', P.PE.cnt))
        for i in range(4):
            s, wv = load_block(l, f"dn{f}_{i}")
            for j in range(2):
                oc = i * 2 + j
                pb = psum(True)
                for kc in range(FC):
                    mm(pb.t[:, 0:n], wv[:, kc, j * 128:(j + 1) * 128], HIDv[:, kc, 0:n], (s, ARENA), (pb,),
                       start=(kc == 0), stop=(kc == FC - 1))
                stt(DVE, X.t[:, oc, 0:n], pb.t[:, 0:n], 0.5, X.t[:, oc, 0:n], ALU.mult, ALU.add, (pb, XB[oc]), (XB[oc],))

    kio = [0]

    def load_x(src_ap, n):
        for tb in range((n + 127) // 128):
            r = min(128, n - tb * 128)
            xin = XIO[kio[0] % 2]
            kio[0] += 1
            P.dma(POOL, xin.t[0:r, :], src_ap[tb * 128:tb * 128 + r, :], xin, load=True)
            for g in range(2):
                pb = psum()
                for q in range(4):
                    kc = g * 4 + q
                    tr(pb.t[:, q * 128:q * 128 + r], xin.t[0:r, kc * 128:(kc + 1) * 128], ident.t[0:r, 0:r],
                       (xin, ident), (pb,))
                act(X.t[:, g * 4:(g + 1) * 4, tb * 128:tb * 128 + r],
                    pb.t[:, :].rearrange("p (q t) -> p q t", q=4)[:, :, 0:r], AF.Copy, (pb,), tuple(XB[g * 4:(g + 1) * 4]))

    def store_y(dst_ap, n):
        x_stats(n)
        fi = 3 * depth
        for kc in range(KC):
            stt(DVE, X.t[:, kc, 0:n], X.t[:, kc, 0:n], NW.t[:, kc, fi:fi + 1], RSTD.t[:, 0:n], ALU.mult, ALU.mult,
                (XB[kc], NW, RSTD), (XB[kc],))
        for tb in range((n + 127) // 128):
            r = min(128, n - tb * 128)
            yo = XIO[kio[0] % 2]
            kio[0] += 1
            for g in range(2):
                pb = psum()
                for q in range(4):
                    kc = g * 4 + q
                    tr(pb.t[0:r, q * 128:(q + 1) * 128], X.t[:, kc, tb * 128:tb * 128 + r], ident.t[:, :], (XB[kc], ident), (pb,))
                act(yo.t[0:r, g * 512:(g + 1) * 512], pb.t[0:r, :], AF.Copy, (pb,), (yo,))
            P.dma(POOL, dst_ap[tb * 128:tb * 128 + r, :], yo.t[0:r, :], yo, load=False)

    def head_norm_scale(src_ap, src_bufs, n, wcol, out_ap, out_buf, gate=None):
        act(SQH.t[:, 0:n], src_ap, AF.Square, src_bufs, (SQH,))
        pm = psum()
        mm(pm.t[:, 0:n], ones_h.t[:], SQH.t[:, 0:n], (ones_h, SQH), (pm,))
        rstd_from(pm, n)
        if gate is None:
            stt(DVE, out_ap, src_ap, wcol, RSTD.t[:, 0:n], ALU.mult, ALU.mult, src_bufs + (R5, RSTD), (out_buf,))
        else:
            t = TMP[2]
            stt(DVE, t.t[:, 0:n], src_ap, wcol, RSTD.t[:, 0:n], ALU.mult, ALU.mult, src_bufs + (R5, RSTD), (t,))
            tt(DVE, out_ap, t.t[:, 0:n], gate.t[:, 0:n], ALU.mult, (t, gate), (out_buf,))

    def proj_v(l, key, n, L, NCH):
        s, wv = load_block(l, key)
        for c in range(NCH):
            pb = psum(True)
            for kc in range(KC):
                mm(pb.t[0:L, 0:512], H.t[:, kc, c * L:(c + 1) * L], wv[:, kc, 0:512], (s, HB[kc]), (pb,),
                   start=(kc == 0), stop=(kc == KC - 1))
            cp(ACT, V.t[0:L, c, :], pb.t[0:L, 0:512], (pb,), (V,))

    def proj3(s, wv, n, ncol):
        outs = []
        for j in range(ncol):
            pb = psum(True)
            for kc in range(KC):
                mm(pb.t[:, 0:n], wv[:, kc, j * 128:(j + 1) * 128], H.t[:, kc, 0:n], (s, HB[kc]), (pb,),
                   start=(kc == 0), stop=(kc == KC - 1))
            outs.append(pb)
        return outs

    def mg_fill(wmg, smg, n, ocs):
        for oc in ocs:
            pg = PSB[2]
            for kc in range(KC):
                mm(pg.t[:, 0:n], wmg[:, kc, oc * 128:(oc + 1) * 128], H.t[:, kc, 0:n], (smg, HB[kc]), (pg,),
                   start=(kc == 0), stop=(kc == KC - 1))
            act(SQ.t[:, oc, 0:n], pg.t[:, 0:n], AF.Sigmoid, (pg,), (SQ,))

    def fill_ocs(c, NCH):
        per = KC // NCH
        return list(range(c * per, (c + 1) * per))

    def merge_branch(l, a, n):
        sbr, wbr = load_block(l, f"br{a}")
        for oc in range(KC):
            pp = psum(True)
            for hd in range(4):
                mm(pp.t[:, 0:n], wbr[:, hd, oc * 128:(oc + 1) * 128], Y[hd].t[:, 0:n], (sbr, Y[hd]), (pp,),
                   start=(hd == 0), stop=(hd == 3))
            if a == 0:
                tt(DVE, MERGEDv[:, oc, 0:n], SQ.t[:, oc, 0:n], pp.t[:, 0:n], ALU.mult, (SQ, pp), (ARENA,))
            else:
                sg = SG[oc % 2]
                tt(DVE, sg.t[:, 0:n], SQ.t[:, oc, 0:n], pp.t[:, 0:n], ALU.mult, (SQ, pp), (sg,))
                if a == 2:
                    tt(DVE, H.t[:, oc, 0:n], MERGEDv[:, oc, 0:n], sg.t[:, 0:n], ALU.add, (ARENA, sg), (HB[oc],))
                else:
                    tt(DVE, MERGEDv[:, oc, 0:n], MERGEDv[:, oc, 0:n], sg.t[:, 0:n], ALU.add, (ARENA, sg), (ARENA,))

    def mixer(l, n, L, NCH, sample, last):
        PHASES.append(('mix_norm', P.PE.cnt))
        rmsnorm(3 * l + 1, n)
        RM = CMASK.t[:, 512:512 + n] if sample else CMASK.t[:, 0:n]

        def sset(c):
            return c if sample else l

        def chunked(ap):
            return ap.rearrange("p (c t) -> p c t", c=NCH)

        if cfg.stage <= 0:
            return
        PHASES.append(('A_proj', P.PE.cnt))
        proj_v(l, "a_v", n, L, NCH)
        for hd in range(4):
            s, wv = load_block(l, f"a_qfg{hd}")
            pq, pf, pg = proj3(s, wv, n, 3)
            t0, t1, t2, t3 = TMP[0], TMP[1], TMP[2], TMP[3]
            act(t0.t[:, 0:n], pf.t[:, 0:n], AF.Sigmoid, (pf,), (t0,))
            ts(DVE, t0.t[:, 0:n], t0.t[:, 0:n], OML.t[:, hd, l:l + 1], LB.t[:, hd, l:l + 1], ALU.mult, ALU.add,
               (t0, OML, LB), (t0,))
            ts(DVE, t1.t[:, 0:n], t0.t[:, 0:n], -1.0, 1.0, ALU.mult, ALU.add, (t0,), (t1,))
            act(t0.t[:, 0:n], t0.t[:, 0:n], AF.Ln, (t0,), (t0,))
            scan(t2.t[:, 0:n], RM, t0.t[:, 0:n], 0.0, ALU.mult, ALU.add, (CMASK, t0), (t2,))
            act(t0.t[:, 0:n], t2.t[:, 0:n], AF.Exp, (t2,), (t0,))
            act(t3.t[:, 0:n], t2.t[:, 0:n], AF.Exp, (t2,), (t3,), scale=-1.0)
            act(t2.t[:, 0:n], pq.t[:, 0:n], AF.Silu, (pq,), (t2,))
            tt(DVE, QH[hd].t[:, 0:n], t2.t[:, 0:n], t0.t[:, 0:n], ALU.mult, (t2, t0), (QH[hd],))
            tt(DVE, KT[hd].t[:, 0:n], t1.t[:, 0:n], t3.t[:, 0:n], ALU.mult, (t1, t3), (KT[hd],))
            cp(DVE, DL[hd].t[:, 0:NCH], chunked(t0.t[:, 0:n])[:, :, L - 1], (t0,), (DL[hd],))
            act(G[hd].t[:, 0:n], pg.t[:, 0:n], AF.Sigmoid, (pg,), (G[hd],))
        if cfg.stage <= 1:
            return
        PHASES.append(('A_chunks', P.PE.cnt))
        def a_stage1(c):
            cs = slice(c * L, (c + 1) * L)
            pas, pts, ams, kks = [], [], [], []
            for hd in range(4):
                pa = PAR[(c % 2) * 4 + hd]
                mm(pa.t[0:L, 0:L], KT[hd].t[:, cs], QH[hd].t[:, cs], (KT[hd], QH[hd]), (pa.buf,))
                pt = ptr()
                tr(pt.t[0:L, 0:128], KT[hd].t[:, cs], identb.t[:], (KT[hd], identb), (pt.buf,))
                pas.append(pa)
                pts.append(pt)
            for hd in range(4):
                am = ATM[(c % 2) * 4 + hd]
                tt(DVE, am.t[0:L, 0:L], pas[hd].t[0:L, 0:L], maskT.t[0:L, 0:L], ALU.mult, (pas[hd].buf, maskT), (am,))
                kk = KTOK[(c % 2) * 4 + hd]
                cp(ACT, kk.t[0:L, :], pts[hd].t[0:L, 0:128], (pts[hd].buf,), (kk,))
                ams.append(am)
                kks.append(kk)
            return ams, kks

        def a_stage2(c, ams, kks):
            cs = slice(c * L, (c + 1) * L)
            st = sset(c)
            pus = []
            for hd in range(4):
                S = SA[st][hd]
                if c == 0 or sample:
                    cp(ACT, SAb[hd].t[:], S.t[:], (S,), (SAb[hd],))
                mm(OACC[hd].t[:, cs], V.t[0:L, c, hd * 128:(hd + 1) * 128], ams[hd].t[0:L, 0:L], (V, ams[hd]), (OACC[hd],),
                   start=True, stop=False)
                mm(OACC[hd].t[:, cs], SAb[hd].t[:], QH[hd].t[:, cs], (SAb[hd], QH[hd]), (OACC[hd],), start=False, stop=True)
                pu = View(PSB[1], PSB[1].t[:, hd * 128:(hd + 1) * 128])
                mm(pu.t[:, 0:128], kks[hd].t[0:L, :], V.t[0:L, c, hd * 128:(hd + 1) * 128], (kks[hd], V), (pu.buf,))
                pus.append(pu)
            for hd in range(4):
                S = SA[st][hd]
                tt(DVE, S.t[:], S.t[:], pus[hd].t[:, 0:128], ALU.add, (S, pus[hd].buf), (S,))
                if (not sample) and c < NCH - 1:
                    act(SAb[hd].t[:], S.t[:], AF.Copy, (S, DL[hd]), (SAb[hd],), scale=DL[hd].t[:, c:c + 1])
                ts(DVE, S.t[:], S.t[:], DL[hd].t[:, c:c + 1], None, ALU.mult, None, (S, DL[hd]), (S,))
                if sample:
                    P.dma(SP, o_hgrn_s[l, c, hd], S.t[:], S, load=False)
                elif last and c == NCH - 1:
                    P.dma(SP, o_hgrn_p[l, hd], S.t[:], S, load=False)

        smg, wmg = load_block(l, "mg0")
        nxt = a_stage1(0)
        for c in range(NCH):
            cur = nxt
            if c + 1 < NCH:
                nxt = a_stage1(c + 1)
            a_stage2(c, *cur)
            mg_fill(wmg, smg, n, fill_ocs(c, NCH))
        if cfg.stage <= 2:
            return
        PHASES.append(('A_post', P.PE.cnt))
        for hd in range(4):
            t4 = TMP[4]
            tt(DVE, t4.t[:, 0:n], OACC[hd].t[:, 0:n], G[hd].t[:, 0:n], ALU.mult, (OACC[hd], G[hd]), (t4,))
            head_norm_scale(t4.t[:, 0:n], (t4,), n, R5.t[:, hd, depth + l:depth + l + 1], Y[hd].t[:, 0:n], Y[hd])
        PHASES.append(('A_merge', P.PE.cnt))
        merge_branch(l, 0, n)

        if cfg.stage <= 3:
            return
        PHASES.append(('B_gates', P.PE.cnt))
        ssm, wsm = load_block(l, "small")
        pi = psum(True)
        pf = psum(True)
        plr = psum(True)
        for kc in range(KC):
            mm(pi.t[0:4, 0:n], wsm[:, kc, 0:4], H.t[:, kc, 0:n], (ssm, HB[kc]), (pi,), start=(kc == 0), stop=(kc == KC - 1))
        for kc in range(KC):
            mm(pf.t[0:4, 0:n], wsm[:, kc, 4:8], H.t[:, kc, 0:n], (ssm, HB[kc]), (pf,), start=(kc == 0), stop=(kc == KC - 1))
        for kc in range(KC):
            mm(plr.t[0:16, 0:n], wsm[:, kc, 8:24], H.t[:, kc, 0:n], (ssm, HB[kc]), (plr,), start=(kc == 0), stop=(kc == KC - 1))
        cp(ACT, LR.t[0:16, 0:n], plr.t[0:16, 0:n], (plr,), (LR,))
        g0, g1, g2, g3 = TMP[0], TMP[1], TMP[2], TMP[3]
        act(g0.t[0:4, 0:n], pf.t[0:4, 0:n], AF.Exp, (pf, NGBF), (g0,), scale=-1.0, bias=NGBF.t[:, l:l + 1])
        act(g0.t[0:4, 0:n], g0.t[0:4, 0:n], AF.Ln, (g0,), (g0,), bias=one_c.t[0:4, 0:1])
        if sample:
            scan(g1.t[0:4, 0:n], CMASK.t[0:4, 512:512 + n], g0.t[0:4, 0:n], 0.0, ALU.mult, ALU.add, (CMASK, g0), (g1,))
        else:
            scan(g1.t[0:4, 0:n], ones4.t[0:4, 0:n], g0.t[0:4, 0:n], FCY.t[:, l:l + 1], ALU.mult, ALU.add,
                 (ones4, g0, FCY), (g1,))
        stt(DVE, g2.t[0:4, 0:n], pi.t[0:4, 0:n], GB.t[:, l, 0:1], g1.t[0:4, 0:n], ALU.add, ALU.add,
            (pi, GB, g1), (g2,))
        if sample:
            P.dma(SP, M0.t[0:4, 0:NCH], st_m[l].rearrange("s h -> h s"), M0, load=True, allow_slow_non_contiguous=True)
            ga = TMP[4]
            cp(DVE, ga.t[0:4, 0:n], g2.t[0:4, 0:n], (g2,), (ga,))
            a0 = chunked(ga.t[0:4, 0:n])[:, :, 0]
            tt(DVE, a0, a0, M0.t[0:4, 0:NCH], ALU.max, (ga, M0), (ga,))
            scan(g3.t[0:4, 0:n], CMASK.t[0:4, 576:576 + n], ga.t[0:4, 0:n], NEG_BIG, ALU.add, ALU.max, (CMASK, ga), (g3,))
        else:
            scan(g3.t[0:4, 0:n], zeros4.t[0:4, 0:n], g2.t[0:4, 0:n], MCY.t[:, l:l + 1], ALU.add, ALU.max,
                 (zeros4, g2, MCY), (g3,))
        mcv = chunked(g3.t[0:4, 0:n])[:, :, L - 1]
        if sample:
            cp(DVE, MP.t[0:4, 0:NCH], M0.t[0:4, 0:NCH], (M0,), (MP,))
        else:
            cp(DVE, MP.t[0:4, 0:1], MCY.t[:, l:l + 1], (MCY,), (MP,))
            if NCH > 1:
                cp(DVE, MP.t[0:4, 1:NCH], chunked(g3.t[0:4, 0:n])[:, 0:NCH - 1, L - 1], (g3,), (MP,))
        tt(DVE, R4.t[0:4, 0:NCH], MP.t[0:4, 0:NCH], mcv, ALU.subtract, (MP, g3), (R4,))
        act(R4.t[0:4, 0:NCH], R4.t[0:4, 0:NCH], AF.Exp, (R4,), (R4,))
        tt(DVE, MO.t[0:4, 0:NCH], mcv, chunked(g1.t[0:4, 0:n])[:, :, L - 1], ALU.subtract, (g3, g1), (MO,))
        mcb = mcv.unsqueeze(2).broadcast_to([4, NCH, L])
        g4, g5 = TMP[4], TMP[5]
        tt(DVE, chunked(g4.t[0:4, 0:n]), chunked(g2.t[0:4, 0:n]), mcb, ALU.subtract, (g2, g3), (g4,))
        act(g4.t[0:4, 0:n], g4.t[0:4, 0:n], AF.Exp, (g4,), (g4,))
        tt(DVE, chunked(g5.t[0:4, 0:n]), chunked(g1.t[0:4, 0:n]), mcb, ALU.subtract, (g1, g3), (g5,))
        act(g5.t[0:4, 0:n], g5.t[0:4, 0:n], AF.Exp, (g5,), (g5,))
        if not sample:
            cp(DVE, FCY.t[:, l:l + 1], g1.t[0:4, n - 1:n], (g1,), (FCY,))
            cp(DVE, MCY.t[:, l:l + 1], g3.t[0:4, n - 1:n], (g3,), (MCY,))
        if cfg.stage <= 4:
            return
        pp_ = psum()
        for c in range(NCH):
            tr(pp_.t[0:L, c * 4:(c + 1) * 4], g4.t[0:4, c * L:(c + 1) * L], ident.t[0:4, 0:4], (g4, ident), (pp_,))
        cp(ACT, PTK.t[0:L, 0:NCH * 4], pp_.t[0:L, 0:NCH * 4], (pp_,), (PTK,))
        for hd in range(4):
            pr_ = psum()
            mm(pr_.t[:, 0:NCH], SEL.t[0:4, hd * 128:(hd + 1) * 128], R4.t[0:4, 0:NCH], (SEL, R4), (pr_,))
            cp(ACT, RB.t[:, hd, 0:NCH], pr_.t[:, 0:NCH], (pr_,), (RB,))
        if sample:
            P.dma(SP, o_m_s[l].rearrange("s h -> h s"), MO.t[0:4, 0:NCH], MO, load=False, allow_slow_non_contiguous=True)
        elif last:
            P.dma(SP, o_m_p[l].rearrange("(h o) -> h o", o=1), MO.t[0:4, NCH - 1:NCH], MO, load=False)
        if sample:
            P.dma(SP, NROWS.t[0:4 * NCH, :], st_n[l], NROWS, load=True)
            pn = psum()
            tr(pn.t[:, 0:4 * NCH], NROWS.t[0:4 * NCH, :], ident.t[0:4 * NCH, 0:4 * NCH], (NROWS, ident), (pn,))
            cp(ACT, NT.t[:, 0:4 * NCH], pn.t[:, 0:4 * NCH], (pn,), (NT,))
        if cfg.stage <= 5:
            return
        PHASES.append(('B_proj', P.PE.cnt))
        proj_v(l, "b_v", n, L, NCH)
        for hd in range(4):
            s, wv = load_block(l, f"b_qko{hd}")
            pq, pk, po = proj3(s, wv, n, 3)
            cp(ACT, QH[hd].t[:, 0:n], pq.t[:, 0:n], (pq,), (QH[hd],))
            cp(DVE, KT[hd].t[:, 0:n], pk.t[:, 0:n], (pk,), (KT[hd],))
            act(G[hd].t[:, 0:n], po.t[:, 0:n], AF.Sigmoid, (po,), (G[hd],))
        if cfg.stage <= 6:
            return
        PHASES.append(('B_chunks_post', P.PE.cnt))
        for hp in range(2):
            def b_stage1(c):
                cs = slice(c * L, (c + 1) * L)
                pas, pts, ams, kks = [], [], [], []
                for e in range(2):
                    hd = hp * 2 + e
                    pa = PAR[(c % 2) * 4 + e]
                    mm(pa.t[0:L, 0:L], KT[hd].t[:, cs], QH[hd].t[:, cs], (KT[hd], QH[hd]), (pa.buf,))
                    pt = ptr()
                    tr(pt.t[0:L, 0:128], KT[hd].t[:, cs], identb.t[:], (KT[hd], identb), (pt.buf,))
                    pas.append(pa)
                    pts.append(pt)
                for e in range(2):
                    hd = hp * 2 + e
                    pcol = PTK.t[0:L, c * 4 + hd:c * 4 + hd + 1]
                    am = ATM[(c % 2) * 4 + e]
                    stt(DVE, am.t[0:L, 0:L], pas[e].t[0:L, 0:L], pcol, maskS.t[0:L, 0:L], ALU.mult, ALU.mult,
                        (pas[e].buf, PTK, maskS), (am,))
                    kk = KTOK[(c % 2) * 4 + e]
                    ts(DVE, kk.t[0:L, :], pts[e].t[0:L, 0:128], pcol, 128.0 ** -0.5, ALU.mult, ALU.mult, (pts[e].buf, PTK), (kk,))
                    ams.append(am)
                    kks.append(kk)
                return ams, kks

            def b_stage2(c, ams, kks):
                cs = slice(c * L, (c + 1) * L)
                st = sset(c)
                pus = []
                for e in range(2):
                    hd = hp * 2 + e
                    C = CB[st][hd]
                    NUM = OACC[2 * e]
                    DEN = OACC[2 * e + 1]
                    am = ams[e]
                    kk = kks[e]
                    if sample:
                        cp(DVE, C.t[:, 128:256], NT.t[:, c * 4 + hd:c * 4 + hd + 1].broadcast_to([128, 128]), (NT,), (C,))
                    if c == 0 or sample:
                        act(CBb[hd].t[:], C.t[:], AF.Copy, (C, RB), (CBb[hd],), scale=RB.t[:, hd, c:c + 1])
                    mm(NUM.t[:, cs], V.t[0:L, c, hd * 128:(hd + 1) * 128], am.t[0:L, 0:L], (V, am), (NUM,),
                       start=True, stop=False)
                    mm(NUM.t[:, cs], CBb[hd].t[:, 0:128], QH[hd].t[:, cs], (CBb[hd], QH[hd]), (NUM,), start=False, stop=True)
                    mm(DEN.t[:, cs], ones64.t[0:L, :], am.t[0:L, 0:L], (ones64, am), (DEN,), start=True, stop=False)
                    mm(DEN.t[:, cs], CBb[hd].t[:, 128:256], QH[hd].t[:, cs], (CBb[hd], QH[hd]), (DEN,), start=False, stop=True)
                    pu = PUR[e]
                    mm(pu.t[:, 0:128], kk.t[0:L, :], V.t[0:L, c, hd * 128:(hd + 1) * 128], (kk, V), (pu.buf,))
                    mm(pu.t[:, 128:256], kk.t[0:L, :], ones64.t[0:L, :], (kk, ones64), (pu.buf,))
                    pus.append(pu)
                for e in range(2):
                    hd = hp * 2 + e
                    C = CB[st][hd]
                    stt(DVE, C.t[:], C.t[:], RB.t[:, hd, c:c + 1], pus[e].t[:, 0:256], ALU.mult, ALU.add, (C, RB, pus[e].buf), (C,))
                    if (not sample) and c < NCH - 1:
                        act(CBb[hd].t[:], C.t[:], AF.Copy, (C, RB), (CBb[hd],), scale=RB.t[:, hd, c + 1:c + 2])
                    fin = sample or (last and c == NCH - 1)
                    if fin:
                        idx = (c * 4 + hd) if sample else hd
                        cp(DVE, NT.t[:, idx:idx + 1], C.t[:, 128:129], (C,), (NT,))
                        dst = o_c_s[l, c, hd] if sample else o_c_p[l, hd]
                        P.dma(SP, dst, C.t[:, 0:128], C, load=False)

            if hp == 0:
                smg, wmg = load_block(l, "mg1")
            nxt = b_stage1(0)
            for c in range(NCH):
                cur = nxt
                if c + 1 < NCH:
                    nxt = b_stage1(c + 1)
                b_stage2(c, *cur)
                if hp == 0:
                    mg_fill(wmg, smg, n, fill_ocs(c, NCH))
            for e in range(2):
                hd = hp * 2 + e
                NUM = OACC[2 * e]
                DEN = OACC[2 * e + 1]
                pth = psum()
                mm(pth.t[:, 0:n], SEL.t[0:4, hd * 128:(hd + 1) * 128], g5.t[0:4, 0:n], (SEL, g5), (pth,))
                t0, t1 = TMP[0], TMP[1]
                act(t0.t[:, 0:n], DEN.t[:, 0:n], AF.Abs, (DEN,), (t0,))
                tt(DVE, t0.t[:, 0:n], t0.t[:, 0:n], pth.t[:, 0:n], ALU.max, (t0, pth), (t0,))
                P.op(DVE, lambda: nc.vector.reciprocal(out=t0.t[:, 0:n], in_=t0.t[:, 0:n]), reads=(t0,), writes=(t0,))
                tt(DVE, t1.t[:, 0:n], NUM.t[:, 0:n], t0.t[:, 0:n], ALU.mult, (NUM, t0), (t1,))
                head_norm_scale(t1.t[:, 0:n], (t1,), n, R5.t[:, hd, 2 * depth + l:2 * depth + l + 1],
                                Y[hd].t[:, 0:n], Y[hd], gate=G[hd])
        if cfg.stage <= 7:
            return
        if sample or last:
            ncols = 4 * NCH if sample else 4
            pn = psum()
            tr(pn.t[0:ncols, 0:128], NT.t[:, 0:ncols], ident.t[:, :], (NT, ident), (pn,))
            cp(ACT, NROWS.t[0:ncols, :], pn.t[0:ncols, 0:128], (pn,), (NROWS,))
            P.dma(SP, (o_n_s[l] if sample else o_n_p[l]), NROWS.t[0:ncols, :], NROWS, load=False)
        PHASES.append(('B_merge', P.PE.cnt))
        merge_branch(l, 1, n)

        if cfg.stage <= 8:
            return
        PHASES.append(('C_proj', P.PE.cnt))
        proj_v(l, "c_v", n, L, NCH)
        for pr in range(2):
            s, wv = load_block(l, f"c_qk{pr}")
            for e in range(2):
                hd = 2 * pr + e
                pq = psum(True)
                pk = psum(True)
                pz = psum(True)
                for kc in range(KC):
                    mm(pq.t[0:64, 0:n], wv[:, kc, e * 64:(e + 1) * 64], H.t[:, kc, 0:n], (s, HB[kc]), (pq,),
                       start=(kc == 0), stop=(kc == KC - 1))
                for kc in range(KC):
                    mm(pk.t[0:64, 0:n], wv[:, kc, 128 + e * 64:128 + (e + 1) * 64], H.t[:, kc, 0:n], (s, HB[kc]), (pk,),
                       start=(kc == 0), stop=(kc == KC - 1))
                mm(pz.t[0:64, 0:n], W2b.t[0:16, l, hd * 64:(hd + 1) * 64], LR.t[0:16, 0:n], (W2b, LR), (pz,))
                t0, t2, t3 = TMP[0], TMP[2], TMP[3]
                act(t0.t[0:64, 0:n], pz.t[0:64, 0:n], AF.Exp, (pz, NBA), (t0,), scale=-1.0, bias=NBA.t[0:64, hd, l:l + 1])
                act(t0.t[0:64, 0:n], t0.t[0:64, 0:n], AF.Ln, (t0,), (t0,), bias=one_c.t[0:64, 0:1])
                scan(t2.t[0:64, 0:n], RM[0:64], t0.t[0:64, 0:n], 0.0, ALU.mult, ALU.add, (CMASK, t0), (t2,))
                act(t0.t[0:64, 0:n], t2.t[0:64, 0:n], AF.Exp, (t2,), (t0,), scale=-1.0 / 16.0)
                act(t3.t[0:64, 0:n], t2.t[0:64, 0:n], AF.Exp, (t2,), (t3,), scale=1.0 / 16.0)
                stt(DVE, QH[hd].t[0:64, 0:n], pq.t[0:64, 0:n], 0.125, t0.t[0:64, 0:n], ALU.mult, ALU.mult,
                    (pq, t0), (QH[hd],))
                tt(DVE, KT[hd].t[0:64, 0:n], pk.t[0:64, 0:n], t3.t[0:64, 0:n], ALU.mult, (pk, t3), (KT[hd],))
                cp(DVE, DL[hd].t[0:64, 0:NCH], chunked(t0.t[0:64, 0:n])[:, :, L - 1], (t0,), (DL[hd],))
        if cfg.stage <= 9:
            return
        s, wv = load_block(l, "c_g")
        for hd in range(4):
            pb = psum(True)
            for kc in range(KC):
                mm(pb.t[:, 0:n], wv[:, kc, hd * 128:(hd + 1) * 128], H.t[:, kc, 0:n], (s, HB[kc]), (pb,),
                   start=(kc == 0), stop=(kc == KC - 1))
            act(G[hd].t[:, 0:n], pb.t[:, 0:n], AF.Silu, (pb,), (G[hd],))
        if cfg.stage <= 10:
            return
        PHASES.append(('C_chunks', P.PE.cnt))
        def c_stage1(c):
            cs = slice(c * L, (c + 1) * L)
            pas, pts, ams, kks = [], [], [], []
            for hd in range(4):
                pa = PAR[(c % 2) * 4 + hd]
                mm(pa.t[0:L, 0:L], KT[hd].t[0:64, cs], QH[hd].t[0:64, cs], (KT[hd], QH[hd]), (pa.buf,))
                pt = ptr()
                tr(pt.t[0:L, 0:64], KT[hd].t[0:64, cs], identb.t[0:64, 0:64], (KT[hd], identb), (pt.buf,))
                pas.append(pa)
                pts.append(pt)
            for hd in range(4):
                am = ATM[(c % 2) * 4 + hd]
                tt(DVE, am.t[0:L, 0:L], pas[hd].t[0:L, 0:L], maskT.t[0:L, 0:L], ALU.mult, (pas[hd].buf, maskT), (am,))
                kk = KTOK[(c % 2) * 4 + hd]
                cp(ACT, kk.t[0:L, 0:64], pts[hd].t[0:L, 0:64], (pts[hd].buf,), (kk,))
                ams.append(am)
                kks.append(kk)
            return ams, kks

        def c_stage2(c, ams, kks):
            cs = slice(c * L, (c + 1) * L)
            st = sset(c)
            pus = []
            for hd in range(4):
                S = SC[st][hd]
                Sb = SAb[hd]
                if c == 0 or sample:
                    cp(ACT, Sb.t[0:64, :], S.t[:], (S,), (Sb,))
                mm(OACC[hd].t[:, cs], V.t[0:L, c, hd * 128:(hd + 1) * 128], ams[hd].t[0:L, 0:L], (V, ams[hd]), (OACC[hd],),
                   start=True, stop=False)
                mm(OACC[hd].t[:, cs], Sb.t[0:64, :], QH[hd].t[0:64, cs], (Sb, QH[hd]), (OACC[hd],), start=False, stop=True)
                pu = View(PSB[1], PSB[1].t[:, hd * 128:(hd + 1) * 128])
                mm(pu.t[0:64, 0:128], kks[hd].t[0:L, 0:64], V.t[0:L, c, hd * 128:(hd + 1) * 128], (kks[hd], V), (pu.buf,))
                pus.append(pu)
            for hd in range(4):
                S = SC[st][hd]
                Sb = SAb[hd]
                tt(DVE, S.t[:], S.t[:], pus[hd].t[0:64, 0:128], ALU.add, (S, pus[hd].buf), (S,))
                if (not sample) and c < NCH - 1:
                    act(Sb.t[0:64, :], S.t[:], AF.Copy, (S, DL[hd]), (Sb,), scale=DL[hd].t[0:64, c:c + 1])
                ts(DVE, S.t[:], S.t[:], DL[hd].t[0:64, c:c + 1], None, ALU.mult, None, (S, DL[hd]), (S,))
                if sample:
                    P.dma(SP, o_gla_s[l, c, hd], S.t[:], S, load=False)
                elif last and c == NCH - 1:
                    P.dma(SP, o_gla_p[l, hd], S.t[:], S, load=False)

        smg, wmg = load_block(l, "mg2")
        nxt = c_stage1(0)
        for c in range(NCH):
            cur = nxt
            if c + 1 < NCH:
                nxt = c_stage1(c + 1)
            c_stage2(c, *cur)
            mg_fill(wmg, smg, n, fill_ocs(c, NCH))
        if cfg.stage <= 11:
            return
        PHASES.append(('C_post', P.PE.cnt))
        for hd in range(4):
            head_norm_scale(OACC[hd].t[:, 0:n], (OACC[hd],), n, R5.t[:, hd, 3 * depth + l:3 * depth + l + 1],
                            Y[hd].t[:, 0:n], Y[hd], gate=G[hd])
        PHASES.append(('C_merge', P.PE.cnt))
        merge_branch(l, 2, n)

        if cfg.stage <= 12:
            return
        PHASES.append(('out_proj', P.PE.cnt))
        s, wv = load_block(l, "wout")
        for oc in range(KC):
            pb = psum(True)
            for kc in range(KC):
                mm(pb.t[:, 0:n], wv[:, kc, oc * 128:(oc + 1) * 128], H.t[:, kc, 0:n], (s, HB[kc]), (pb,),
                   start=(kc == 0), stop=(kc == KC - 1))
            tt(DVE, X.t[:, oc, 0:n], X.t[:, oc, 0:n], pb.t[:, 0:n], ALU.add, (XB[oc], pb), (XB[oc],))

    def load_sample_states(l):
        for sq in range(NSEQ):
            for hd in range(4):
                P.dma(SP, SA[sq][hd].t[:], st_hgrn[l, sq, hd], SA[sq][hd], load=True)
                P.dma(SP, CB[sq][hd].t[:, 0:128], st_c[l, sq, hd], CB[sq][hd], load=True)
            for hd in range(4):
                P.dma(SP, SC[sq][hd].t[:], st_gla[l, sq, hd], SC[sq][hd], load=True)

    def zero_states():
        for st in range(NSET):
            for hd in range(4):
                ms(POOL, SA[st][hd], SA[st][hd].t[:], 0.0)
                ms(POOL, CB[st][hd], CB[st][hd].t[:], 0.0)
            for hd in range(4):
                ms(POOL, SC[st][hd], SC[st][hd].t[:], 0.0)

    do_mixer = cfg.mixer
    if NS > 0:
        load_x(xs[0:NS, :], NS)
        for l in range(depth):
            ffn(l, 1, NS)
            if do_mixer:
                load_sample_states(l)
                mixer(l, NS, SL, NSEQ, True, False)
            ffn(l, 2, NS)
        store_y(ys[0:NS, :], NS)
    if NP > 0:
        zero_states()
        ntile = NP // T
        for ti in range(ntile):
            load_x(xp[ti * T:(ti + 1) * T, :], T)
            for l in range(depth):
                ffn(l, 1, T)
                if do_mixer:
                    mixer(l, T, 64, T // 64, False, ti == ntile - 1)
                ffn(l, 2, T)
            store_y(yp[ti * T:(ti + 1) * T, :], T)

    for b in XIO + [MO, NROWS] + [x for st in SA + CB + SC for x in st]:
        if b.dcnt:
            P._wait(SP, (b.dsem, b.dcnt))
    for e in (PE, ACT, DVE, POOL):
        if e.cnt:
            P._wait(SP, (e.sem, e.cnt))
    print(f"[build] inst={P.n_inst} waits={P.n_wait} sems={P.nsem} sbuf_left={nc.sbuf_bytes_remaining}")


_NC_CACHE = {}


def kernel(**inputs):
    inp = {k: np.asarray(v) for k, v in inputs.items()}
    B, SEQ, _ = inp["x_prompt"].shape
    NSAMP, SL, _ = inp["x_sample"].shape
    n_cores = 8
    nseq = NSAMP // n_cores
    cfg = Cfg(SEQ, nseq, SL, depth=DEPTH)
    key = (SEQ, nseq, SL)
    if key not in _NC_CACHE:
        _NC_CACHE[key] = build_program(cfg)
    nc = _NC_CACHE[key]
    in_maps = [core_inputs(inp, cfg, c % B, c, list(range(c * nseq, (c + 1) * nseq))) for c in range(n_cores)]
    res = run_bass_kernel_spmd(nc, in_maps, core_ids=list(range(n_cores)))
    R = res.results
    f32 = np.float32
    y_prompt = np.stack([R[b]["y_prompt"] for b in range(B)]).astype(f32)
    y_sample = np.concatenate([R[c]["y_sample"].reshape(nseq, SL, D_MODEL) for c in range(n_cores)], 0).astype(f32)

    def pstate(name, shape):
        return np.stack([R[b][name].reshape(shape) for b in range(B)], 1).astype(f32)

    def sstate(name, shape):
        return np.concatenate([R[c][name].reshape((DEPTH, nseq) + shape) for c in range(n_cores)], 1).astype(f32)

    return (
        y_prompt, y_sample,
        pstate("hgrn_p", (DEPTH, 4, 128, 128)), pstate("c_p", (DEPTH, 4, 128, 128)), pstate("n_p", (DEPTH, 4, 128)),
        pstate("m_p", (DEPTH, 4)), pstate("gla_p", (DEPTH, 4, 64, 128)),
        sstate("hgrn_s", (4, 128, 128)), sstate("c_s", (4, 128, 128)), sstate("n_s", (4, 128)),
        sstate("m_s", (4,)), sstate("gla_s", (4, 64, 128)),
    )


def core_inputs(inp, cfg, b, core, seqs):
    depth = cfg.depth
    f = np.ascontiguousarray
    NP, NSEQ, SL = cfg.n_prompt, cfg.n_seq, cfg.seq_len
    nsq = max(NSEQ, 1)
    sq = list(seqs) if NSEQ > 0 else [0]
    m = {
        "x_prompt": f(inp["x_prompt"][b, :max(NP, 1)]),
        "x_sample": f(inp["x_sample"][sq].reshape(nsq * SL, D_MODEL)[:max(NSEQ * SL, 1)]),
        "st_hgrn": f(inp["state_hgrn"][:depth, sq]),
        "st_c": f(inp["state_mlstm_c"][:depth, sq]),
        "st_n": f(inp["state_mlstm_n"][:depth, sq].reshape(depth, nsq * 4, 128)),
        "st_m": f(inp["state_mlstm_m"][:depth, sq]),
        "st_gla": f(inp["state_gla"][:depth, sq]),
        "norms": f(np.concatenate([np.stack([inp["ffn1_norm"][l], inp["mix_norm"][l], inp["ffn2_norm"][l]])
                                   for l in range(depth)] + [inp["final_norm"][None]], 0)),
        "rows512": f(np.concatenate([inp["hgrn_lb_logits"][:depth], inp["hgrn_norm"][:depth],
                                     inp["mlstm_norm"][:depth], inp["gla_norm"][:depth]], 0)),
        "gla_b_a": f(inp["gla_b_a"][:depth]),
        "mlstm_gate_bias": f(inp["mlstm_gate_bias"][:depth]),
        "gla_w_a2": f(inp["gla_w_a2"][:depth]),
    }
    for k in ("ffn1_w_up", "ffn1_w_down", "ffn2_w_up", "ffn2_w_down", "w_in", "w_branch", "w_out"):
        m[k] = f(inp[k][:depth])
    m.update(host_consts())
    return m
```
